# Optimizing a Trainium2 kernel written in Bass

```python
import jax, jax.numpy as jnp
from jax import lax
import numpy as np

D_MODEL = 2048
BATCH = 1
SEQ = 16384
DEPTH = 1
DEC_BATCH = 16
DEC_SEQ = 2048
PAST_LEN = 128

CONV_DIM = D_MODEL // 2
RWKV_DIM = D_MODEL // 2
HEAD_SIZE = 64
N_RWKV_HEADS = RWKV_DIM // HEAD_SIZE
CONV_WIDTH = 3
DECAY_LORA = 64
AAA_LORA = 64
GATE_LORA = 160
D_FF = 5632
FFN_RESIDUAL_SCALE = 0.5
RMS_EPS = 1e-6
GN_EPS = 64e-5
L2_EPS = 1e-12

CONV_COLS = 3 * CONV_DIM
RWKV_COLS = 3 * RWKV_DIM + 2 * DECAY_LORA + 2 * AAA_LORA + GATE_LORA
IN_COLS = CONV_COLS + RWKV_COLS
RWKV_SPLITS = (RWKV_DIM, 2 * RWKV_DIM, 3 * RWKV_DIM,
               3 * RWKV_DIM + DECAY_LORA, 3 * RWKV_DIM + 2 * DECAY_LORA,
               3 * RWKV_DIM + 2 * DECAY_LORA + AAA_LORA,
               3 * RWKV_DIM + 2 * DECAY_LORA + 2 * AAA_LORA)

kernel_name = "hymba_conv_rwkv7_macaron_encoder"


def rms_norm(x, g):
    xf = x.astype(jnp.float32)
    y = xf * lax.rsqrt(jnp.mean(xf * xf, axis=-1, keepdims=True) + RMS_EPS)
    return (y * g.astype(jnp.float32)).astype(x.dtype)


def swiglu(h, w_gate, w_up, w_down):
    return (jax.nn.silu(h @ w_gate) * (h @ w_up)) @ w_down


def centred_token_shift(z, mu):
    zp = jnp.pad(z, ((0, 0), (1, 1), (0, 0)))
    nb = 0.5 * (zp[:, :-2] + zp[:, 2:])
    return z + mu * (nb - z)


def short_conv_mixer(zc, conv_w):
    b_gate, c_gate, xin = jnp.split(zc, 3, axis=-1)
    u = c_gate * xin
    up = jnp.pad(u, ((0, 0), (1, 1), (0, 0)))
    conv = conv_w[0] * up[:, :-2] + conv_w[1] * up[:, 1:-1] + conv_w[2] * up[:, 2:]
    return b_gate * conv


def wkv7_scan(r, decay, k, v, kk, a, reverse):
    bsz = r.shape[0]
    s0 = jnp.zeros((bsz, N_RWKV_HEADS, HEAD_SIZE, HEAD_SIZE), jnp.float32)

    def step(s, inp):
        r_t, w_t, k_t, v_t, kk_t, a_t = inp
        sa = jnp.einsum('bhvk,bhk->bhv', s, -kk_t)
        s = (s * w_t[:, :, None, :]
             + jnp.einsum('bhv,bhk->bhvk', sa, kk_t * a_t)
             + jnp.einsum('bhv,bhk->bhvk', v_t, k_t))
        o = jnp.einsum('bhvk,bhk->bhv', s, r_t)
        return s, o

    xs = tuple(jnp.moveaxis(t, 1, 0) for t in (r, decay, k, v, kk, a))
    _, o = lax.scan(step, s0, xs, reverse=reverse)
    return jnp.moveaxis(o, 0, 1)


def rwkv7_mixer(zr, w0, w2, a0, a2, g2, k_k, k_a, r_k, ln_w, ln_b):
    bsz, t_len, _ = zr.shape
    f32 = jnp.float32
    zr = zr.astype(f32)
    r, k, v, zw_f, zw_b, za_f, za_b, zg = jnp.split(zr, RWKV_SPLITS, axis=-1)
    heads = lambda t: t.reshape(bsz, t_len, N_RWKV_HEADS, HEAD_SIZE)

    kk = heads(k * k_k.astype(f32))
    kk = kk / jnp.maximum(jnp.linalg.norm(kk, axis=-1, keepdims=True), L2_EPS)
    g = jax.nn.sigmoid(zg) @ g2.astype(f32)

    o_sum = jnp.zeros((bsz, t_len, N_RWKV_HEADS, HEAD_SIZE), f32)
    bonus = jnp.zeros_like(o_sum)
    for d, (zw, za) in enumerate(((zw_f, za_f), (zw_b, za_b))):
        w_log = -jax.nn.softplus(-(w0[d].astype(f32) + jnp.tanh(zw) @ w2[d].astype(f32))) - 0.5
        decay = jnp.exp(-jnp.exp(w_log))
        a = jax.nn.sigmoid(a0[d].astype(f32) + za @ a2[d].astype(f32))
        k_d = k * (1.0 + (a - 1.0) * k_a.astype(f32))
        o_sum = o_sum + wkv7_scan(heads(r), heads(decay), heads(k_d), heads(v),
                                  kk, heads(a), reverse=(d == 1))
        bonus = bonus + jnp.sum(heads(r * k_d * r_k.astype(f32)), axis=-1, keepdims=True) * heads(v)

    mean = jnp.mean(o_sum, axis=-1, keepdims=True)
    var = jnp.mean(jnp.square(o_sum - mean), axis=-1, keepdims=True)
    o = (o_sum - mean) * lax.rsqrt(var + GN_EPS)
    o = o.reshape(bsz, t_len, RWKV_DIM) * ln_w.astype(f32) + ln_b.astype(f32)
    o = o + bonus.reshape(bsz, t_len, RWKV_DIM)
    return o * g


def encoder_layer(x, ffn1_norm, ffn1_w_gate, ffn1_w_up, ffn1_w_down,
                  mix_norm, w_in, conv_w, mu_shift, w0, w2, a0, a2, g2,
                  k_k, k_a, r_k, ln_x_w, ln_x_b, w_out,
                  ffn2_norm, ffn2_w_gate, ffn2_w_up, ffn2_w_down):
    x = x + FFN_RESIDUAL_SCALE * swiglu(rms_norm(x, ffn1_norm), ffn1_w_gate, ffn1_w_up, ffn1_w_down)
    h = rms_norm(x, mix_norm)
    z = h @ w_in
    zc, zr = z[..., :CONV_COLS], z[..., CONV_COLS:]
    y_conv = short_conv_mixer(zc, conv_w)
    y_rwkv = rwkv7_mixer(centred_token_shift(zr, mu_shift), w0, w2, a0, a2, g2,
                         k_k, k_a, r_k, ln_x_w, ln_x_b).astype(x.dtype)
    x = x + jnp.concatenate([y_conv, y_rwkv], axis=-1) @ w_out
    x = x + FFN_RESIDUAL_SCALE * swiglu(rms_norm(x, ffn2_norm), ffn2_w_gate, ffn2_w_up, ffn2_w_down)
    return x


def setup_inputs(seed: int = 0) -> dict:
    key = jax.random.key(seed)
    ks = jax.random.split(key, 32)
    f32 = jnp.float32
    nrm = lambda k, shape, scale: jax.random.normal(k, shape, f32) * scale
    gain = lambda k, shape: 1.0 + 0.02 * jax.random.normal(k, shape, f32)
    L = DEPTH
    return {
        "x_prompt": nrm(ks[0], (BATCH, SEQ, D_MODEL), 1.0),
        "x_sample": nrm(ks[1], (DEC_BATCH, DEC_SEQ, D_MODEL), 1.0),
        "ffn1_norm": gain(ks[2], (L, D_MODEL)),
        "ffn1_w_gate": nrm(ks[3], (L, D_MODEL, D_FF), D_MODEL ** -0.5),
        "ffn1_w_up": nrm(ks[4], (L, D_MODEL, D_FF), D_MODEL ** -0.5),
        "ffn1_w_down": nrm(ks[5], (L, D_FF, D_MODEL), D_FF ** -0.5),
        "mix_norm": gain(ks[6], (L, D_MODEL)),
        "w_in": nrm(ks[7], (L, D_MODEL, IN_COLS), D_MODEL ** -0.5),
        "conv_w": nrm(ks[8], (L, CONV_WIDTH, CONV_DIM), 0.5),
        "mu_shift": jax.random.uniform(ks[9], (L, RWKV_COLS), f32, 0.0, 0.5),
        "w0": jax.random.uniform(ks[10], (L, 2, RWKV_DIM), f32, -6.0, -1.0),
        "w2": nrm(ks[11], (L, 2, DECAY_LORA, RWKV_DIM), 0.1),
        "a0": nrm(ks[12], (L, 2, RWKV_DIM), 0.1),
        "a2": nrm(ks[13], (L, 2, AAA_LORA, RWKV_DIM), 0.1),
        "g2": nrm(ks[14], (L, GATE_LORA, RWKV_DIM), GATE_LORA ** -0.5),
        "k_k": 0.85 + nrm(ks[15], (L, RWKV_DIM), 0.02),
        "k_a": gain(ks[16], (L, RWKV_DIM)),
        "r_k": nrm(ks[17], (L, RWKV_DIM), 0.1),
        "ln_x_w": gain(ks[18], (L, RWKV_DIM)),
        "ln_x_b": nrm(ks[19], (L, RWKV_DIM), 0.01),
        "w_out": nrm(ks[20], (L, D_MODEL, D_MODEL), D_MODEL ** -0.5),
        "ffn2_norm": gain(ks[21], (L, D_MODEL)),
        "ffn2_w_gate": nrm(ks[22], (L, D_MODEL, D_FF), D_MODEL ** -0.5),
        "ffn2_w_up": nrm(ks[23], (L, D_MODEL, D_FF), D_MODEL ** -0.5),
        "ffn2_w_down": nrm(ks[24], (L, D_FF, D_MODEL), D_FF ** -0.5),
        "final_norm": gain(ks[25], (D_MODEL,)),
    }


def reference(x_prompt, x_sample, ffn1_norm, ffn1_w_gate, ffn1_w_up, ffn1_w_down,
              mix_norm, w_in, conv_w, mu_shift, w0, w2, a0, a2, g2,
              k_k, k_a, r_k, ln_x_w, ln_x_b, w_out,
              ffn2_norm, ffn2_w_gate, ffn2_w_up, ffn2_w_down, final_norm):
    def trunk(x):
        for l in range(DEPTH):
            x = encoder_layer(x, ffn1_norm[l], ffn1_w_gate[l], ffn1_w_up[l], ffn1_w_down[l],
                              mix_norm[l], w_in[l], conv_w[l], mu_shift[l], w0[l], w2[l],
                              a0[l], a2[l], g2[l], k_k[l], k_a[l], r_k[l],
                              ln_x_w[l], ln_x_b[l], w_out[l],
                              ffn2_norm[l], ffn2_w_gate[l], ffn2_w_up[l], ffn2_w_down[l])
        return rms_norm(x, final_norm)

    y_prompt = trunk(x_prompt)
    y_sample = trunk(x_sample)
    return (y_prompt, y_sample)
```

```python
import math
from contextlib import ExitStack
import numpy as np
import concourse.bass as bass
import concourse.mybir as mybir
from concourse.bass_utils import run_bass_kernel_spmd

F32 = mybir.dt.float32
BF16 = mybir.dt.bfloat16
AF = mybir.ActivationFunctionType
ALU = mybir.AluOpType
AX = mybir.AxisListType

NCORES = 8
MODE = "split"
FULL = dict(D=2048, DFF=5632, T=2048, NSEG=3)
RMS_EPS = 1e-6
GN_EPS = 64e-5
CH = 64


class _Stop(Exception):
    pass


class Tok:
    __slots__ = ("sem", "val", "eng")

    def __init__(self, sem, val, eng):
        self.sem, self.val, self.eng = sem, val, eng


class Buf:
    __slots__ = ("w", "r")

    def __init__(self):
        self.w = None
        self.r = {}


class Eng:
    def __init__(self, ctx, name, h):
        self.ctx, self.name, self.h = ctx, name, h
        self.seen = {}
        self.nsem = 0
        self.new_sem()

    def new_sem(self):
        self.sem = self.ctx.es.enter_context(self.ctx.nc.semaphore(f"s_{self.name}{self.nsem}"))
        self.nsem += 1
        self.cnt = 0

    def wait(self, tok):
        if tok is None:
            return
        k = id(tok.sem)
        if self.seen.get(k, 0) < tok.val:
            self.h.wait_ge(tok.sem, tok.val)
            self.seen[k] = tok.val

    def stamp(self, ins):
        if self.cnt >= 30000:
            self.new_sem()
        self.cnt += 1
        ins.then_inc(self.sem, 1)
        return Tok(self.sem, self.cnt, self.name)


class Ctx:
    def __init__(self, nc, es):
        self.nc, self.es = nc, es
        self.E = {}
        for name, h in (("pe", nc.tensor), ("act", nc.scalar), ("dve", nc.vector),
                        ("pool", nc.gpsimd), ("sp", nc.sync)):
            self.E[name] = Eng(self, name, h)
        self.dsem = {}
        for q in ("sp", "pool"):
            self.dsem[q] = [[es.enter_context(nc.semaphore(f"d_{q}{i}")), 0] for i in range(24)]
        self.dptr = {"sp": 0, "pool": 0}
        self.rr = 0

    def _deps(self, e, R, W):
        eng = self.E[e]
        pe = e == "pe"
        for b in R:
            t = b.w
            if t is not None and not (pe and t.eng == "pe"):
                eng.wait(t)
        for b in W:
            t = b.w
            if t is not None and not (pe and t.eng == "pe"):
                eng.wait(t)
            for t in b.r.values():
                if not (pe and t.eng == "pe"):
                    eng.wait(t)

    def _mark(self, tok, R, W):
        for b in R:
            b.r[tok.eng if tok.eng != "dma" else id(tok.sem)] = tok
        for b in W:
            b.w = tok
            b.r = {}

    def op(self, e, fn, R=(), W=()):
        self._deps(e, R, W)
        tok = self.E[e].stamp(fn())
        self._mark(tok, R, W)
        return tok

    def mmg(self, items, R=(), W=()):
        self._deps("pe", R, W)
        ins = None
        for (o, l, r, st, sp) in items:
            ins = self.nc.tensor.matmul(o, l, r, start=st, stop=sp)
        tok = self.E["pe"].stamp(ins)
        self._mark(tok, R, W)
        return tok

    def dma(self, q, out, in_, R=(), W=()):
        self._deps(q, R, W)
        eng = self.E[q]
        ring = self.dsem[q]
        i = self.dptr[q]
        self.dptr[q] = (i + 1) % len(ring)
        sem, uses = ring[i]
        if uses >= 1800:
            sem = self.es.enter_context(self.nc.semaphore(f"d_{q}{i}_{self.rr}"))
            self.rr += 1
            ring[i] = [sem, 0]
            uses = 0
        elif uses > 0:
            eng.wait(Tok(sem, 16 * uses, "dma"))
        eng.h.dma_start(out=out, in_=in_).then_inc(sem, 16)
        ring[i][1] = uses + 1
        tok = Tok(sem, 16 * (uses + 1), "dma")
        self._mark(tok, R, W)
        return tok

    def barrier(self):
        toks = [Tok(e.sem, e.cnt, e.name) for e in self.E.values() if e.cnt > 0]
        for q in self.dsem:
            for sem, uses in self.dsem[q]:
                if uses > 0:
                    toks.append(Tok(sem, 16 * uses, "dma"))
        for e in self.E.values():
            for t in toks:
                if t.eng != e.name:
                    e.wait(t)


def APx(t, dims, off=0):
    return bass.AP(t.tensor, t.offset + off, [list(t.ap[0])] + [list(d) for d in dims])


def derive(cfg):
    D, DFF, T, NSEG = cfg["D"], cfg["DFF"], cfg["T"], cfg["NSEG"]
    c = dict(cfg)
    c["KC"] = D // 128
    c["NFF"] = DFF // 128
    c["CD"] = D // 2
    c["RD"] = D // 2
    c["CC"] = c["CD"] // 128
    c["NHP"] = c["RD"] // 128
    c["INC"] = 3 * c["CD"] + 3 * c["RD"] + 128 + 128 + 160
    c["NZC"] = (c["INC"] + 127) // 128
    c["NTOK"] = NSEG * T
    c["TB"] = min(512, T)
    c["NT"] = c["TB"] // 128
    c["CB"] = min(512, D)
    c["NB"] = D // c["CB"]
    c["NCH"] = T // CH
    o = 0
    pc = {}
    for nm, n in (("n1", c["KC"]), ("nm", c["KC"]), ("n2", c["KC"]), ("cw", 3 * c["CC"]),
                  ("mu", c["NZC"] - 3 * c["CC"]), ("kk", c["NHP"]), ("ka", c["NHP"]), ("rk", c["NHP"]),
                  ("a0", 2 * c["NHP"]), ("mk", 16)):
        pc[nm] = o
        o += n
    c["pc"] = pc
    c["NPC"] = o
    return c


def build(cfg):
    try:
        return _build(cfg)
    except _Stop as e:
        return e.nc


def _build(cfg):
    c = derive(cfg)
    D, DFF, T, NSEG = c["D"], c["DFF"], c["T"], c["NSEG"]
    KC, NFF, CD, RD, CC, NHP, NZC = c["KC"], c["NFF"], c["CD"], c["RD"], c["CC"], c["NHP"], c["NZC"]
    NTOK, TB, NT, CB, NB, NCH, pc, NPC = c["NTOK"], c["TB"], c["NT"], c["CB"], c["NB"], c["NCH"], c["pc"], c["NPC"]
    NBLK = NTOK // TB
    TQ = (T + 511) // 512 if T >= 512 else 1
    QW = min(512, T)
    NQ = T // QW

    nc = bass.Bass("TRN2", target_bir_lowering=False)
    dt = nc.dram_tensor
    mode = cfg.get("mode", "fused")
    pre, main_ = mode == "pre", mode == "main"
    ein = lambda name, shape: dt(name, shape, F32, kind="ExternalInput").ap()
    x_in = ein("x", [T if pre else NTOK, D])
    xh_in = ein("xh", [128, D])
    nffn = (1,) if pre else (1, 2)
    wg_in = [ein(f"wg{i}", [NFF, 128, KC * 128]) for i in nffn]
    wu_in = [ein(f"wu{i}", [NFF, 128, KC * 128]) for i in nffn]
    wd_in = [ein(f"wd{i}", [NFF, 128, D]) for i in nffn]
    win_in = ein("win", [NZC, 128, KC * 128])
    wo_in = None if pre else ein("wo", [KC, 128, D])
    pcol_in = ein("pcol", [128, NPC])
    cst_in = ein("cst", [128, 12 * 128])
    w2t_in = ein("w2t", [128, RD])
    a2t_in = ein("a2t", [128, RD])
    g2_in = ein("g2", [160, RD])
    w0r_in = ein("w0r", [1, 2 * RD])
    lnw_in = ein("lnw", [128, NHP * 64])
    lnb_in = ein("lnb", [128, NHP * 64])
    fng_in = None if pre else ein("fng", [1, D])
    out_d = None if pre else dt("out", [NTOK, D], F32, kind="ExternalOutput").ap()
    dbg = cfg.get("debug", False)
    skind = "ExternalOutput" if dbg else "Internal"
    x1s = dt("x1s", [NTOK, D], F32, kind=skind).ap()
    zT = dt("zT", [NZC * 128, NSEG, T + 2], F32, kind=skind).ap()
    yT = dt("yT", [D, NTOK], BF16, kind=skind).ap()
    gin = dt("gin", [128, NHP * 2 * 192], F32, kind="ExternalOutput" if pre else "Internal").ap()
    gout = dt("gout", [NCORES * 128, NHP * 2 * 192], F32, kind="ExternalInput" if main_ else "Internal").ap()

    with ExitStack() as es:
        es.enter_context(nc.allow_non_contiguous_dma(reason="single halo columns"))
        cx = Ctx(nc, es)
        uid = [0]

        def sb(st, name, shape, dty):
            uid[0] += 1
            return st.enter_context(nc.sbuf_tensor(f"sb{uid[0]}_{name}", shape, dty))
        ps = es.enter_context(nc.psum_tensor("ps", [128, 8 * 512], F32))
        PSB = [Buf() for _ in range(8)]
        bank = lambda b: ps[:, b * 512:(b + 1) * 512]

        pcol = sb(es, "pcol", [128, NPC], F32)
        cstf = sb(es, "cstf", [128, 12 * 128], F32)
        cstb = sb(es, "cstb", [128, 12 * 128], BF16)
        B_c = Buf()
        cx.dma("sp", pcol[:], pcol_in, W=[B_c])
        cx.dma("sp", cstf[:], cst_in, W=[B_c])
        cx.dma("pool", cstb[:], cst_in, W=[B_c])
        CI = dict(ident=0, bones=1, trif=2, trixf=3, trib=4, trixb=5, msf=6, mstf=7, mitf=8, msb=9, mstb=10, mitb=11)
        cf = lambda nm: cstf[:, CI[nm] * 128:(CI[nm] + 1) * 128]
        cb = lambda nm: cstb[:, CI[nm] * 128:(CI[nm] + 1) * 128]
        identb = cb("ident")

        def norm_transpose(st, xt, XB, ncol, hT, HB, nt, tagc):
            ss, junk, xn, XNB, SSB = st["ss"], st["junk"], st["xn"], st["XNB"], st["SSB"]
            for t in range(nt):
                cx.op("act", lambda t=t: nc.scalar.activation(out=junk[:], in_=xt[t][:], func=AF.Square,
                                                              accum_out=ss[:, 4 * t:4 * t + 1]), R=[XB[t]], W=[SSB[t]])
                cx.op("dve", lambda t=t: nc.vector.tensor_scalar(out=ss[:, 4 * t + 1:4 * t + 2], in0=ss[:, 4 * t:4 * t + 1],
                                                                 scalar1=1.0 / D, scalar2=RMS_EPS, op0=ALU.mult, op1=ALU.add),
                      R=[SSB[t]], W=[SSB[t]])
                cx.op("act", lambda t=t: nc.scalar.activation(out=ss[:, 4 * t + 2:4 * t + 3], in_=ss[:, 4 * t + 1:4 * t + 2],
                                                              func=AF.Sqrt), R=[SSB[t]], W=[SSB[t]])
                cx.op("dve", lambda t=t: nc.vector.reciprocal(out=ss[:, 4 * t + 3:4 * t + 4], in_=ss[:, 4 * t + 2:4 * t + 3]),
                      R=[SSB[t]], W=[SSB[t]])
                cx.op("dve", lambda t=t: nc.vector.tensor_scalar(out=xn[t][:], in0=xt[t][:], scalar1=ss[:, 4 * t + 3:4 * t + 4],
                                                                 scalar2=None, op0=ALU.mult), R=[XB[t], SSB[t]], W=[XNB[t]])
            for kc in range(KC):
                b = st["pb"][kc % 2]
                cx.mmg([(bank(b)[:, t * 128:(t + 1) * 128], xn[t][:, kc * 128:(kc + 1) * 128], identb, True, True)
                        for t in range(nt)], R=[XNB[t] for t in range(nt)] + [B_c], W=[PSB[b]])
                e = "act" if kc % 2 == 0 else "dve"
                if e == "act":
                    cx.op("act", lambda kc=kc, b=b: nc.scalar.activation(out=hT[:, kc, 0:nt * 128], in_=bank(b)[:, 0:nt * 128],
                                                                         func=AF.Copy, scale=pcol[:, ncol + kc:ncol + kc + 1]),
                          R=[PSB[b], B_c], W=[HB])
                else:
                    cx.op("dve", lambda kc=kc, b=b: nc.vector.tensor_scalar(out=hT[:, kc, 0:nt * 128], in0=bank(b)[:, 0:nt * 128],
                                                                            scalar1=pcol[:, ncol + kc:ncol + kc + 1], scalar2=None,
                                                                            op0=ALU.mult), R=[PSB[b], B_c], W=[HB])

        def wload(st, src, cols):
            i = st["wptr"]
            st["wptr"] = (i + 1) % len(st["wring"])
            slot, b = st["wring"][i], st["WRB"][i]
            cx.dma("pool", slot[:, 0:cols], src, W=[b])
            return slot, b

        def ffn(st, wg, wu, wd, hT, HB, aT, AB, xt, XB, nt):
            ntok = nt * 128
            for j in range(NFF):
                sg_, bg = wload(st, wg[j], KC * 128)
                su_, bu = wload(st, wu[j], KC * 128)
                pg, pu = 2 + (j % 2) * 2, 3 + (j % 2) * 2
                cx.mmg([(bank(pg)[:, 0:ntok], sg_[:, kc * 128:(kc + 1) * 128], hT[:, kc, 0:ntok], kc == 0, kc == KC - 1)
                        for kc in range(KC)], R=[bg, HB], W=[PSB[pg]])
                cx.mmg([(bank(pu)[:, 0:ntok], su_[:, kc * 128:(kc + 1) * 128], hT[:, kc, 0:ntok], kc == 0, kc == KC - 1)
                        for kc in range(KC)], R=[bu, HB], W=[PSB[pu]])
                sgt, SGB = st["sg"][j % 2], st["SGB"][j % 2]
                cx.op("act", lambda pg=pg, sgt=sgt: nc.scalar.activation(out=sgt[:, 0:ntok], in_=bank(pg)[:, 0:ntok], func=AF.Silu),
                      R=[PSB[pg]], W=[SGB])
                cx.op("dve", lambda pu=pu, sgt=sgt, j=j: nc.vector.tensor_tensor(out=aT[:, j, 0:ntok], in0=sgt[:, 0:ntok],
                                                                                   in1=bank(pu)[:, 0:ntok], op=ALU.mult),
                      R=[SGB, PSB[pu]], W=[AB])
            proj_tok(st, aT, AB, NFF, wd, xt, XB, nt, 0.5)

        def proj_tok(st, lT, LB, J, w, xt, XB, nt, scale):
            npb = max(1, min(NB, 8 // nt))
            for n0 in range(0, NB, npb):
                nbs = list(range(n0, min(NB, n0 + npb)))
                accs = [(t, n, (t * npb + (n - n0))) for t in range(nt) for n in nbs]
                for j in range(J):
                    wcols = len(nbs) * CB
                    sl, bw = wload(st, w[j][:, n0 * CB:n0 * CB + wcols], wcols)
                    for (t, n, bk) in accs:
                        cx.mmg([(bank(bk)[:, 0:CB], lT[:, j, t * 128:(t + 1) * 128], sl[:, (n - n0) * CB:(n - n0 + 1) * CB],
                                 j == 0, j == J - 1)], R=[bw, LB], W=[PSB[bk]])
                for (t, n, bk) in accs:
                    cx.op("dve", lambda t=t, n=n, bk=bk: nc.vector.scalar_tensor_tensor(
                        out=xt[t][:, n * CB:(n + 1) * CB], in0=bank(bk)[:, 0:CB], scalar=float(scale),
                        in1=xt[t][:, n * CB:(n + 1) * CB], op0=ALU.mult, op1=ALU.add), R=[PSB[bk]], W=[XB[t]])

        def alloc13(ph, ntile):
            st = {}
            st["xt"] = [sb(ph, f"xt{t}", [128, D], F32) for t in range(ntile)]
            st["XB"] = [Buf() for _ in range(ntile)]
            st["xn"] = [sb(ph, f"xn{t}", [128, D], BF16) for t in range(ntile)]
            st["XNB"] = [Buf() for _ in range(ntile)]
            st["junk"] = sb(ph, "junk", [128, D], BF16)
            st["ss"] = sb(ph, "ss", [128, 4 * ntile], F32)
            st["SSB"] = [Buf() for _ in range(ntile)]
            st["hT"] = sb(ph, "hT", [128, KC, ntile * 128], BF16)
            st["HB"] = Buf()
            st["aT"] = sb(ph, "aT", [128, NFF, ntile * 128], BF16)
            st["AB"] = Buf()
            st["sg"] = [sb(ph, f"sg{i}", [128, ntile * 128], F32) for i in range(2)]
            st["SGB"] = [Buf(), Buf()]
            NW = 6
            st["wring"] = [sb(ph, f"wr{i}", [128, D], BF16) for i in range(NW)]
            st["WRB"] = [Buf() for _ in range(NW)]
            st["wptr"] = 0
            st["pb"] = [0, 1]
            return st

        def stop_at(k):
            if cfg.get('phases', 99) == k:
                cx.barrier()
                e_ = _Stop()
                e_.nc = nc
                raise e_

        with ExitStack() as ph:
            st = alloc13(ph, NT)
            zst = [sb(ph, f"zst{i}", [128, TB], F32) for i in range(3)]
            ZSB = [Buf() for _ in range(3)]
            xt, XB, hT, HB, aT, AB = st["xt"], st["XB"], st["hT"], st["HB"], st["aT"], st["AB"]
            blocks = [(b, NT) for b in range(T // TB if pre else NBLK)] + [(-1, 1)]
            for (b, nt) in blocks:
                for t in range(nt):
                    src = x_in[b * TB + t * 128:b * TB + (t + 1) * 128, :] if b >= 0 else xh_in
                    cx.dma("sp", xt[t][:], src, W=[XB[t]])
                norm_transpose(st, xt, XB, pc["n1"], hT, HB, nt, 0)
                ffn(st, wg_in[0], wu_in[0], wd_in[0], hT, HB, aT, AB, xt, XB, nt)
                if b >= 0:
                    for t in range(nt):
                        cx.dma("sp", x1s[b * TB + t * 128:b * TB + (t + 1) * 128, :], xt[t][:], R=[XB[t]])
                norm_transpose(st, xt, XB, pc["nm"], hT, HB, nt, 1)
                ntok = nt * 128
                for j in range(NZC):
                    sw, bw = wload(st, win_in[j], KC * 128)
                    pz = 2 + (j % 4)
                    cx.mmg([(bank(pz)[:, 0:ntok], sw[:, kc * 128:(kc + 1) * 128], hT[:, kc, 0:ntok], kc == 0, kc == KC - 1)
                            for kc in range(KC)], R=[bw, HB], W=[PSB[pz]])
                    zs, zb = zst[j % 3], ZSB[j % 3]
                    if j % 2 == 0:
                        cx.op("act", lambda pz=pz, zs=zs: nc.scalar.copy(out=zs[:, 0:ntok], in_=bank(pz)[:, 0:ntok]), R=[PSB[pz]], W=[zb])
                    else:
                        cx.op("dve", lambda pz=pz, zs=zs: nc.vector.tensor_copy(out=zs[:, 0:ntok], in_=bank(pz)[:, 0:ntok]), R=[PSB[pz]], W=[zb])
                    if b >= 0:
                        seg, t0 = (b * TB) // T, (b * TB) % T
                        cx.dma("sp", zT[j * 128:(j + 1) * 128, seg, 1 + t0:1 + t0 + TB], zs[:, 0:TB], R=[zb])
                    else:
                        cx.dma("sp", zT[j * 128:(j + 1) * 128, 0, 0:1], zs[:, 0:1], R=[zb])
                        cx.dma("sp", zT[j * 128:(j + 1) * 128, 0, T + 1:T + 2], zs[:, 1:2], R=[zb])
            cx.barrier()
            stop_at(1)

        with ExitStack() as ph:
            TP = T + 2
            stg = [sb(ph, "stg0", [128, TP], F32)]
            stg.append(stg[0])
            STB = [Buf()]
            STB.append(STB[0])
            stp = [0]
            tmpA = sb(ph, "tmpA", [128, T], F32)
            tmpB = sb(ph, "tmpB", [128, T], F32)
            TAB, TBB = Buf(), Buf()
            TW = sb(ph, "TW", [128, T], BF16)
            ZA = sb(ph, "ZA", [128, T], BF16)
            SG0 = sb(ph, "SG0", [128, T], BF16)
            SG1 = sb(ph, "SG1", [32, T], BF16)
            SHB = Buf()
            w2t = sb(ph, "w2t", [128, RD], BF16)
            a2t = sb(ph, "a2t", [128, RD], BF16)
            g2a = sb(ph, "g2a", [128, RD], BF16)
            g2b = sb(ph, "g2b", [32, RD], BF16)
            w0r = sb(ph, "w0r", [1, 2 * RD], BF16)
            ones1 = sb(ph, "ones1", [1, 128], BF16)
            onesc = sb(ph, "onesc", [128, 1], BF16)
            lnw = sb(ph, "lnw", [128, NHP * 64], F32)
            lnb = sb(ph, "lnb", [128, NHP * 64], F32)
            omka = sb(ph, "omka", [128, NHP], F32)
            B_p = Buf()
            cx.dma("pool", w2t[:], w2t_in, W=[B_p])
            cx.dma("pool", a2t[:], a2t_in, W=[B_p])
            cx.dma("pool", g2a[:], g2_in[0:128, :], W=[B_p])
            cx.dma("pool", g2b[:], g2_in[128:160, :], W=[B_p])
            cx.dma("pool", w0r[:], w0r_in, W=[B_p])
            cx.dma("sp", lnw[:], lnw_in, W=[B_p])
            cx.dma("sp", lnb[:], lnb_in, W=[B_p])
            cx.op("dve", lambda: nc.vector.memset(ones1[:], 1.0), W=[B_p])
            cx.op("dve", lambda: nc.vector.memset(onesc[:], 1.0), W=[B_p])
            cx.op("dve", lambda: nc.vector.tensor_scalar(out=omka[:], in0=pcol[:, pc["ka"]:pc["ka"] + NHP], scalar1=-1.0,
                                                          scalar2=1.0, op0=ALU.mult, op1=ALU.add), R=[B_c], W=[B_p])
            Rt = sb(ph, "Rt", [128, T], F32)
            Kx = sb(ph, "Kx", [128, T], F32)
            Vt = sb(ph, "Vt", [128, T], BF16)
            KK = sb(ph, "KK", [128, T], F32)
            RKV = [Buf(), Buf(), Buf(), Buf()]
            VC = sb(ph, "VC", [128, NCH, 64], BF16)
            VCB = Buf()
            Aa = sb(ph, "Aa", [128, T], F32)
            LW = sb(ph, "LW", [128, T // 128, 128], F32)
            E1 = sb(ph, "E1", [128, T], F32)
            E2 = sb(ph, "E2", [128, T], F32)
            KD = tmpB
            AAB, LWB, E1B, E2B, KDB = Buf(), Buf(), Buf(), Buf(), TBB
            ATt = sb(ph, "ATt", [128, T], BF16)
            BTt = sb(ph, "BTt", [128, T], BF16)
            KTt = sb(ph, "KTt", [128, T], BF16)
            RTt = sb(ph, "RTt", [128, T], BF16)
            RKt = sb(ph, "RKt", [128, T], BF16)
            ATB, BTB, KTB, RTB, RKB = Buf(), Buf(), Buf(), Buf(), Buf()
            GCt = sb(ph, "GCt", [128, NCH], F32)
            GCB = Buf()
            bns = sb(ph, "bns", [128, NCH], F32)
            BNB = Buf()
            OS = sb(ph, "OS", [128, NCH, 64], F32)
            OSB = Buf()
            GR = cfg.get("GR", 4)
            NR = 2
            garr = lambda nm, w, dty: [sb(ph, f"{nm}{i}", [128, GR, w], dty) for i in range(NR)]
            X_ = [garr("X_a", 128, BF16), garr("X_b", 128, BF16)]
            Y_ = [garr("Y_a", 128, BF16), garr("Y_b", 128, BF16)]
            P_ = [garr("P_a", 128, BF16), garr("P_b", 128, BF16)]
            XB_ = [[Buf() for _ in range(NR)] for _ in range(2)]
            YB_ = [[Buf() for _ in range(NR)] for _ in range(2)]
            PB_ = [[Buf() for _ in range(NR)] for _ in range(2)]
            names = ["LAKT", "MRBT", "MRKT", "Atok", "Btok", "Ktok", "W1T"]
            SM = {n: garr(n, 128, BF16) for n in names}
            SMB = {n: [Buf() for _ in range(NR)] for n in names}
            W2p = garr("W2p", 64, BF16)
            W2 = garr("W2", 64, F32)
            W2pB = [Buf() for _ in range(NR)]
            W2B = [Buf() for _ in range(NR)]
            Ut = [sb(ph, f"Ut{i}", [128, 192], BF16) for i in range(3)]
            UB = [Buf() for _ in range(3)]
            Hf = sb(ph, "Hf", [128, 192], F32)
            Hb = sb(ph, "Hb", [128, 192], BF16)
            Htmp = sb(ph, "Htmp", [128, 192], F32)
            HFB, HBB, HTB = Buf(), Buf(), Buf()
            Hin = sb(ph, "Hin", [128, NHP * 2, 64], F32)
            HINB = Buf()
            gtoks = []
            GA = sb(ph, "GA", [128, NCORES, 192], F32)
            GAB = Buf()
            PTt = sb(ph, "PTt", [128, 128], F32)
            PTB = Buf()
            ytok = sb(ph, "ytok", [128, NCH, 64], BF16)
            YKB = Buf()
            YTs, YTB = RKt, RKB
            gn1 = sb(ph, "gn1", [128, NCH], F32)
            gn2 = sb(ph, "gn2", [128, NCH], F32)
            GNB = Buf()
            cx.op("dve", lambda: nc.vector.memset(ps[:, 0:2048], 0.0), W=[PSB[0], PSB[1], PSB[2], PSB[3]])
            bigq = lambda q: ps[:, (4 + q) * 512:(4 + q) * 512 + QW]
            bdp, cbp, gset = [0], [0], [0]

            def nbd():
                i = bdp[0]
                bdp[0] = (i + 1) % 4
                return i

            def ncb():
                i = cbp[0]
                cbp[0] = 1 - i
                return 6 + i

            def load_shift(chunk, seg, outs, func=None, rows=128, mucol=None):
                i = stp[0]
                stp[0] = 1 - i
                s, SB_ = stg[i], STB[i]
                cx.dma("sp", s[0:rows, :] if seg == 0 else s[0:rows, 1:T + 1],
                       zT[chunk * 128:chunk * 128 + rows, seg, :] if seg == 0 else zT[chunk * 128:chunk * 128 + rows, seg, 1:T + 1],
                       W=[SB_])
                if seg != 0:
                    cx.op("dve", lambda: nc.vector.memset(s[0:rows, 0:1], 0.0), W=[SB_])
                    cx.op("dve", lambda: nc.vector.memset(s[0:rows, T + 1:T + 2], 0.0), W=[SB_])
                out, OB, odt = outs
                mu = pcol[0:rows, mucol:mucol + 1]
                cx.op("dve", lambda: nc.vector.tensor_tensor(out=tmpA[0:rows, :], in0=s[0:rows, 0:T], in1=s[0:rows, 2:T + 2], op=ALU.add),
                      R=[SB_], W=[TAB])
                cx.op("dve", lambda: nc.vector.scalar_tensor_tensor(out=tmpA[0:rows, :], in0=tmpA[0:rows, :], scalar=0.5,
                                                                      in1=s[0:rows, 1:T + 1], op0=ALU.mult, op1=ALU.subtract),
                      R=[SB_, TAB], W=[TAB])
                if func is None:
                    cx.op("dve", lambda: nc.vector.scalar_tensor_tensor(out=out, in0=tmpA[0:rows, :], scalar=mu,
                                                                          in1=s[0:rows, 1:T + 1], op0=ALU.mult, op1=ALU.add),
                          R=[SB_, TAB, B_c], W=[OB])
                else:
                    cx.op("dve", lambda: nc.vector.scalar_tensor_tensor(out=tmpB[0:rows, :], in0=tmpA[0:rows, :], scalar=mu,
                                                                          in1=s[0:rows, 1:T + 1], op0=ALU.mult, op1=ALU.add),
                          R=[SB_, TAB, B_c], W=[TBB])
                    cx.op("act", lambda: nc.scalar.activation(out=out, in_=tmpB[0:rows, :], func=func), R=[TBB], W=[OB])

            def conv_seg(seg):
                for cc in range(CC):
                    sb_, SBb = E1, E1B
                    sc_, SBc = stg[0], STB[0]
                    chb, chc, chx = cc, CC + cc, 2 * CC + cc
                    cx.dma("sp", sb_[:, 0:T], zT[chb * 128:(chb + 1) * 128, seg, 1:T + 1], W=[SBb])
                    lo, hi = (0, TP) if seg == 0 else (1, T + 1)
                    cx.dma("sp", sc_[:, lo:hi], zT[chc * 128:(chc + 1) * 128, seg, lo:hi], W=[SBc])
                    cx.dma("sp", tmpB[:, :], zT[chx * 128:(chx + 1) * 128, seg, 1:T + 1], W=[TBB])
                    if seg == 0:
                        cx.dma("sp", gn1[:, 0:1], zT[chx * 128:(chx + 1) * 128, seg, 0:1], W=[GNB])
                        cx.dma("sp", gn1[:, 1:2], zT[chx * 128:(chx + 1) * 128, seg, T + 1:T + 2], W=[GNB])
                    else:
                        cx.op("dve", lambda: nc.vector.memset(gn1[:, 0:2], 0.0), W=[GNB])
                        cx.op("dve", lambda: nc.vector.memset(sc_[:, 0:1], 0.0), W=[SBc])
                        cx.op("dve", lambda: nc.vector.memset(sc_[:, T + 1:T + 2], 0.0), W=[SBc])
                    cx.op("dve", lambda: nc.vector.tensor_tensor(out=sc_[:, 1:T + 1], in0=sc_[:, 1:T + 1], in1=tmpB[:, :], op=ALU.mult),
                          R=[TBB], W=[SBc])
                    cx.op("dve", lambda: nc.vector.tensor_tensor(out=sc_[:, 0:1], in0=sc_[:, 0:1], in1=gn1[:, 0:1], op=ALU.mult),
                          R=[GNB], W=[SBc])
                    cx.op("dve", lambda: nc.vector.tensor_tensor(out=sc_[:, T + 1:T + 2], in0=sc_[:, T + 1:T + 2], in1=gn1[:, 1:2], op=ALU.mult),
                          R=[GNB], W=[SBc])
                    cw = lambda k: pcol[:, pc["cw"] + k * CC + cc:pc["cw"] + k * CC + cc + 1]
                    cx.op("dve", lambda: nc.vector.tensor_scalar(out=tmpA[:, :], in0=sc_[:, 0:T], scalar1=cw(0), scalar2=None, op0=ALU.mult),
                          R=[SBc, B_c], W=[TAB])
                    cx.op("dve", lambda: nc.vector.scalar_tensor_tensor(out=tmpA[:, :], in0=sc_[:, 1:T + 1], scalar=cw(1), in1=tmpA[:, :],
                                                                          op0=ALU.mult, op1=ALU.add), R=[SBc, B_c, TAB], W=[TAB])
                    cx.op("dve", lambda: nc.vector.scalar_tensor_tensor(out=tmpA[:, :], in0=sc_[:, 2:T + 2], scalar=cw(2), in1=tmpA[:, :],
                                                                          op0=ALU.mult, op1=ALU.add), R=[SBc, B_c, TAB], W=[TAB])
                    cx.op("dve", lambda: nc.vector.tensor_tensor(out=YTs[:, :], in0=tmpA[:, :], in1=sb_[:, 0:T], op=ALU.mult),
                          R=[TAB, SBb], W=[YTB])
                    cx.dma("sp", yT[cc * 128:(cc + 1) * 128, seg * T:(seg + 1) * T], YTs[:, :], R=[YTB])

            def seg_shared(seg):
                base = 3 * CC + 3 * NHP
                mub = pc["mu"]
                load_shift(base, seg, (TW[:, :], SHB, BF16), func=AF.Tanh, mucol=mub + 3 * NHP)
                load_shift(base + 1, seg, (ZA[:, :], SHB, BF16), func=None, mucol=mub + 3 * NHP + 1)
                load_shift(base + 2, seg, (SG0[:, :], SHB, BF16), func=AF.Sigmoid, mucol=mub + 3 * NHP + 2)
                load_shift(base + 3, seg, (SG1[:, :], SHB, BF16), func=AF.Sigmoid, rows=32, mucol=mub + 3 * NHP + 3)

            def hp_prep(seg, hp):
                mub = pc["mu"]
                load_shift(3 * CC + hp, seg, (Rt[:, :], RKV[0], F32), mucol=mub + hp)
                load_shift(3 * CC + NHP + hp, seg, (Kx[:, :], RKV[1], F32), mucol=mub + NHP + hp)
                load_shift(3 * CC + 2 * NHP + hp, seg, (Vt[:, :], RKV[2], BF16), mucol=mub + 2 * NHP + hp)
                cx.op("dve", lambda: nc.vector.tensor_scalar(out=KK[:, :], in0=Kx[:, :], scalar1=pcol[:, pc["kk"] + hp:pc["kk"] + hp + 1],
                                                              scalar2=None, op0=ALU.mult), R=[RKV[1], B_c], W=[RKV[3]])
                cx.op("act", lambda: nc.scalar.activation(out=RKt[:, :], in_=KK[:, :], func=AF.Square), R=[RKV[3]], W=[RKB])
                for q in range(NQ):
                    cx.mmg([(bigq(q), cb("bones"), RKt[:, q * QW:(q + 1) * QW], True, True)], R=[RKB, B_c], W=[PSB[4 + q]])
                    cx.op("dve", lambda q=q: nc.vector.tensor_scalar(out=tmpA[:, q * QW:(q + 1) * QW], in0=bigq(q), scalar1=1e-24,
                                                                      scalar2=None, op0=ALU.max), R=[PSB[4 + q]], W=[TAB])
                cx.op("act", lambda: nc.scalar.activation(out=tmpA[:, :], in_=tmpA[:, :], func=AF.Sqrt), R=[TAB], W=[TAB])
                cx.op("dve", lambda: nc.vector.reciprocal(out=tmpA[:, :], in_=tmpA[:, :]), R=[TAB], W=[TAB])
                cx.op("dve", lambda: nc.vector.tensor_tensor(out=KK[:, :], in0=KK[:, :], in1=tmpA[:, :], op=ALU.mult), R=[TAB, RKV[3]], W=[RKV[3]])
                for c0 in range(NCH):
                    q, off = (c0 * 64) // QW, (c0 * 64) % QW
                    items = []
                    for h in range(2):
                        items.append((ps[h * 64:(h + 1) * 64, (4 + q) * 512 + off:(4 + q) * 512 + off + 64],
                                      Vt[h * 64:(h + 1) * 64, c0 * 64:(c0 + 1) * 64],
                                      cstb[h * 64:(h + 1) * 64, CI["ident"] * 128 + h * 64:CI["ident"] * 128 + (h + 1) * 64], True, True))
                    cx.mmg(items, R=[RKV[2], B_c], W=[PSB[4 + q]])
                for q in range(NQ):
                    cx.op("act", lambda q=q: nc.scalar.copy(out=VC[:, q * (QW // 64):(q + 1) * (QW // 64), :],
                                                            in_=bigq(q).rearrange("p (c v) -> p c v", v=64)), R=[PSB[4 + q]], W=[VCB])

            def dir_prep(seg, hp, d):
                dn = "f" if d == 0 else "b"
                hs = slice(d * 64, (d + 1) * 64)
                cs = slice(hp * 128, (hp + 1) * 128)
                for q in range(NQ):
                    cx.mmg([(bigq(q), a2t[hs, cs], ZA[hs, q * QW:(q + 1) * QW], True, True)], R=[SHB, B_p], W=[PSB[4 + q]])
                    cx.op("act", lambda q=q: nc.scalar.activation(out=Aa[:, q * QW:(q + 1) * QW], in_=bigq(q), func=AF.Sigmoid,
                                                                  bias=pcol[:, pc["a0"] + d * NHP + hp:pc["a0"] + d * NHP + hp + 1]),
                          R=[PSB[4 + q], B_c], W=[AAB])
                for tt in range(T // 128):
                    q, off = (tt * 128) // QW, (tt * 128) % QW
                    o = ps[:, (4 + q) * 512 + off:(4 + q) * 512 + off + 128]
                    cx.mmg([(o, TW[hs, tt * 128:(tt + 1) * 128], w2t[hs, cs], True, False),
                            (o, ones1[0:1, 0:128], w0r[0:1, d * RD + hp * 128:d * RD + (hp + 1) * 128], False, True)],
                           R=[SHB, B_p], W=[PSB[4 + q]])
                for q in range(NQ):
                    cx.op("act", lambda q=q: nc.scalar.activation(out=LW[:, q * (QW // 128):(q + 1) * (QW // 128), :],
                                                                  in_=bigq(q).rearrange("p (t c) -> p t c", c=128), func=AF.Sigmoid),
                          R=[PSB[4 + q]], W=[LWB])
                for (trn, which) in (("tri" + dn, 0), ("trix" + dn, 1)):
                    for tt in range(T // 128):
                        q, off = (tt * 128) // QW, (tt * 128) % QW
                        o = ps[:, (4 + q) * 512 + off:(4 + q) * 512 + off + 128]
                        cx.mmg([(o, LW[:, tt, :], cf(trn), True, True)], R=[LWB, B_c], W=[PSB[4 + q]])
                    if which == 0:
                        for q in range(NQ):
                            cx.op("act", lambda q=q: nc.scalar.activation(out=E1[:, q * QW:(q + 1) * QW], in_=bigq(q), func=AF.Exp),
                                  R=[PSB[4 + q]], W=[E1B])
                            cx.op("act", lambda q=q: nc.scalar.activation(out=E2[:, q * QW:(q + 1) * QW], in_=bigq(q), func=AF.Exp, scale=-1.0),
                                  R=[PSB[4 + q]], W=[E2B])
                        cx.op("dve", lambda: nc.vector.tensor_tensor(out=RTt[:, :], in0=Rt[:, :], in1=E1[:, :], op=ALU.mult),
                              R=[RKV[0], E1B], W=[RTB])
                        gcol = 63 if d == 0 else 0
                        cx.op("dve", lambda: nc.vector.tensor_copy(out=GCt[:, :], in_=APx(E1[:, :], [[64, NCH]], off=gcol)),
                              R=[E1B], W=[GCB])
                        cx.op("dve", lambda: nc.vector.tensor_tensor(out=tmpA[:, :], in0=KK[:, :], in1=Aa[:, :], op=ALU.mult),
                              R=[RKV[3], AAB], W=[TAB])
                        cx.op("dve", lambda: nc.vector.tensor_tensor(out=BTt[:, :], in0=tmpA[:, :], in1=E2[:, :], op=ALU.mult),
                              R=[TAB, E2B], W=[BTB])
                        cx.op("dve", lambda: nc.vector.tensor_scalar(out=tmpA[:, :], in0=Aa[:, :],
                                                                      scalar1=pcol[:, pc["ka"] + hp:pc["ka"] + hp + 1],
                                                                      scalar2=omka[:, hp:hp + 1], op0=ALU.mult, op1=ALU.add),
                              R=[AAB, B_c, B_p], W=[TAB])
                        cx.op("dve", lambda: nc.vector.tensor_tensor(out=KD[:, :], in0=Kx[:, :], in1=tmpA[:, :], op=ALU.mult),
                              R=[TAB, RKV[1]], W=[KDB])
                        cx.op("dve", lambda: nc.vector.tensor_tensor(out=KTt[:, :], in0=KD[:, :], in1=E2[:, :], op=ALU.mult),
                              R=[KDB, E2B], W=[KTB])
                        cx.op("dve", lambda: nc.vector.scalar_tensor_tensor(out=RKt[:, :], in0=Rt[:, :],
                                                                              scalar=pcol[:, pc["rk"] + hp:pc["rk"] + hp + 1],
                                                                              in1=KD[:, :], op0=ALU.mult, op1=ALU.mult),
                              R=[RKV[0], KDB, B_c], W=[RKB])
                    else:
                        for q in range(NQ):
                            cx.op("act", lambda q=q: nc.scalar.activation(out=E1[:, q * QW:(q + 1) * QW], in_=bigq(q), func=AF.Exp),
                                  R=[PSB[4 + q]], W=[E1B])
                        cx.op("dve", lambda: nc.vector.scalar_tensor_tensor(out=ATt[:, :], in0=KK[:, :], scalar=-1.0, in1=E1[:, :],
                                                                              op0=ALU.mult, op1=ALU.mult), R=[RKV[3], E1B], W=[ATB])
                q0 = 0
                for c0 in range(NCH):
                    items = [(ps[h * 64:(h + 1) * 64, 4 * 512 + c0:4 * 512 + c0 + 1], RKt[h * 64:(h + 1) * 64, c0 * 64:(c0 + 1) * 64],
                              onesc[h * 64:(h + 1) * 64, 0:1], True, True) for h in range(2)]
                    cx.mmg(items, R=[RKB, B_p], W=[PSB[4]])
                if d == 0:
                    cx.op("dve", lambda: nc.vector.tensor_copy(out=bns[:, :], in_=ps[:, 4 * 512:4 * 512 + NCH]), R=[PSB[4]], W=[BNB])
                else:
                    cx.op("dve", lambda: nc.vector.tensor_tensor(out=bns[:, :], in0=bns[:, :], in1=ps[:, 4 * 512:4 * 512 + NCH], op=ALU.add),
                          R=[PSB[4]], W=[BNB])

            def blk(ap, h):
                return ap[h * 64:(h + 1) * 64, h * 64:(h + 1) * 64]

            def chunk_loop(hp, d, aug, outmode):
                dn = "f" if d == 0 else "b"
                NS = 192 if aug else 64
                order = list(range(NCH)) if d == 0 else list(range(NCH - 1, -1, -1))
                for g0 in range(0, NCH, GR):
                    grp = order[g0:g0 + GR]
                    n_ = len(grp)
                    st_ = gset[0]
                    gset[0] = 1 - st_
                    csl = [slice(c0 * 64, (c0 + 1) * 64) for c0 in grp]
                    bankv = lambda b_, w: ps[:, b_ * 512:b_ * 512 + n_ * w].rearrange("p (g c) -> p g c", c=w)
                    bcast = lambda ap_: APx(ap_, [[0, n_], [1, 128]])

                    def bd_mm(lt, LB_, rt, RB_, ident_rhs=False):
                        b_ = nbd()
                        items = []
                        for gi in range(n_):
                            for h in range(2):
                                o = ps[h * 64:(h + 1) * 64, b_ * 512 + gi * 128 + h * 64:b_ * 512 + gi * 128 + (h + 1) * 64]
                                rr = blk(identb, h) if ident_rhs else rt[h * 64:(h + 1) * 64, csl[gi]]
                                items.append((o, lt[h * 64:(h + 1) * 64, csl[gi]], rr, True, True))
                        cx.mmg(items, R=[LB_, RB_] if not ident_rhs else [LB_, B_c], W=[PSB[b_]])
                        return b_

                    def masked(b_, mask, outt, OB_):
                        cx.op("dve", lambda: nc.vector.tensor_tensor(out=outt[:, 0:n_, :], in0=bankv(b_, 128), in1=bcast(cf(mask)), op=ALU.mult),
                              R=[PSB[b_], B_c], W=[OB_])
                    X0, Y0, P0 = X_[0][st_], Y_[0][st_], P_[0][st_]
                    masked(bd_mm(ATt, ATB, BTt, BTB), "ms" + dn, X0, XB_[0][st_])
                    masked(bd_mm(BTt, BTB, ATt, ATB), "mst" + dn, Y0, YB_[0][st_])
                    cx.op("dve", lambda: nc.vector.tensor_tensor(out=P0[:, 0:n_, :], in0=Y0[:, 0:n_, :], in1=bcast(cb("ident")), op=ALU.add),
                          R=[YB_[0][st_], B_c], W=[PB_[0][st_]])
                    masked(bd_mm(KTt, KTB, ATt, ATB), "mst" + dn, SM["LAKT"][st_], SMB["LAKT"][st_])
                    if not aug:
                        masked(bd_mm(BTt, BTB, RTt, RTB), "mit" + dn, SM["MRBT"][st_], SMB["MRBT"][st_])
                        masked(bd_mm(KTt, KTB, RTt, RTB), "mit" + dn, SM["MRKT"][st_], SMB["MRKT"][st_])
                    for (nm, src, SB_) in (("Atok", ATt, ATB), ("Btok", BTt, BTB), ("Ktok", KTt, KTB)):
                        b_ = bd_mm(src, SB_, None, None, ident_rhs=True)
                        cx.op("act", lambda nm=nm, b_=b_: nc.scalar.copy(out=SM[nm][st_][:, 0:n_, :], in_=bankv(b_, 128)),
                              R=[PSB[b_]], W=[SMB[nm][st_]])
                    cur = 0
                    for lev in range(5):
                        nxt = 1 - cur
                        last = lev == 4
                        Xc, Yc, Pc = X_[cur][st_], Y_[cur][st_], P_[cur][st_]
                        Xn, Yn, Pn = X_[nxt][st_], Y_[nxt][st_], P_[nxt][st_]
                        b_ = nbd()
                        cx.mmg([(ps[:, b_ * 512 + gi * 128:b_ * 512 + (gi + 1) * 128], Yc[:, gi, :], Xc[:, gi, :], True, True) for gi in range(n_)],
                               R=[YB_[cur][st_], XB_[cur][st_]], W=[PSB[b_]])
                        cx.op("act", lambda b_=b_, Xn=Xn: nc.scalar.copy(out=Xn[:, 0:n_, :], in_=bankv(b_, 128)), R=[PSB[b_]], W=[XB_[nxt][st_]])
                        if not last:
                            b_ = nbd()
                            cx.mmg([(ps[:, b_ * 512 + gi * 128:b_ * 512 + (gi + 1) * 128], Xc[:, gi, :], Yc[:, gi, :], True, True) for gi in range(n_)],
                                   R=[YB_[cur][st_], XB_[cur][st_]], W=[PSB[b_]])
                            cx.op("dve", lambda b_=b_, Yn=Yn: nc.vector.tensor_copy(out=Yn[:, 0:n_, :], in_=bankv(b_, 128)), R=[PSB[b_]], W=[YB_[nxt][st_]])
                        b_ = nbd()
                        cx.mmg([(ps[:, b_ * 512 + gi * 128:b_ * 512 + (gi + 1) * 128], Xn[:, gi, :], Pc[:, gi, :], True, True) for gi in range(n_)],
                               R=[XB_[nxt][st_], PB_[cur][st_]], W=[PSB[b_]])
                        cx.op("dve", lambda b_=b_, Pn=Pn, Pc=Pc: nc.vector.tensor_tensor(out=Pn[:, 0:n_, :], in0=bankv(b_, 128), in1=Pc[:, 0:n_, :], op=ALU.add),
                              R=[PSB[b_], PB_[cur][st_]], W=[PB_[nxt][st_]])
                        cur = nxt
                    PF, PFB = P_[cur][st_], PB_[cur][st_]
                    LAKT, Atok, Btok, Ktok, W1T = SM["LAKT"][st_], SM["Atok"][st_], SM["Btok"][st_], SM["Ktok"][st_], SM["W1T"][st_]
                    b_ = ncb()
                    cx.mmg([(ps[:, b_ * 512 + gi * 64:b_ * 512 + (gi + 1) * 64], LAKT[:, gi, :], VC[:, grp[gi], :], True, True) for gi in range(n_)],
                           R=[SMB["LAKT"][st_], VCB], W=[PSB[b_]])
                    cx.op("act", lambda b_=b_: nc.scalar.copy(out=W2p[st_][:, 0:n_, :], in_=bankv(b_, 64)), R=[PSB[b_]], W=[W2pB[st_]])
                    b_ = nbd()
                    cx.mmg([(ps[:, b_ * 512 + gi * 128:b_ * 512 + (gi + 1) * 128], Atok[:, gi, :], PF[:, gi, :], True, True) for gi in range(n_)],
                           R=[SMB["Atok"][st_], PFB], W=[PSB[b_]])
                    cx.op("dve", lambda b_=b_: nc.vector.tensor_copy(out=W1T[:, 0:n_, :], in_=bankv(b_, 128)), R=[PSB[b_]], W=[SMB["W1T"][st_]])
                    b_ = ncb()
                    cx.mmg([(ps[:, b_ * 512 + gi * 64:b_ * 512 + (gi + 1) * 64], PF[:, gi, :], W2p[st_][:, gi, :], True, True) for gi in range(n_)],
                           R=[PFB, W2pB[st_]], W=[PSB[b_]])
                    cx.op("act", lambda b_=b_: nc.scalar.copy(out=W2[st_][:, 0:n_, :], in_=bankv(b_, 64)), R=[PSB[b_]], W=[W2B[st_]])
                    for gi, c0 in enumerate(grp):
                        ui = c0 % 3
                        U, UB_ = Ut[ui], UB[ui]
                        qa = ps[:, 4 * 512:4 * 512 + NS]
                        cx.mmg([(qa, W1T[:, gi, :], Hb[:, 0:NS], True, True)], R=[SMB["W1T"][st_], HBB], W=[PSB[4]])
                        cx.op("dve", lambda qa=qa, U=U, gi=gi: nc.vector.tensor_tensor(out=U[:, 0:64], in0=qa[:, 0:64], in1=W2[st_][:, gi, :], op=ALU.add),
                              R=[PSB[4], W2B[st_]], W=[UB_])
                        if aug:
                            cx.op("act", lambda qa=qa, U=U: nc.scalar.copy(out=U[:, 64:192], in_=qa[:, 64:192]), R=[PSB[4]], W=[UB_])
                        else:
                            ob = ncb()
                            oa = ps[:, ob * 512:ob * 512 + 64]
                            items = [(oa[h * 64:(h + 1) * 64, :], RTt[h * 64:(h + 1) * 64, csl[gi]], Hb[h * 64:(h + 1) * 64, 0:64], True, False)
                                     for h in range(2)]
                            items += [(oa, SM["MRKT"][st_][:, gi, :], VC[:, c0, :], False, False),
                                      (oa, SM["MRBT"][st_][:, gi, :], U[:, 0:64], False, True)]
                            cx.mmg(items, R=[SMB["MRKT"][st_], SMB["MRBT"][st_], VCB, UB_, RTB, HBB], W=[PSB[ob]])
                            if outmode == 0:
                                cx.op("act", lambda oa=oa, c0=c0: nc.scalar.copy(out=OS[:, c0, :], in_=oa), R=[PSB[ob]], W=[OSB])
                            else:
                                cx.op("dve", lambda oa=oa, c0=c0: nc.vector.tensor_tensor(out=OS[:, c0, :], in0=oa, in1=OS[:, c0, :], op=ALU.add),
                                      R=[PSB[ob], OSB], W=[OSB])
                        ha = ps[:, 5 * 512:5 * 512 + NS]
                        cx.mmg([(ha, Btok[:, gi, :], U[:, 0:NS], True, False),
                                (ha[:, 0:64], Ktok[:, gi, :], VC[:, c0, :], False, True)],
                               R=[SMB["Btok"][st_], SMB["Ktok"][st_], UB_, VCB], W=[PSB[5]])
                        cx.op("dve", lambda ha=ha: nc.vector.tensor_tensor(out=Htmp[:, 0:NS], in0=ha, in1=Hf[:, 0:NS], op=ALU.add),
                              R=[PSB[5], HFB], W=[HTB])
                        cx.op("dve", lambda c0=c0: nc.vector.tensor_scalar(out=Hf[:, 0:NS], in0=Htmp[:, 0:NS], scalar1=GCt[:, c0:c0 + 1],
                                                                             scalar2=None, op0=ALU.mult), R=[HTB, GCB], W=[HFB])
                        cx.op("act", lambda c0=c0: nc.scalar.activation(out=Hb[:, 0:NS], in_=Htmp[:, 0:NS], func=AF.Copy,
                                                                         scale=GCt[:, c0:c0 + 1]), R=[HTB, GCB], W=[HBB])

            def init_state(hp, d, mode):
                if mode == "zero":
                    cx.op("dve", lambda: nc.vector.memset(Hf[:, :], 0.0), W=[HFB])
                    cx.op("dve", lambda: nc.vector.memset(Hb[:, :], 0.0), W=[HBB])
                elif mode == "aug":
                    cx.op("dve", lambda: nc.vector.memset(Hf[:, 0:64], 0.0), W=[HFB])
                    cx.op("dve", lambda: nc.vector.tensor_copy(out=Hf[:, 64:192], in_=cf("ident")), R=[B_c], W=[HFB])
                    cx.op("dve", lambda: nc.vector.memset(Hb[:, 0:64], 0.0), W=[HBB])
                    cx.op("dve", lambda: nc.vector.tensor_copy(out=Hb[:, 64:192], in_=cf("ident")), R=[B_c], W=[HBB])
                else:
                    i = hp * 2 + d
                    cx.op("dve", lambda: nc.vector.tensor_copy(out=Hf[:, 0:64], in_=Hin[:, i, :]), R=[HINB], W=[HFB])
                    cx.op("dve", lambda: nc.vector.tensor_copy(out=Hb[:, 0:64], in_=Hin[:, i, :]), R=[HINB], W=[HBB])

            def post(seg, hp):
                NC_ = NCH
                bc = lambda t_: APx(t_[:, :], [[1, NC_], [0, 64]])
                lw = APx(lnw[:, hp * 64:(hp + 1) * 64], [[0, NC_], [1, 64]])
                lb = APx(lnb[:, hp * 64:(hp + 1) * 64], [[0, NC_], [1, 64]])
                OSv = OS[:, :, :]
                cx.op("dve", lambda: nc.vector.tensor_reduce(out=gn1[:, :], in_=OSv, axis=AX.X, op=ALU.add), R=[OSB], W=[GNB])
                cx.op("dve", lambda: nc.vector.tensor_scalar(out=gn1[:, :], in0=gn1[:, :], scalar1=1.0 / 64, scalar2=None, op0=ALU.mult),
                      R=[GNB], W=[GNB])
                cx.op("dve", lambda: nc.vector.tensor_tensor(out=OSv, in0=OSv, in1=bc(gn1), op=ALU.subtract), R=[GNB, OSB], W=[OSB])
                tA = tmpA[:, 0:NC_ * 64].rearrange("p (c v) -> p c v", v=64)
                cx.op("act", lambda: nc.scalar.activation(out=tA, in_=OSv, func=AF.Square), R=[OSB], W=[TAB])
                cx.op("dve", lambda: nc.vector.tensor_reduce(out=gn2[:, :], in_=tA, axis=AX.X, op=ALU.add), R=[TAB], W=[GNB])
                cx.op("dve", lambda: nc.vector.tensor_scalar(out=gn2[:, :], in0=gn2[:, :], scalar1=1.0 / 64, scalar2=GN_EPS,
                                                              op0=ALU.mult, op1=ALU.add), R=[GNB], W=[GNB])
                cx.op("act", lambda: nc.scalar.activation(out=gn2[:, :], in_=gn2[:, :], func=AF.Sqrt), R=[GNB], W=[GNB])
                cx.op("dve", lambda: nc.vector.reciprocal(out=gn2[:, :], in_=gn2[:, :]), R=[GNB], W=[GNB])
                cx.op("dve", lambda: nc.vector.tensor_tensor(out=OSv, in0=OSv, in1=bc(gn2), op=ALU.mult), R=[GNB, OSB], W=[OSB])
                cx.op("dve", lambda: nc.vector.tensor_tensor(out=OSv, in0=OSv, in1=lw, op=ALU.mult), R=[B_p, OSB], W=[OSB])
                cx.op("dve", lambda: nc.vector.tensor_tensor(out=OSv, in0=OSv, in1=lb, op=ALU.add), R=[B_p, OSB], W=[OSB])
                cx.op("dve", lambda: nc.vector.tensor_tensor(out=tA, in0=VC[:, :, :], in1=bc(bns), op=ALU.mult), R=[VCB, BNB], W=[TAB])
                cx.op("dve", lambda: nc.vector.tensor_tensor(out=OSv, in0=OSv, in1=tA, op=ALU.add), R=[TAB, OSB], W=[OSB])
                for c0 in range(NC_):
                    q, off = (c0 * 64) // QW, (c0 * 64) % QW
                    items = []
                    for h in range(2):
                        o = ps[h * 64:(h + 1) * 64, (4 + q) * 512 + off:(4 + q) * 512 + off + 64]
                        gc_ = slice(hp * 128 + h * 64, hp * 128 + (h + 1) * 64)
                        items.append((o, SG0[:, c0 * 64:(c0 + 1) * 64], g2a[:, gc_], True, False))
                        items.append((o, SG1[:, c0 * 64:(c0 + 1) * 64], g2b[:, gc_], False, True))
                    cx.mmg(items, R=[SHB, B_p], W=[PSB[4 + q]])
                for q in range(NQ):
                    cw_ = QW // 64
                    cx.op("dve", lambda q=q: nc.vector.tensor_tensor(out=ytok[:, q * cw_:(q + 1) * cw_, :], in0=OS[:, q * cw_:(q + 1) * cw_, :],
                                                                      in1=bigq(q).rearrange("p (c v) -> p c v", v=64), op=ALU.mult),
                          R=[OSB, PSB[4 + q]], W=[YKB])
                for c0 in range(NC_):
                    q, off = (c0 * 64) // QW, (c0 * 64) % QW
                    items = []
                    for h in range(2):
                        o = ps[h * 64:(h + 1) * 64, (4 + q) * 512 + off:(4 + q) * 512 + off + 64]
                        items.append((o, ytok[h * 64:(h + 1) * 64, c0, :], blk(identb, h), True, True))
                    cx.mmg(items, R=[YKB, B_c], W=[PSB[4 + q]])
                for q in range(NQ):
                    cx.op("act", lambda q=q: nc.scalar.copy(out=YTs[:, q * QW:(q + 1) * QW], in_=bigq(q)), R=[PSB[4 + q]], W=[YTB])
                cx.dma("sp", yT[CD + hp * 128:CD + (hp + 1) * 128, seg * T:(seg + 1) * T], YTs[:, :], R=[YTB])

            def run_seg(seg, mode):
                seg_shared(seg)
                stop_at(2.01)
                if mode != "aug":
                    conv_seg(seg)
                for hp in range(NHP):
                    hp_prep(seg, hp)
                    stop_at(2.02)
                    for d in range(2):
                        dir_prep(seg, hp, d)
                        stop_at(2.03)
                        init_state(hp, d, mode)
                        chunk_loop(hp, d, mode == "aug", d)
                        if mode == "aug":
                            i_ = hp * 2 + d
                            gtoks.append(cx.dma("sp", gin[:, i_ * 192:(i_ + 1) * 192], Hf[:, :], R=[HFB]))
                    if mode != "aug":
                        post(seg, hp)

            stop_at(1.5)
            B_g = Buf()
            if not main_:
                run_seg(0, "aug")
                stop_at(2.1)
                if pre:
                    cx.barrier()
                    e_ = _Stop()
                    e_.nc = nc
                    raise e_
                for tg in gtoks:
                    cx.E["pool"].wait(tg)
                cc_tok = cx.E["pool"].stamp(nc.gpsimd.collective_compute("AllGather", ALU.bypass, replica_groups=[list(range(NCORES))],
                                                                         ins=[gin], outs=[gout]))
                B_g.w = cc_tok
                stop_at(2.2)
            for seg in range(1, NSEG):
                run_seg(seg, "zero")
            stop_at(2.3)
            gv = gout.rearrange("(k p) (i c) -> p k i c", p=128, c=192)
            for hp in range(NHP):
                for d in range(2):
                    i = hp * 2 + d
                    cx.dma("sp", GA[:, :, :], gv[:, :, i, :], R=[B_g], W=[GAB])
                    cx.op("dve", lambda: nc.vector.memset(Hf[:, 0:64], 0.0), W=[HFB])
                    ks = list(range(NCORES)) if d == 0 else list(range(NCORES - 1, -1, -1))
                    for k in ks:
                        g = ncb()
                        cx.mmg([(ps[:, g * 512:g * 512 + 128], GA[:, k, 64:192], cf("ident"), True, True)], R=[GAB, B_c], W=[PSB[g]])
                        cx.op("act", lambda g=g: nc.scalar.copy(out=PTt[:, :], in_=ps[:, g * 512:g * 512 + 128]), R=[PSB[g]], W=[PTB])
                        g = ncb()
                        cx.mmg([(ps[:, g * 512:g * 512 + 64], PTt[:, :], Hf[:, 0:64], True, True)], R=[PTB, HFB], W=[PSB[g]])
                        cx.op("dve", lambda g=g, k=k: nc.vector.tensor_tensor(out=Htmp[:, 0:64], in0=ps[:, g * 512:g * 512 + 64], in1=GA[:, k, 0:64], op=ALU.add),
                              R=[PSB[g], GAB], W=[HTB])
                        cx.op("dve", lambda: nc.vector.tensor_tensor(out=Htmp[:, 0:64], in0=Htmp[:, 0:64], in1=Hf[:, 0:64], op=ALU.subtract),
                              R=[HTB, HFB], W=[HTB])
                        mc = pcol[:, pc["mk"] + d * 8 + k:pc["mk"] + d * 8 + k + 1]
                        cx.op("dve", lambda mc=mc: nc.vector.scalar_tensor_tensor(out=Hf[:, 0:64], in0=Htmp[:, 0:64], scalar=mc, in1=Hf[:, 0:64],
                                                                                    op0=ALU.mult, op1=ALU.add), R=[HTB, HFB, B_c], W=[HFB])
                    cx.op("dve", lambda i=i: nc.vector.tensor_copy(out=Hin[:, i, :], in_=Hf[:, 0:64]), R=[HFB], W=[HINB])
            stop_at(2.4)
            run_seg(0, "hin")
            cx.barrier()
            stop_at(2.5)

        with ExitStack() as ph:
            st = alloc13(ph, NT)
            yTt = sb(ph, "yTt", [128, KC, TB], BF16)
            YB3 = Buf()
            fng = sb(ph, "fng", [128, D], F32)
            FGB = Buf()
            cx.dma("sp", fng[:], bass.AP(fng_in.tensor, 0, [[0, 128], [1, D]]), W=[FGB])
            xt, XB, hT, HB, aT, AB = st["xt"], st["XB"], st["hT"], st["HB"], st["aT"], st["AB"]
            otoks = []
            for b in range(NBLK):
                for t in range(NT):
                    cx.dma("sp", xt[t][:], x1s[b * TB + t * 128:b * TB + (t + 1) * 128, :], W=[XB[t]])
                for kc in range(KC):
                    cx.dma("sp", yTt[:, kc, :], yT[kc * 128:(kc + 1) * 128, b * TB:(b + 1) * TB], W=[YB3])
                proj_tok(st, yTt, YB3, KC, wo_in, xt, XB, NT, 1.0)
                norm_transpose(st, xt, XB, pc["n2"], hT, HB, NT, 2)
                ffn(st, wg_in[1], wu_in[1], wd_in[1], hT, HB, aT, AB, xt, XB, NT)
                ss, junk, SSB = st["ss"], st["junk"], st["SSB"]
                for t in range(NT):
                    cx.op("act", lambda t=t: nc.scalar.activation(out=junk[:], in_=xt[t][:], func=AF.Square, accum_out=ss[:, 4 * t:4 * t + 1]),
                          R=[XB[t]], W=[SSB[t]])
                    cx.op("dve", lambda t=t: nc.vector.tensor_scalar(out=ss[:, 4 * t + 1:4 * t + 2], in0=ss[:, 4 * t:4 * t + 1], scalar1=1.0 / D,
                                                                     scalar2=RMS_EPS, op0=ALU.mult, op1=ALU.add), R=[SSB[t]], W=[SSB[t]])
                    cx.op("act", lambda t=t: nc.scalar.activation(out=ss[:, 4 * t + 2:4 * t + 3], in_=ss[:, 4 * t + 1:4 * t + 2], func=AF.Sqrt),
                          R=[SSB[t]], W=[SSB[t]])
                    cx.op("dve", lambda t=t: nc.vector.reciprocal(out=ss[:, 4 * t + 3:4 * t + 4], in_=ss[:, 4 * t + 2:4 * t + 3]),
                          R=[SSB[t]], W=[SSB[t]])
                    cx.op("dve", lambda t=t: nc.vector.scalar_tensor_tensor(out=xt[t][:], in0=xt[t][:], scalar=ss[:, 4 * t + 3:4 * t + 4],
                                                                              in1=fng[:], op0=ALU.mult, op1=ALU.mult),
                          R=[SSB[t], FGB], W=[XB[t]])
                    otoks.append(cx.dma("sp", out_d[b * TB + t * 128:b * TB + (t + 1) * 128, :], xt[t][:], R=[XB[t]]))
            for tk in otoks[-24:]:
                cx.E["sp"].wait(tk)
            cx.barrier()
    return nc


def host_prep(cfg, inputs, xs, xhs, core):
    c = derive(cfg)
    D, KC, NFF, CD, RD, CC, NHP, NZC, pc, NPC = c["D"], c["KC"], c["NFF"], c["CD"], c["RD"], c["CC"], c["NHP"], c["NZC"], c["pc"], c["NPC"]
    f = np.float32
    m = {"x": np.ascontiguousarray(xs, f), "xh": np.ascontiguousarray(xhs, f)}
    return m


def shared_prep(cfg, I):
    c = derive(cfg)
    D, DFF, KC, NFF, CD, RD, CC, NHP, NZC, pc, NPC = (c["D"], c["DFF"], c["KC"], c["NFF"], c["CD"], c["RD"], c["CC"], c["NHP"],
                                                      c["NZC"], c["pc"], c["NPC"])
    f = np.float32
    sh = {}

    def chunked(w):
        N = w.shape[1]
        return np.ascontiguousarray(w.reshape(KC, 128, N // 128, 128).transpose(2, 1, 0, 3).reshape(N // 128, 128, KC * 128), f)

    for i, nm in ((1, "ffn1"), (2, "ffn2")):
        sh[f"wg{i}"] = chunked(I[f"{nm}_w_gate"][0])
        sh[f"wu{i}"] = chunked(I[f"{nm}_w_up"][0])
        sh[f"wd{i}"] = np.ascontiguousarray(I[f"{nm}_w_down"][0].reshape(NFF, 128, D), f)
    win = np.zeros((D, NZC * 128), f)
    win[:, :c["INC"]] = I["w_in"][0]
    sh["win"] = chunked(win)
    sh["wo"] = np.ascontiguousarray(I["w_out"][0].reshape(KC, 128, D), f)
    pcol = np.zeros((128, NPC), f)
    col = lambda v: np.asarray(v, f).reshape(-1, 128).T
    pcol[:, pc["n1"]:pc["n1"] + KC] = col(I["ffn1_norm"][0])
    pcol[:, pc["nm"]:pc["nm"] + KC] = col(I["mix_norm"][0])
    pcol[:, pc["n2"]:pc["n2"] + KC] = col(I["ffn2_norm"][0])
    for k in range(3):
        pcol[:, pc["cw"] + k * CC:pc["cw"] + (k + 1) * CC] = col(I["conv_w"][0, k])
    mu = np.zeros(((NZC - 3 * CC) * 128,), f)
    mu[:I["mu_shift"].shape[1]] = I["mu_shift"][0]
    pcol[:, pc["mu"]:pc["mu"] + NZC - 3 * CC] = col(mu)
    pcol[:, pc["kk"]:pc["kk"] + NHP] = col(I["k_k"][0])
    pcol[:, pc["ka"]:pc["ka"] + NHP] = col(I["k_a"][0])
    pcol[:, pc["rk"]:pc["rk"] + NHP] = col(I["r_k"][0])
    for d in range(2):
        pcol[:, pc["a0"] + d * NHP:pc["a0"] + (d + 1) * NHP] = col(I["a0"][0, d])
    sh["pcol"] = pcol
    idx = np.arange(128)
    same = (idx[:, None] // 64) == (idx[None, :] // 64)
    s_ = idx[:, None] % 64
    t_ = idx[None, :] % 64
    sc = -math.exp(-0.5)
    mats = [np.eye(128), same * 1.0,
            same * (s_ <= t_) * sc, same * (s_ < t_) * sc, same * (s_ >= t_) * sc, same * (s_ > t_) * sc,
            same * (t_ < s_), same * (s_ < t_), same * (s_ <= t_),
            same * (t_ > s_), same * (s_ > t_), same * (s_ >= t_)]
    sh["cst"] = np.ascontiguousarray(np.concatenate([np.asarray(a, f) for a in mats], axis=1), f)
    sh["w2t"] = np.ascontiguousarray(I["w2"][0].reshape(128, RD), f)
    sh["a2t"] = np.ascontiguousarray(I["a2"][0].reshape(128, RD), f)
    sh["g2"] = np.ascontiguousarray(I["g2"][0], f)
    sh["w0r"] = np.ascontiguousarray(I["w0"][0].reshape(1, 2 * RD), f)
    lw = np.zeros((128, NHP * 64), f)
    lb = np.zeros((128, NHP * 64), f)
    for hp in range(NHP):
        for h in range(2):
            lw[h * 64:(h + 1) * 64, hp * 64:(hp + 1) * 64] = I["ln_x_w"][0][(2 * hp + h) * 64:(2 * hp + h + 1) * 64][None, :]
            lb[h * 64:(h + 1) * 64, hp * 64:(hp + 1) * 64] = I["ln_x_b"][0][(2 * hp + h) * 64:(2 * hp + h + 1) * 64][None, :]
    sh["lnw"], sh["lnb"] = lw, lb
    sh["fng"] = np.ascontiguousarray(I["final_norm"].reshape(1, D), f)
    return sh, c


def run(cfg, I):
    sh, c = shared_prep(cfg, I)
    T, D, pc = c["T"], c["D"], c["pc"]
    xp = np.asarray(I["x_prompt"], np.float32)[0]
    xsamp = np.asarray(I["x_sample"], np.float32)
    in_maps = []
    for core in range(NCORES):
        xs = np.concatenate([xp[core * T:(core + 1) * T], xsamp[2 * core], xsamp[2 * core + 1]], axis=0)
        xh = np.zeros((128, D), np.float32)
        if core > 0:
            xh[0] = xp[core * T - 1]
        if core < NCORES - 1:
            xh[1] = xp[(core + 1) * T]
        m = dict(sh)
        pcol = sh["pcol"].copy()
        for k in range(NCORES):
            pcol[:, pc["mk"] + k] = 1.0 if k < core else 0.0
            pcol[:, pc["mk"] + 8 + k] = 1.0 if k > core else 0.0
        m["pcol"] = pcol
        m["x"] = np.ascontiguousarray(xs)
        m["xh"] = xh
        in_maps.append(m)
    if cfg.get("mode", "fused") == "split":
        pre_keys = ("xh", "wg1", "wu1", "wd1", "win", "pcol", "cst", "w2t", "a2t", "g2", "w0r", "lnw", "lnb")
        pre_maps = []
        for m in in_maps:
            pm = {k: m[k] for k in pre_keys}
            pm["x"] = np.ascontiguousarray(m["x"][0:T])
            pre_maps.append(pm)
        nc1 = build(dict(cfg, mode="pre"))
        r1 = run_bass_kernel_spmd(nc1, pre_maps, core_ids=list(range(NCORES)))
        gall = np.ascontiguousarray(np.concatenate([np.asarray(r1.results[k]["gin"], np.float32) for k in range(NCORES)], axis=0))
        for m in in_maps:
            m["gout"] = gall
        nc = build(dict(cfg, mode="main"))
    else:
        nc = build(cfg)
    res = run_bass_kernel_spmd(nc, in_maps, core_ids=list(range(NCORES)))
    yp = np.zeros((1, NCORES * T, D), np.float32)
    ysm = np.zeros((2 * NCORES, T, D), np.float32)
    for core in range(NCORES):
        o = res.results[core]["out"]
        yp[0, core * T:(core + 1) * T] = o[0:T]
        ysm[2 * core] = o[T:2 * T]
        ysm[2 * core + 1] = o[2 * T:3 * T]
    return (yp, ysm), res


def kernel(**inputs):
    I = {k: np.asarray(v) for k, v in inputs.items()}
    (yp, ysm), _ = run(dict(FULL, mode=MODE), I)
    return (yp, ysm)
```

```python
import math
from contextlib import ExitStack
import numpy as np
import concourse.bass as bass
import concourse.mybir as mybir
from concourse.bass_utils import run_bass_kernel_spmd

F32 = mybir.dt.float32
BF16 = mybir.dt.bfloat16
AF = mybir.ActivationFunctionType
ALU = mybir.AluOpType
AX = mybir.AxisListType

NCORES = 8
MODE = "split"
FULL = dict(D=2048, DFF=5632, T=2048, NSEG=3)
RMS_EPS = 1e-6
GN_EPS = 64e-5
CH = 64


class _Stop(Exception):
    pass


class Tok:
    __slots__ = ("sem", "val", "eng")

    def __init__(self, sem, val, eng):
        self.sem, self.val, self.eng = sem, val, eng


class Buf:
    __slots__ = ("w", "r")

    def __init__(self):
        self.w = None
        self.r = {}


class Eng:
    def __init__(self, ctx, name, h):
        self.ctx, self.name, self.h = ctx, name, h
        self.seen = {}
        self.nsem = 0
        self.new_sem()

    def new_sem(self):
        self.sem = self.ctx.es.enter_context(self.ctx.nc.semaphore(f"s_{self.name}{self.nsem}"))
        self.nsem += 1
        self.cnt = 0

    def wait(self, tok):
        if tok is None:
            return
        k = id(tok.sem)
        if self.seen.get(k, 0) < tok.val:
            self.h.wait_ge(tok.sem, tok.val)
            self.seen[k] = tok.val

    def stamp(self, ins):
        if self.cnt >= 30000:
            self.new_sem()
        self.cnt += 1
        ins.then_inc(self.sem, 1)
        return Tok(self.sem, self.cnt, self.name)


class Ctx:
    def __init__(self, nc, es):
        self.nc, self.es = nc, es
        self.E = {}
        for name, h in (("pe", nc.tensor), ("act", nc.scalar), ("dve", nc.vector),
                        ("pool", nc.gpsimd), ("sp", nc.sync)):
            self.E[name] = Eng(self, name, h)
        self.dsem = {}
        for q in ("sp", "pool"):
            self.dsem[q] = [[es.enter_context(nc.semaphore(f"d_{q}{i}")), 0] for i in range(24)]
        self.dptr = {"sp": 0, "pool": 0}
        self.rr = 0

    def _deps(self, e, R, W):
        eng = self.E[e]
        pe = e == "pe"
        for b in R:
            t = b.w
            if t is not None and not (pe and t.eng == "pe"):
                eng.wait(t)
        for b in W:
            t = b.w
            if t is not None and not (pe and t.eng == "pe"):
                eng.wait(t)
            for t in b.r.values():
                if not (pe and t.eng == "pe"):
                    eng.wait(t)

    def _mark(self, tok, R, W):
        for b in R:
            b.r[tok.eng if tok.eng != "dma" else id(tok.sem)] = tok
        for b in W:
            b.w = tok
            b.r = {}

    def op(self, e, fn, R=(), W=()):
        self._deps(e, R, W)
        tok = self.E[e].stamp(fn())
        self._mark(tok, R, W)
        return tok

    def mmg(self, items, R=(), W=()):
        self._deps("pe", R, W)
        ins = None
        for (o, l, r, st, sp) in items:
            ins = self.nc.tensor.matmul(o, l, r, start=st, stop=sp)
        tok = self.E["pe"].stamp(ins)
        self._mark(tok, R, W)
        return tok

    def dma(self, q, out, in_, R=(), W=()):
        self._deps(q, R, W)
        eng = self.E[q]
        ring = self.dsem[q]
        i = self.dptr[q]
        self.dptr[q] = (i + 1) % len(ring)
        sem, uses = ring[i]
        if uses >= 1800:
            sem = self.es.enter_context(self.nc.semaphore(f"d_{q}{i}_{self.rr}"))
            self.rr += 1
            ring[i] = [sem, 0]
            uses = 0
        elif uses > 0:
            eng.wait(Tok(sem, 16 * uses, "dma"))
        eng.h.dma_start(out=out, in_=in_).then_inc(sem, 16)
        ring[i][1] = uses + 1
        tok = Tok(sem, 16 * (uses + 1), "dma")
        self._mark(tok, R, W)
        return tok

    def barrier(self):
        toks = [Tok(e.sem, e.cnt, e.name) for e in self.E.values() if e.cnt > 0]
        for q in self.dsem:
            for sem, uses in self.dsem[q]:
                if uses > 0:
                    toks.append(Tok(sem, 16 * uses, "dma"))
        for e in self.E.values():
            for t in toks:
                if t.eng != e.name:
                    e.wait(t)


def APx(t, dims, off=0):
    return bass.AP(t.tensor, t.offset + off, [list(t.ap[0])] + [list(d) for d in dims])


def derive(cfg):
    D, DFF, T, NSEG = cfg["D"], cfg["DFF"], cfg["T"], cfg["NSEG"]
    c = dict(cfg)
    c["KC"] = D // 128
    c["NFF"] = DFF // 128
    c["CD"] = D // 2
    c["RD"] = D // 2
    c["CC"] = c["CD"] // 128
    c["NHP"] = c["RD"] // 128
    c["INC"] = 3 * c["CD"] + 3 * c["RD"] + 128 + 128 + 160
    c["NZC"] = (c["INC"] + 127) // 128
    c["NTOK"] = NSEG * T
    c["TB"] = min(512, T)
    c["NT"] = c["TB"] // 128
    c["CB"] = min(512, D)
    c["NB"] = D // c["CB"]
    c["NCH"] = T // CH
    o = 0
    pc = {}
    for nm, n in (("n1", c["KC"]), ("nm", c["KC"]), ("n2", c["KC"]), ("cw", 3 * c["CC"]),
                  ("mu", c["NZC"] - 3 * c["CC"]), ("kk", c["NHP"]), ("ka", c["NHP"]), ("rk", c["NHP"]),
                  ("a0", 2 * c["NHP"]), ("mk", 16)):
        pc[nm] = o
        o += n
    c["pc"] = pc
    c["NPC"] = o
    return c


def build(cfg):
    try:
        return _build(cfg)
    except _Stop as e:
        return e.nc


def _build(cfg):
    c = derive(cfg)
    D, DFF, T, NSEG = c["D"], c["DFF"], c["T"], c["NSEG"]
    KC, NFF, CD, RD, CC, NHP, NZC = c["KC"], c["NFF"], c["CD"], c["RD"], c["CC"], c["NHP"], c["NZC"]
    NTOK, TB, NT, CB, NB, NCH, pc, NPC = c["NTOK"], c["TB"], c["NT"], c["CB"], c["NB"], c["NCH"], c["pc"], c["NPC"]
    NBLK = NTOK // TB
    TQ = (T + 511) // 512 if T >= 512 else 1
    QW = min(512, T)
    NQ = T // QW

    nc = bass.Bass("TRN2", target_bir_lowering=False)
    dt = nc.dram_tensor
    mode = cfg.get("mode", "fused")
    pre, main_ = mode == "pre", mode == "main"
    ein = lambda name, shape: dt(name, shape, F32, kind="ExternalInput").ap()
    x_in = ein("x", [T if pre else NTOK, D])
    xh_in = ein("xh", [128, D])
    nffn = (1,) if pre else (1, 2)
    wg_in = [ein(f"wg{i}", [NFF, 128, KC * 128]) for i in nffn]
    wu_in = [ein(f"wu{i}", [NFF, 128, KC * 128]) for i in nffn]
    wd_in = [ein(f"wd{i}", [NFF, 128, D]) for i in nffn]
    win_in = ein("win", [NZC, 128, KC * 128])
    wo_in = None if pre else ein("wo", [KC, 128, D])
    pcol_in = ein("pcol", [128, NPC])
    cst_in = ein("cst", [128, 12 * 128])
    w2t_in = ein("w2t", [128, RD])
    a2t_in = ein("a2t", [128, RD])
    g2_in = ein("g2", [160, RD])
    w0r_in = ein("w0r", [1, 2 * RD])
    lnw_in = ein("lnw", [128, NHP * 64])
    lnb_in = ein("lnb", [128, NHP * 64])
    fng_in = None if pre else ein("fng", [1, D])
    out_d = None if pre else dt("out", [NTOK, D], F32, kind="ExternalOutput").ap()
    dbg = cfg.get("debug", False)
    skind = "ExternalOutput" if dbg else "Internal"
    k0 = "ExternalOutput" if pre else ("ExternalInput" if main_ else skind)
    x1s0 = dt("x1s0", [T, D], F32, kind=k0).ap()
    x1s12 = dt("x1s12", [NTOK - T, D], F32, kind=skind).ap()
    zT0 = dt("zT0", [NZC * 128, T + 2], F32, kind=k0).ap()
    zT12 = dt("zT12", [NZC * 128, NSEG - 1, T + 2], F32, kind=skind).ap()

    class _ZT:
        def __getitem__(self, idx):
            r, sg, c_ = idx
            return zT0[r, c_] if sg == 0 else zT12[r, sg - 1, c_]

    class _X1:
        def __getitem__(self, idx):
            r, c_ = idx
            if r.start < T:
                return x1s0[r, c_]
            return x1s12[r.start - T:r.stop - T, c_]
    zT, x1s = _ZT(), _X1()
    yT = dt("yT", [D, NTOK], BF16, kind=skind).ap()
    gin = dt("gin", [128, NHP * 2 * 192], F32, kind="ExternalOutput" if pre else "Internal").ap()
    gout = dt("gout", [NCORES * 128, NHP * 2 * 192], F32, kind="ExternalInput" if main_ else "Internal").ap()

    with ExitStack() as es:
        es.enter_context(nc.allow_non_contiguous_dma(reason="single halo columns"))
        cx = Ctx(nc, es)
        uid = [0]

        def sb(st, name, shape, dty):
            uid[0] += 1
            return st.enter_context(nc.sbuf_tensor(f"sb{uid[0]}_{name}", shape, dty))
        ps = es.enter_context(nc.psum_tensor("ps", [128, 8 * 512], F32))
        PSB = [Buf() for _ in range(8)]
        bank = lambda b: ps[:, b * 512:(b + 1) * 512]

        pcol = sb(es, "pcol", [128, NPC], F32)
        cstf = sb(es, "cstf", [128, 12 * 128], F32)
        cstb = sb(es, "cstb", [128, 12 * 128], BF16)
        B_c = Buf()
        cx.dma("sp", pcol[:], pcol_in, W=[B_c])
        cx.dma("sp", cstf[:], cst_in, W=[B_c])
        cx.dma("pool", cstb[:], cst_in, W=[B_c])
        CI = dict(ident=0, bones=1, trif=2, trixf=3, trib=4, trixb=5, msf=6, mstf=7, mitf=8, msb=9, mstb=10, mitb=11)
        cf = lambda nm: cstf[:, CI[nm] * 128:(CI[nm] + 1) * 128]
        cb = lambda nm: cstb[:, CI[nm] * 128:(CI[nm] + 1) * 128]
        identb = cb("ident")

        def norm_transpose(st, xt, XB, ncol, hT, HB, nt, tagc):
            ss, junk, xn, XNB, SSB = st["ss"], st["junk"], st["xn"], st["XNB"], st["SSB"]
            for t in range(nt):
                cx.op("act", lambda t=t: nc.scalar.activation(out=junk[:], in_=xt[t][:], func=AF.Square,
                                                              accum_out=ss[:, 4 * t:4 * t + 1]), R=[XB[t]], W=[SSB[t]])
                cx.op("dve", lambda t=t: nc.vector.tensor_scalar(out=ss[:, 4 * t + 1:4 * t + 2], in0=ss[:, 4 * t:4 * t + 1],
                                                                 scalar1=1.0 / D, scalar2=RMS_EPS, op0=ALU.mult, op1=ALU.add),
                      R=[SSB[t]], W=[SSB[t]])
                cx.op("act", lambda t=t: nc.scalar.activation(out=ss[:, 4 * t + 2:4 * t + 3], in_=ss[:, 4 * t + 1:4 * t + 2],
                                                              func=AF.Sqrt), R=[SSB[t]], W=[SSB[t]])
                cx.op("dve", lambda t=t: nc.vector.reciprocal(out=ss[:, 4 * t + 3:4 * t + 4], in_=ss[:, 4 * t + 2:4 * t + 3]),
                      R=[SSB[t]], W=[SSB[t]])
                cx.op("dve", lambda t=t: nc.vector.tensor_scalar(out=xn[t][:], in0=xt[t][:], scalar1=ss[:, 4 * t + 3:4 * t + 4],
                                                                 scalar2=None, op0=ALU.mult), R=[XB[t], SSB[t]], W=[XNB[t]])
            for kc in range(KC):
                b = st["pb"][kc % 2]
                cx.mmg([(bank(b)[:, t * 128:(t + 1) * 128], xn[t][:, kc * 128:(kc + 1) * 128], identb, True, True)
                        for t in range(nt)], R=[XNB[t] for t in range(nt)] + [B_c], W=[PSB[b]])
                e = "act" if kc % 2 == 0 else "dve"
                if e == "act":
                    cx.op("act", lambda kc=kc, b=b: nc.scalar.activation(out=hT[:, kc, 0:nt * 128], in_=bank(b)[:, 0:nt * 128],
                                                                         func=AF.Copy, scale=pcol[:, ncol + kc:ncol + kc + 1]),
                          R=[PSB[b], B_c], W=[HB])
                else:
                    cx.op("dve", lambda kc=kc, b=b: nc.vector.tensor_scalar(out=hT[:, kc, 0:nt * 128], in0=bank(b)[:, 0:nt * 128],
                                                                            scalar1=pcol[:, ncol + kc:ncol + kc + 1], scalar2=None,
                                                                            op0=ALU.mult), R=[PSB[b], B_c], W=[HB])

        def wload(st, src, cols):
            i = st["wptr"]
            st["wptr"] = (i + 1) % len(st["wring"])
            slot, b = st["wring"][i], st["WRB"][i]
            cx.dma("pool", slot[:, 0:cols], src, W=[b])
            return slot, b

        def ffn(st, wg, wu, wd, hT, HB, aT, AB, xt, XB, nt):
            ntok = nt * 128
            for j in range(NFF):
                sg_, bg = wload(st, wg[j], KC * 128)
                su_, bu = wload(st, wu[j], KC * 128)
                pg, pu = 2 + (j % 2) * 2, 3 + (j % 2) * 2
                cx.mmg([(bank(pg)[:, 0:ntok], sg_[:, kc * 128:(kc + 1) * 128], hT[:, kc, 0:ntok], kc == 0, kc == KC - 1)
                        for kc in range(KC)], R=[bg, HB], W=[PSB[pg]])
                cx.mmg([(bank(pu)[:, 0:ntok], su_[:, kc * 128:(kc + 1) * 128], hT[:, kc, 0:ntok], kc == 0, kc == KC - 1)
                        for kc in range(KC)], R=[bu, HB], W=[PSB[pu]])
                sgt, SGB = st["sg"][j % 2], st["SGB"][j % 2]
                cx.op("act", lambda pg=pg, sgt=sgt: nc.scalar.activation(out=sgt[:, 0:ntok], in_=bank(pg)[:, 0:ntok], func=AF.Silu),
                      R=[PSB[pg]], W=[SGB])
                cx.op("dve", lambda pu=pu, sgt=sgt, j=j: nc.vector.tensor_tensor(out=aT[:, j, 0:ntok], in0=sgt[:, 0:ntok],
                                                                                   in1=bank(pu)[:, 0:ntok], op=ALU.mult),
                      R=[SGB, PSB[pu]], W=[AB])
            proj_tok(st, aT, AB, NFF, wd, xt, XB, nt, 0.5)

        def proj_tok(st, lT, LB, J, w, xt, XB, nt, scale):
            npb = max(1, min(NB, 8 // nt))
            for n0 in range(0, NB, npb):
                nbs = list(range(n0, min(NB, n0 + npb)))
                accs = [(t, n, (t * npb + (n - n0))) for t in range(nt) for n in nbs]
                for j in range(J):
                    wcols = len(nbs) * CB
                    sl, bw = wload(st, w[j][:, n0 * CB:n0 * CB + wcols], wcols)
                    for (t, n, bk) in accs:
                        cx.mmg([(bank(bk)[:, 0:CB], lT[:, j, t * 128:(t + 1) * 128], sl[:, (n - n0) * CB:(n - n0 + 1) * CB],
                                 j == 0, j == J - 1)], R=[bw, LB], W=[PSB[bk]])
                for (t, n, bk) in accs:
                    cx.op("dve", lambda t=t, n=n, bk=bk: nc.vector.scalar_tensor_tensor(
                        out=xt[t][:, n * CB:(n + 1) * CB], in0=bank(bk)[:, 0:CB], scalar=float(scale),
                        in1=xt[t][:, n * CB:(n + 1) * CB], op0=ALU.mult, op1=ALU.add), R=[PSB[bk]], W=[XB[t]])

        def alloc13(ph, ntile):
            st = {}
            st["xt"] = [sb(ph, f"xt{t}", [128, D], F32) for t in range(ntile)]
            st["XB"] = [Buf() for _ in range(ntile)]
            st["xn"] = [sb(ph, f"xn{t}", [128, D], BF16) for t in range(ntile)]
            st["XNB"] = [Buf() for _ in range(ntile)]
            st["junk"] = sb(ph, "junk", [128, D], BF16)
            st["ss"] = sb(ph, "ss", [128, 4 * ntile], F32)
            st["SSB"] = [Buf() for _ in range(ntile)]
            st["hT"] = sb(ph, "hT", [128, KC, ntile * 128], BF16)
            st["HB"] = Buf()
            st["aT"] = sb(ph, "aT", [128, NFF, ntile * 128], BF16)
            st["AB"] = Buf()
            st["sg"] = [sb(ph, f"sg{i}", [128, ntile * 128], F32) for i in range(2)]
            st["SGB"] = [Buf(), Buf()]
            NW = 6
            st["wring"] = [sb(ph, f"wr{i}", [128, D], BF16) for i in range(NW)]
            st["WRB"] = [Buf() for _ in range(NW)]
            st["wptr"] = 0
            st["pb"] = [0, 1]
            return st

        def stop_at(k):
            if cfg.get('phases', 99) == k:
                cx.barrier()
                e_ = _Stop()
                e_.nc = nc
                raise e_

        with ExitStack() as ph:
            st = alloc13(ph, NT)
            zst = [sb(ph, f"zst{i}", [128, TB], F32) for i in range(3)]
            ZSB = [Buf() for _ in range(3)]
            xt, XB, hT, HB, aT, AB = st["xt"], st["XB"], st["hT"], st["HB"], st["aT"], st["AB"]
            if main_:
                blocks = [(b, NT) for b in range(T // TB, NBLK)]
            else:
                blocks = [(b, NT) for b in range(T // TB if pre else NBLK)] + [(-1, 1)]
            for (b, nt) in blocks:
                for t in range(nt):
                    src = x_in[b * TB + t * 128:b * TB + (t + 1) * 128, :] if b >= 0 else xh_in
                    cx.dma("sp", xt[t][:], src, W=[XB[t]])
                norm_transpose(st, xt, XB, pc["n1"], hT, HB, nt, 0)
                ffn(st, wg_in[0], wu_in[0], wd_in[0], hT, HB, aT, AB, xt, XB, nt)
                if b >= 0:
                    for t in range(nt):
                        cx.dma("sp", x1s[b * TB + t * 128:b * TB + (t + 1) * 128, :], xt[t][:], R=[XB[t]])
                norm_transpose(st, xt, XB, pc["nm"], hT, HB, nt, 1)
                ntok = nt * 128
                for j in range(NZC):
                    sw, bw = wload(st, win_in[j], KC * 128)
                    pz = 2 + (j % 4)
                    cx.mmg([(bank(pz)[:, 0:ntok], sw[:, kc * 128:(kc + 1) * 128], hT[:, kc, 0:ntok], kc == 0, kc == KC - 1)
                            for kc in range(KC)], R=[bw, HB], W=[PSB[pz]])
                    zs, zb = zst[j % 3], ZSB[j % 3]
                    if j % 2 == 0:
                        cx.op("act", lambda pz=pz, zs=zs: nc.scalar.copy(out=zs[:, 0:ntok], in_=bank(pz)[:, 0:ntok]), R=[PSB[pz]], W=[zb])
                    else:
                        cx.op("dve", lambda pz=pz, zs=zs: nc.vector.tensor_copy(out=zs[:, 0:ntok], in_=bank(pz)[:, 0:ntok]), R=[PSB[pz]], W=[zb])
                    if b >= 0:
                        seg, t0 = (b * TB) // T, (b * TB) % T
                        cx.dma("sp", zT[j * 128:(j + 1) * 128, seg, 1 + t0:1 + t0 + TB], zs[:, 0:TB], R=[zb])
                    else:
                        cx.dma("sp", zT[j * 128:(j + 1) * 128, 0, 0:1], zs[:, 0:1], R=[zb])
                        cx.dma("sp", zT[j * 128:(j + 1) * 128, 0, T + 1:T + 2], zs[:, 1:2], R=[zb])
            cx.barrier()
            stop_at(1)

        with ExitStack() as ph:
            TP = T + 2
            stg = [sb(ph, "stg0", [128, TP], F32)]
            stg.append(stg[0])
            STB = [Buf()]
            STB.append(STB[0])
            stp = [0]
            tmpA = sb(ph, "tmpA", [128, T], F32)
            tmpB = sb(ph, "tmpB", [128, T], F32)
            TAB, TBB = Buf(), Buf()
            TW = sb(ph, "TW", [128, T], BF16)
            ZA = sb(ph, "ZA", [128, T], BF16)
            SG0 = sb(ph, "SG0", [128, T], BF16)
            SG1 = sb(ph, "SG1", [32, T], BF16)
            SHB = Buf()
            w2t = sb(ph, "w2t", [128, RD], BF16)
            a2t = sb(ph, "a2t", [128, RD], BF16)
            g2a = sb(ph, "g2a", [128, RD], BF16)
            g2b = sb(ph, "g2b", [32, RD], BF16)
            w0r = sb(ph, "w0r", [1, 2 * RD], BF16)
            ones1 = sb(ph, "ones1", [1, 128], BF16)
            onesc = sb(ph, "onesc", [128, 1], BF16)
            lnw = sb(ph, "lnw", [128, NHP * 64], F32)
            lnb = sb(ph, "lnb", [128, NHP * 64], F32)
            omka = sb(ph, "omka", [128, NHP], F32)
            B_p = Buf()
            cx.dma("pool", w2t[:], w2t_in, W=[B_p])
            cx.dma("pool", a2t[:], a2t_in, W=[B_p])
            cx.dma("pool", g2a[:], g2_in[0:128, :], W=[B_p])
            cx.dma("pool", g2b[:], g2_in[128:160, :], W=[B_p])
            cx.dma("pool", w0r[:], w0r_in, W=[B_p])
            cx.dma("sp", lnw[:], lnw_in, W=[B_p])
            cx.dma("sp", lnb[:], lnb_in, W=[B_p])
            cx.op("dve", lambda: nc.vector.memset(ones1[:], 1.0), W=[B_p])
            cx.op("dve", lambda: nc.vector.memset(onesc[:], 1.0), W=[B_p])
            cx.op("dve", lambda: nc.vector.tensor_scalar(out=omka[:], in0=pcol[:, pc["ka"]:pc["ka"] + NHP], scalar1=-1.0,
                                                          scalar2=1.0, op0=ALU.mult, op1=ALU.add), R=[B_c], W=[B_p])
            Rt = sb(ph, "Rt", [128, T], F32)
            Kx = sb(ph, "Kx", [128, T], F32)
            Vt = sb(ph, "Vt", [128, T], BF16)
            KK = sb(ph, "KK", [128, T], F32)
            RKV = [Buf(), Buf(), Buf(), Buf()]
            VC = sb(ph, "VC", [128, NCH, 64], BF16)
            VCB = Buf()
            Aa = sb(ph, "Aa", [128, T], F32)
            LW = sb(ph, "LW", [128, T // 128, 128], F32)
            E1 = sb(ph, "E1", [128, T], F32)
            E2 = sb(ph, "E2", [128, T], F32)
            KD = tmpB
            AAB, LWB, E1B, E2B, KDB = Buf(), Buf(), Buf(), Buf(), TBB
            ATt = sb(ph, "ATt", [128, T], BF16)
            BTt = sb(ph, "BTt", [128, T], BF16)
            KTt = sb(ph, "KTt", [128, T], BF16)
            RTt = sb(ph, "RTt", [128, T], BF16)
            RKt = sb(ph, "RKt", [128, T], BF16)
            ATB, BTB, KTB, RTB, RKB = Buf(), Buf(), Buf(), Buf(), Buf()
            GCt = sb(ph, "GCt", [128, NCH], F32)
            GCB = Buf()
            bns = sb(ph, "bns", [128, NCH], F32)
            BNB = Buf()
            OS = sb(ph, "OS", [128, NCH, 64], F32)
            OSB = Buf()
            GR = cfg.get("GR", 4)
            NR = 2
            garr = lambda nm, w, dty: [sb(ph, f"{nm}{i}", [128, GR, w], dty) for i in range(NR)]
            X_ = [garr("X_a", 128, BF16), garr("X_b", 128, BF16)]
            Y_ = [garr("Y_a", 128, BF16), garr("Y_b", 128, BF16)]
            P_ = [garr("P_a", 128, BF16), garr("P_b", 128, BF16)]
            XB_ = [[Buf() for _ in range(NR)] for _ in range(2)]
            YB_ = [[Buf() for _ in range(NR)] for _ in range(2)]
            PB_ = [[Buf() for _ in range(NR)] for _ in range(2)]
            names = ["LAKT", "MRBT", "MRKT", "Atok", "Btok", "Ktok", "W1T"]
            SM = {n: garr(n, 128, BF16) for n in names}
            SMB = {n: [Buf() for _ in range(NR)] for n in names}
            W2p = garr("W2p", 64, BF16)
            W2 = garr("W2", 64, F32)
            W2pB = [Buf() for _ in range(NR)]
            W2B = [Buf() for _ in range(NR)]
            Ut = [sb(ph, f"Ut{i}", [128, 192], BF16) for i in range(3)]
            UB = [Buf() for _ in range(3)]
            Hf = sb(ph, "Hf", [128, 192], F32)
            Hb = sb(ph, "Hb", [128, 192], BF16)
            Htmp = sb(ph, "Htmp", [128, 192], F32)
            HFB, HBB, HTB = Buf(), Buf(), Buf()
            Hin = sb(ph, "Hin", [128, NHP * 2, 64], F32)
            HINB = Buf()
            gtoks = []
            GA = sb(ph, "GA", [128, NCORES, 192], F32)
            GAB = Buf()
            PTt = sb(ph, "PTt", [128, 128], F32)
            PTB = Buf()
            ytok = sb(ph, "ytok", [128, NCH, 64], BF16)
            YKB = Buf()
            YTs, YTB = RKt, RKB
            gn1 = sb(ph, "gn1", [128, NCH], F32)
            gn2 = sb(ph, "gn2", [128, NCH], F32)
            GNB = Buf()
            cx.op("dve", lambda: nc.vector.memset(ps[:, 0:2048], 0.0), W=[PSB[0], PSB[1], PSB[2], PSB[3]])
            bigq = lambda q: ps[:, (4 + q) * 512:(4 + q) * 512 + QW]
            bdp, cbp, gset = [0], [0], [0]

            def nbd():
                i = bdp[0]
                bdp[0] = (i + 1) % 4
                return i

            def ncb():
                i = cbp[0]
                cbp[0] = 1 - i
                return 6 + i

            def load_shift(chunk, seg, outs, func=None, rows=128, mucol=None):
                i = stp[0]
                stp[0] = 1 - i
                s, SB_ = stg[i], STB[i]
                cx.dma("sp", s[0:rows, :] if seg == 0 else s[0:rows, 1:T + 1],
                       zT[chunk * 128:chunk * 128 + rows, seg, :] if seg == 0 else zT[chunk * 128:chunk * 128 + rows, seg, 1:T + 1],
                       W=[SB_])
                if seg != 0:
                    cx.op("dve", lambda: nc.vector.memset(s[0:rows, 0:1], 0.0), W=[SB_])
                    cx.op("dve", lambda: nc.vector.memset(s[0:rows, T + 1:T + 2], 0.0), W=[SB_])
                out, OB, odt = outs
                mu = pcol[0:rows, mucol:mucol + 1]
                cx.op("dve", lambda: nc.vector.tensor_tensor(out=tmpA[0:rows, :], in0=s[0:rows, 0:T], in1=s[0:rows, 2:T + 2], op=ALU.add),
                      R=[SB_], W=[TAB])
                cx.op("dve", lambda: nc.vector.scalar_tensor_tensor(out=tmpA[0:rows, :], in0=tmpA[0:rows, :], scalar=0.5,
                                                                      in1=s[0:rows, 1:T + 1], op0=ALU.mult, op1=ALU.subtract),
                      R=[SB_, TAB], W=[TAB])
                if func is None:
                    cx.op("dve", lambda: nc.vector.scalar_tensor_tensor(out=out, in0=tmpA[0:rows, :], scalar=mu,
                                                                          in1=s[0:rows, 1:T + 1], op0=ALU.mult, op1=ALU.add),
                          R=[SB_, TAB, B_c], W=[OB])
                else:
                    cx.op("dve", lambda: nc.vector.scalar_tensor_tensor(out=tmpB[0:rows, :], in0=tmpA[0:rows, :], scalar=mu,
                                                                          in1=s[0:rows, 1:T + 1], op0=ALU.mult, op1=ALU.add),
                          R=[SB_, TAB, B_c], W=[TBB])
                    cx.op("act", lambda: nc.scalar.activation(out=out, in_=tmpB[0:rows, :], func=func), R=[TBB], W=[OB])

            def conv_seg(seg):
                for cc in range(CC):
                    sb_, SBb = E1, E1B
                    sc_, SBc = stg[0], STB[0]
                    chb, chc, chx = cc, CC + cc, 2 * CC + cc
                    cx.dma("sp", sb_[:, 0:T], zT[chb * 128:(chb + 1) * 128, seg, 1:T + 1], W=[SBb])
                    lo, hi = (0, TP) if seg == 0 else (1, T + 1)
                    cx.dma("sp", sc_[:, lo:hi], zT[chc * 128:(chc + 1) * 128, seg, lo:hi], W=[SBc])
                    cx.dma("sp", tmpB[:, :], zT[chx * 128:(chx + 1) * 128, seg, 1:T + 1], W=[TBB])
                    if seg == 0:
                        cx.dma("sp", gn1[:, 0:1], zT[chx * 128:(chx + 1) * 128, seg, 0:1], W=[GNB])
                        cx.dma("sp", gn1[:, 1:2], zT[chx * 128:(chx + 1) * 128, seg, T + 1:T + 2], W=[GNB])
                    else:
                        cx.op("dve", lambda: nc.vector.memset(gn1[:, 0:2], 0.0), W=[GNB])
                        cx.op("dve", lambda: nc.vector.memset(sc_[:, 0:1], 0.0), W=[SBc])
                        cx.op("dve", lambda: nc.vector.memset(sc_[:, T + 1:T + 2], 0.0), W=[SBc])
                    cx.op("dve", lambda: nc.vector.tensor_tensor(out=sc_[:, 1:T + 1], in0=sc_[:, 1:T + 1], in1=tmpB[:, :], op=ALU.mult),
                          R=[TBB], W=[SBc])
                    cx.op("dve", lambda: nc.vector.tensor_tensor(out=sc_[:, 0:1], in0=sc_[:, 0:1], in1=gn1[:, 0:1], op=ALU.mult),
                          R=[GNB], W=[SBc])
                    cx.op("dve", lambda: nc.vector.tensor_tensor(out=sc_[:, T + 1:T + 2], in0=sc_[:, T + 1:T + 2], in1=gn1[:, 1:2], op=ALU.mult),
                          R=[GNB], W=[SBc])
                    cw = lambda k: pcol[:, pc["cw"] + k * CC + cc:pc["cw"] + k * CC + cc + 1]
                    cx.op("dve", lambda: nc.vector.tensor_scalar(out=tmpA[:, :], in0=sc_[:, 0:T], scalar1=cw(0), scalar2=None, op0=ALU.mult),
                          R=[SBc, B_c], W=[TAB])
                    cx.op("dve", lambda: nc.vector.scalar_tensor_tensor(out=tmpA[:, :], in0=sc_[:, 1:T + 1], scalar=cw(1), in1=tmpA[:, :],
                                                                          op0=ALU.mult, op1=ALU.add), R=[SBc, B_c, TAB], W=[TAB])
                    cx.op("dve", lambda: nc.vector.scalar_tensor_tensor(out=tmpA[:, :], in0=sc_[:, 2:T + 2], scalar=cw(2), in1=tmpA[:, :],
                                                                          op0=ALU.mult, op1=ALU.add), R=[SBc, B_c, TAB], W=[TAB])
                    cx.op("dve", lambda: nc.vector.tensor_tensor(out=YTs[:, :], in0=tmpA[:, :], in1=sb_[:, 0:T], op=ALU.mult),
                          R=[TAB, SBb], W=[YTB])
                    cx.dma("sp", yT[cc * 128:(cc + 1) * 128, seg * T:(seg + 1) * T], YTs[:, :], R=[YTB])

            def seg_shared(seg):
                base = 3 * CC + 3 * NHP
                mub = pc["mu"]
                load_shift(base, seg, (TW[:, :], SHB, BF16), func=AF.Tanh, mucol=mub + 3 * NHP)
                load_shift(base + 1, seg, (ZA[:, :], SHB, BF16), func=None, mucol=mub + 3 * NHP + 1)
                load_shift(base + 2, seg, (SG0[:, :], SHB, BF16), func=AF.Sigmoid, mucol=mub + 3 * NHP + 2)
                load_shift(base + 3, seg, (SG1[:, :], SHB, BF16), func=AF.Sigmoid, rows=32, mucol=mub + 3 * NHP + 3)

            def hp_prep(seg, hp):
                mub = pc["mu"]
                load_shift(3 * CC + hp, seg, (Rt[:, :], RKV[0], F32), mucol=mub + hp)
                load_shift(3 * CC + NHP + hp, seg, (Kx[:, :], RKV[1], F32), mucol=mub + NHP + hp)
                load_shift(3 * CC + 2 * NHP + hp, seg, (Vt[:, :], RKV[2], BF16), mucol=mub + 2 * NHP + hp)
                cx.op("dve", lambda: nc.vector.tensor_scalar(out=KK[:, :], in0=Kx[:, :], scalar1=pcol[:, pc["kk"] + hp:pc["kk"] + hp + 1],
                                                              scalar2=None, op0=ALU.mult), R=[RKV[1], B_c], W=[RKV[3]])
                cx.op("act", lambda: nc.scalar.activation(out=RKt[:, :], in_=KK[:, :], func=AF.Square), R=[RKV[3]], W=[RKB])
                for q in range(NQ):
                    cx.mmg([(bigq(q), cb("bones"), RKt[:, q * QW:(q + 1) * QW], True, True)], R=[RKB, B_c], W=[PSB[4 + q]])
                    cx.op("dve", lambda q=q: nc.vector.tensor_scalar(out=tmpA[:, q * QW:(q + 1) * QW], in0=bigq(q), scalar1=1e-24,
                                                                      scalar2=None, op0=ALU.max), R=[PSB[4 + q]], W=[TAB])
                cx.op("act", lambda: nc.scalar.activation(out=tmpA[:, :], in_=tmpA[:, :], func=AF.Sqrt), R=[TAB], W=[TAB])
                cx.op("dve", lambda: nc.vector.reciprocal(out=tmpA[:, :], in_=tmpA[:, :]), R=[TAB], W=[TAB])
                cx.op("dve", lambda: nc.vector.tensor_tensor(out=KK[:, :], in0=KK[:, :], in1=tmpA[:, :], op=ALU.mult), R=[TAB, RKV[3]], W=[RKV[3]])
                for c0 in range(NCH):
                    q, off = (c0 * 64) // QW, (c0 * 64) % QW
                    items = []
                    for h in range(2):
                        items.append((ps[h * 64:(h + 1) * 64, (4 + q) * 512 + off:(4 + q) * 512 + off + 64],
                                      Vt[h * 64:(h + 1) * 64, c0 * 64:(c0 + 1) * 64],
                                      cstb[h * 64:(h + 1) * 64, CI["ident"] * 128 + h * 64:CI["ident"] * 128 + (h + 1) * 64], True, True))
                    cx.mmg(items, R=[RKV[2], B_c], W=[PSB[4 + q]])
                for q in range(NQ):
                    cx.op("act", lambda q=q: nc.scalar.copy(out=VC[:, q * (QW // 64):(q + 1) * (QW // 64), :],
                                                            in_=bigq(q).rearrange("p (c v) -> p c v", v=64)), R=[PSB[4 + q]], W=[VCB])

            def dir_prep(seg, hp, d):
                dn = "f" if d == 0 else "b"
                hs = slice(d * 64, (d + 1) * 64)
                cs = slice(hp * 128, (hp + 1) * 128)
                for q in range(NQ):
                    cx.mmg([(bigq(q), a2t[hs, cs], ZA[hs, q * QW:(q + 1) * QW], True, True)], R=[SHB, B_p], W=[PSB[4 + q]])
                    cx.op("act", lambda q=q: nc.scalar.activation(out=Aa[:, q * QW:(q + 1) * QW], in_=bigq(q), func=AF.Sigmoid,
                                                                  bias=pcol[:, pc["a0"] + d * NHP + hp:pc["a0"] + d * NHP + hp + 1]),
                          R=[PSB[4 + q], B_c], W=[AAB])
                for tt in range(T // 128):
                    q, off = (tt * 128) // QW, (tt * 128) % QW
                    o = ps[:, (4 + q) * 512 + off:(4 + q) * 512 + off + 128]
                    cx.mmg([(o, TW[hs, tt * 128:(tt + 1) * 128], w2t[hs, cs], True, False),
                            (o, ones1[0:1, 0:128], w0r[0:1, d * RD + hp * 128:d * RD + (hp + 1) * 128], False, True)],
                           R=[SHB, B_p], W=[PSB[4 + q]])
                for q in range(NQ):
                    cx.op("act", lambda q=q: nc.scalar.activation(out=LW[:, q * (QW // 128):(q + 1) * (QW // 128), :],
                                                                  in_=bigq(q).rearrange("p (t c) -> p t c", c=128), func=AF.Sigmoid),
                          R=[PSB[4 + q]], W=[LWB])
                for (trn, which) in (("tri" + dn, 0), ("trix" + dn, 1)):
                    for tt in range(T // 128):
                        q, off = (tt * 128) // QW, (tt * 128) % QW
                        o = ps[:, (4 + q) * 512 + off:(4 + q) * 512 + off + 128]
                        cx.mmg([(o, LW[:, tt, :], cf(trn), True, True)], R=[LWB, B_c], W=[PSB[4 + q]])
                    if which == 0:
                        for q in range(NQ):
                            cx.op("act", lambda q=q: nc.scalar.activation(out=E1[:, q * QW:(q + 1) * QW], in_=bigq(q), func=AF.Exp),
                                  R=[PSB[4 + q]], W=[E1B])
                            cx.op("act", lambda q=q: nc.scalar.activation(out=E2[:, q * QW:(q + 1) * QW], in_=bigq(q), func=AF.Exp, scale=-1.0),
                                  R=[PSB[4 + q]], W=[E2B])
                        cx.op("dve", lambda: nc.vector.tensor_tensor(out=RTt[:, :], in0=Rt[:, :], in1=E1[:, :], op=ALU.mult),
                              R=[RKV[0], E1B], W=[RTB])
                        gcol = 63 if d == 0 else 0
                        cx.op("dve", lambda: nc.vector.tensor_copy(out=GCt[:, :], in_=APx(E1[:, :], [[64, NCH]], off=gcol)),
                              R=[E1B], W=[GCB])
                        cx.op("dve", lambda: nc.vector.tensor_tensor(out=tmpA[:, :], in0=KK[:, :], in1=Aa[:, :], op=ALU.mult),
                              R=[RKV[3], AAB], W=[TAB])
                        cx.op("dve", lambda: nc.vector.tensor_tensor(out=BTt[:, :], in0=tmpA[:, :], in1=E2[:, :], op=ALU.mult),
                              R=[TAB, E2B], W=[BTB])
                        cx.op("dve", lambda: nc.vector.tensor_scalar(out=tmpA[:, :], in0=Aa[:, :],
                                                                      scalar1=pcol[:, pc["ka"] + hp:pc["ka"] + hp + 1],
                                                                      scalar2=omka[:, hp:hp + 1], op0=ALU.mult, op1=ALU.add),
                              R=[AAB, B_c, B_p], W=[TAB])
                        cx.op("dve", lambda: nc.vector.tensor_tensor(out=KD[:, :], in0=Kx[:, :], in1=tmpA[:, :], op=ALU.mult),
                              R=[TAB, RKV[1]], W=[KDB])
                        cx.op("dve", lambda: nc.vector.tensor_tensor(out=KTt[:, :], in0=KD[:, :], in1=E2[:, :], op=ALU.mult),
                              R=[KDB, E2B], W=[KTB])
                        cx.op("dve", lambda: nc.vector.scalar_tensor_tensor(out=RKt[:, :], in0=Rt[:, :],
                                                                              scalar=pcol[:, pc["rk"] + hp:pc["rk"] + hp + 1],
                                                                              in1=KD[:, :], op0=ALU.mult, op1=ALU.mult),
                              R=[RKV[0], KDB, B_c], W=[RKB])
                    else:
                        for q in range(NQ):
                            cx.op("act", lambda q=q: nc.scalar.activation(out=E1[:, q * QW:(q + 1) * QW], in_=bigq(q), func=AF.Exp),
                                  R=[PSB[4 + q]], W=[E1B])
                        cx.op("dve", lambda: nc.vector.scalar_tensor_tensor(out=ATt[:, :], in0=KK[:, :], scalar=-1.0, in1=E1[:, :],
                                                                              op0=ALU.mult, op1=ALU.mult), R=[RKV[3], E1B], W=[ATB])
                q0 = 0
                for c0 in range(NCH):
                    items = [(ps[h * 64:(h + 1) * 64, 4 * 512 + c0:4 * 512 + c0 + 1], RKt[h * 64:(h + 1) * 64, c0 * 64:(c0 + 1) * 64],
                              onesc[h * 64:(h + 1) * 64, 0:1], True, True) for h in range(2)]
                    cx.mmg(items, R=[RKB, B_p], W=[PSB[4]])
                if d == 0:
                    cx.op("dve", lambda: nc.vector.tensor_copy(out=bns[:, :], in_=ps[:, 4 * 512:4 * 512 + NCH]), R=[PSB[4]], W=[BNB])
                else:
                    cx.op("dve", lambda: nc.vector.tensor_tensor(out=bns[:, :], in0=bns[:, :], in1=ps[:, 4 * 512:4 * 512 + NCH], op=ALU.add),
                          R=[PSB[4]], W=[BNB])

            def blk(ap, h):
                return ap[h * 64:(h + 1) * 64, h * 64:(h + 1) * 64]

            def chunk_loop(hp, d, aug, outmode):
                dn = "f" if d == 0 else "b"
                NS = 192 if aug else 64
                order = list(range(NCH)) if d == 0 else list(range(NCH - 1, -1, -1))
                groups = [order[g0:g0 + GR] for g0 in range(0, NCH, GR)]
                gsets = []
                for _ in groups:
                    gsets.append(gset[0])
                    gset[0] = 1 - gset[0]

                def pre_gen(grp, st_):
                    n_ = len(grp)
                    csl = [slice(c0 * 64, (c0 + 1) * 64) for c0 in grp]
                    bankv = lambda b_, w: ps[:, b_ * 512:b_ * 512 + n_ * w].rearrange("p (g c) -> p g c", c=w)
                    bcast = lambda ap_: APx(ap_, [[0, n_], [1, 128]])

                    def bd_mm(lt, LB_, rt, RB_, ident_rhs=False):
                        b_ = nbd()
                        items = []
                        for gi in range(n_):
                            for h in range(2):
                                o = ps[h * 64:(h + 1) * 64, b_ * 512 + gi * 128 + h * 64:b_ * 512 + gi * 128 + (h + 1) * 64]
                                rr = blk(identb, h) if ident_rhs else rt[h * 64:(h + 1) * 64, csl[gi]]
                                items.append((o, lt[h * 64:(h + 1) * 64, csl[gi]], rr, True, True))
                        cx.mmg(items, R=[LB_, RB_] if not ident_rhs else [LB_, B_c], W=[PSB[b_]])
                        return b_

                    def masked(b_, mask, outt, OB_):
                        cx.op("dve", lambda: nc.vector.tensor_tensor(out=outt[:, 0:n_, :], in0=bankv(b_, 128), in1=bcast(cf(mask)), op=ALU.mult),
                              R=[PSB[b_], B_c], W=[OB_])
                    X0, Y0, P0 = X_[0][st_], Y_[0][st_], P_[0][st_]
                    masked(bd_mm(ATt, ATB, BTt, BTB), "ms" + dn, X0, XB_[0][st_])
                    yield
                    masked(bd_mm(BTt, BTB, ATt, ATB), "mst" + dn, Y0, YB_[0][st_])
                    yield
                    cx.op("dve", lambda: nc.vector.tensor_tensor(out=P0[:, 0:n_, :], in0=Y0[:, 0:n_, :], in1=bcast(cb("ident")), op=ALU.add),
                          R=[YB_[0][st_], B_c], W=[PB_[0][st_]])
                    yield
                    masked(bd_mm(KTt, KTB, ATt, ATB), "mst" + dn, SM["LAKT"][st_], SMB["LAKT"][st_])
                    yield
                    if not aug:
                        masked(bd_mm(BTt, BTB, RTt, RTB), "mit" + dn, SM["MRBT"][st_], SMB["MRBT"][st_])
                        yield
                        masked(bd_mm(KTt, KTB, RTt, RTB), "mit" + dn, SM["MRKT"][st_], SMB["MRKT"][st_])
                        yield
                    for (nm, src, SB_) in (("Atok", ATt, ATB), ("Btok", BTt, BTB), ("Ktok", KTt, KTB)):
                        b_ = bd_mm(src, SB_, None, None, ident_rhs=True)
                        cx.op("act", lambda nm=nm, b_=b_: nc.scalar.copy(out=SM[nm][st_][:, 0:n_, :], in_=bankv(b_, 128)),
                              R=[PSB[b_]], W=[SMB[nm][st_]])
                        yield
                    cur = 0
                    for lev in range(5):
                        nxt = 1 - cur
                        last = lev == 4
                        Xc, Yc, Pc = X_[cur][st_], Y_[cur][st_], P_[cur][st_]
                        Xn, Yn, Pn = X_[nxt][st_], Y_[nxt][st_], P_[nxt][st_]
                        b_ = nbd()
                        cx.mmg([(ps[:, b_ * 512 + gi * 128:b_ * 512 + (gi + 1) * 128], Yc[:, gi, :], Xc[:, gi, :], True, True) for gi in range(n_)],
                               R=[YB_[cur][st_], XB_[cur][st_]], W=[PSB[b_]])
                        cx.op("act", lambda b_=b_, Xn=Xn: nc.scalar.copy(out=Xn[:, 0:n_, :], in_=bankv(b_, 128)), R=[PSB[b_]], W=[XB_[nxt][st_]])
                        yield
                        if not last:
                            b_ = nbd()
                            cx.mmg([(ps[:, b_ * 512 + gi * 128:b_ * 512 + (gi + 1) * 128], Xc[:, gi, :], Yc[:, gi, :], True, True) for gi in range(n_)],
                                   R=[YB_[cur][st_], XB_[cur][st_]], W=[PSB[b_]])
                            cx.op("dve", lambda b_=b_, Yn=Yn: nc.vector.tensor_copy(out=Yn[:, 0:n_, :], in_=bankv(b_, 128)), R=[PSB[b_]], W=[YB_[nxt][st_]])
                            yield
                        b_ = nbd()
                        cx.mmg([(ps[:, b_ * 512 + gi * 128:b_ * 512 + (gi + 1) * 128], Xn[:, gi, :], Pc[:, gi, :], True, True) for gi in range(n_)],
                               R=[XB_[nxt][st_], PB_[cur][st_]], W=[PSB[b_]])
                        cx.op("dve", lambda b_=b_, Pn=Pn, Pc=Pc: nc.vector.tensor_tensor(out=Pn[:, 0:n_, :], in0=bankv(b_, 128), in1=Pc[:, 0:n_, :], op=ALU.add),
                              R=[PSB[b_], PB_[cur][st_]], W=[PB_[nxt][st_]])
                        yield
                        cur = nxt
                    PF, PFB = P_[cur][st_], PB_[cur][st_]
                    LAKT, Atok, Btok, Ktok, W1T = SM["LAKT"][st_], SM["Atok"][st_], SM["Btok"][st_], SM["Ktok"][st_], SM["W1T"][st_]
                    b_ = ncb()
                    cx.mmg([(ps[:, b_ * 512 + gi * 64:b_ * 512 + (gi + 1) * 64], LAKT[:, gi, :], VC[:, grp[gi], :], True, True) for gi in range(n_)],
                           R=[SMB["LAKT"][st_], VCB], W=[PSB[b_]])
                    cx.op("act", lambda b_=b_: nc.scalar.copy(out=W2p[st_][:, 0:n_, :], in_=bankv(b_, 64)), R=[PSB[b_]], W=[W2pB[st_]])
                    yield
                    b_ = nbd()
                    cx.mmg([(ps[:, b_ * 512 + gi * 128:b_ * 512 + (gi + 1) * 128], Atok[:, gi, :], PF[:, gi, :], True, True) for gi in range(n_)],
                           R=[SMB["Atok"][st_], PFB], W=[PSB[b_]])
                    cx.op("dve", lambda b_=b_: nc.vector.tensor_copy(out=W1T[:, 0:n_, :], in_=bankv(b_, 128)), R=[PSB[b_]], W=[SMB["W1T"][st_]])
                    yield
                    b_ = ncb()
                    cx.mmg([(ps[:, b_ * 512 + gi * 64:b_ * 512 + (gi + 1) * 64], PF[:, gi, :], W2p[st_][:, gi, :], True, True) for gi in range(n_)],
                           R=[PFB, W2pB[st_]], W=[PSB[b_]])
                    cx.op("act", lambda b_=b_: nc.scalar.copy(out=W2[st_][:, 0:n_, :], in_=bankv(b_, 64)), R=[PSB[b_]], W=[W2B[st_]])
                    yield

                def chain_gen(grp, st_):
                    n_ = len(grp)
                    csl = [slice(c0 * 64, (c0 + 1) * 64) for c0 in grp]
                    LAKT, Atok, Btok, Ktok, W1T = SM["LAKT"][st_], SM["Atok"][st_], SM["Btok"][st_], SM["Ktok"][st_], SM["W1T"][st_]
                    for gi, c0 in enumerate(grp):
                        ui = c0 % 3
                        U, UB_ = Ut[ui], UB[ui]
                        qa = ps[:, 4 * 512:4 * 512 + NS]
                        cx.mmg([(qa, W1T[:, gi, :], Hb[:, 0:NS], True, True)], R=[SMB["W1T"][st_], HBB], W=[PSB[4]])
                        cx.op("dve", lambda qa=qa, U=U, gi=gi: nc.vector.tensor_tensor(out=U[:, 0:64], in0=qa[:, 0:64], in1=W2[st_][:, gi, :], op=ALU.add),
                              R=[PSB[4], W2B[st_]], W=[UB_])
                        yield
                        if aug:
                            cx.op("act", lambda qa=qa, U=U: nc.scalar.copy(out=U[:, 64:192], in_=qa[:, 64:192]), R=[PSB[4]], W=[UB_])
                            yield
                        else:
                            ob = ncb()
                            oa = ps[:, ob * 512:ob * 512 + 64]
                            items = [(oa[h * 64:(h + 1) * 64, :], RTt[h * 64:(h + 1) * 64, csl[gi]], Hb[h * 64:(h + 1) * 64, 0:64], True, False)
                                     for h in range(2)]
                            items += [(oa, SM["MRKT"][st_][:, gi, :], VC[:, c0, :], False, False),
                                      (oa, SM["MRBT"][st_][:, gi, :], U[:, 0:64], False, True)]
                            cx.mmg(items, R=[SMB["MRKT"][st_], SMB["MRBT"][st_], VCB, UB_, RTB, HBB], W=[PSB[ob]])
                            if outmode == 0:
                                cx.op("act", lambda oa=oa, c0=c0: nc.scalar.copy(out=OS[:, c0, :], in_=oa), R=[PSB[ob]], W=[OSB])
                                yield
                            else:
                                cx.op("dve", lambda oa=oa, c0=c0: nc.vector.tensor_tensor(out=OS[:, c0, :], in0=oa, in1=OS[:, c0, :], op=ALU.add),
                                      R=[PSB[ob], OSB], W=[OSB])
                                yield
                        ha = ps[:, 5 * 512:5 * 512 + NS]
                        cx.mmg([(ha, Btok[:, gi, :], U[:, 0:NS], True, False),
                                (ha[:, 0:64], Ktok[:, gi, :], VC[:, c0, :], False, True)],
                               R=[SMB["Btok"][st_], SMB["Ktok"][st_], UB_, VCB], W=[PSB[5]])
                        cx.op("dve", lambda ha=ha: nc.vector.tensor_tensor(out=Htmp[:, 0:NS], in0=ha, in1=Hf[:, 0:NS], op=ALU.add),
                              R=[PSB[5], HFB], W=[HTB])
                        cx.op("dve", lambda c0=c0: nc.vector.tensor_scalar(out=Hf[:, 0:NS], in0=Htmp[:, 0:NS], scalar1=GCt[:, c0:c0 + 1],
                                                                             scalar2=None, op0=ALU.mult), R=[HTB, GCB], W=[HFB])
                        cx.op("act", lambda c0=c0: nc.scalar.activation(out=Hb[:, 0:NS], in_=Htmp[:, 0:NS], func=AF.Copy,
                                                                         scale=GCt[:, c0:c0 + 1]), R=[HTB, GCB], W=[HBB])
                        yield


                for _ in pre_gen(groups[0], gsets[0]):
                    pass
                for gix, grp in enumerate(groups):
                    cg = chain_gen(grp, gsets[gix])
                    pg = pre_gen(groups[gix + 1], gsets[gix + 1]) if gix + 1 < len(groups) else iter(())
                    alive_c = alive_p = True
                    while alive_c or alive_p:
                        if alive_c:
                            try:
                                next(cg)
                            except StopIteration:
                                alive_c = False
                        for _ in range(cfg.get("PRATIO", 2)):
                            if alive_p:
                                try:
                                    next(pg)
                                except StopIteration:
                                    alive_p = False

            def init_state(hp, d, mode):
                if mode == "zero":
                    cx.op("dve", lambda: nc.vector.memset(Hf[:, :], 0.0), W=[HFB])
                    cx.op("dve", lambda: nc.vector.memset(Hb[:, :], 0.0), W=[HBB])
                elif mode == "aug":
                    cx.op("dve", lambda: nc.vector.memset(Hf[:, 0:64], 0.0), W=[HFB])
                    cx.op("dve", lambda: nc.vector.tensor_copy(out=Hf[:, 64:192], in_=cf("ident")), R=[B_c], W=[HFB])
                    cx.op("dve", lambda: nc.vector.memset(Hb[:, 0:64], 0.0), W=[HBB])
                    cx.op("dve", lambda: nc.vector.tensor_copy(out=Hb[:, 64:192], in_=cf("ident")), R=[B_c], W=[HBB])
                else:
                    i = hp * 2 + d
                    cx.op("dve", lambda: nc.vector.tensor_copy(out=Hf[:, 0:64], in_=Hin[:, i, :]), R=[HINB], W=[HFB])
                    cx.op("dve", lambda: nc.vector.tensor_copy(out=Hb[:, 0:64], in_=Hin[:, i, :]), R=[HINB], W=[HBB])

            def post(seg, hp):
                NC_ = NCH
                bc = lambda t_: APx(t_[:, :], [[1, NC_], [0, 64]])
                lw = APx(lnw[:, hp * 64:(hp + 1) * 64], [[0, NC_], [1, 64]])
                lb = APx(lnb[:, hp * 64:(hp + 1) * 64], [[0, NC_], [1, 64]])
                OSv = OS[:, :, :]
                cx.op("dve", lambda: nc.vector.tensor_reduce(out=gn1[:, :], in_=OSv, axis=AX.X, op=ALU.add), R=[OSB], W=[GNB])
                cx.op("dve", lambda: nc.vector.tensor_scalar(out=gn1[:, :], in0=gn1[:, :], scalar1=1.0 / 64, scalar2=None, op0=ALU.mult),
                      R=[GNB], W=[GNB])
                cx.op("dve", lambda: nc.vector.tensor_tensor(out=OSv, in0=OSv, in1=bc(gn1), op=ALU.subtract), R=[GNB, OSB], W=[OSB])
                tA = tmpA[:, 0:NC_ * 64].rearrange("p (c v) -> p c v", v=64)
                cx.op("act", lambda: nc.scalar.activation(out=tA, in_=OSv, func=AF.Square), R=[OSB], W=[TAB])
                cx.op("dve", lambda: nc.vector.tensor_reduce(out=gn2[:, :], in_=tA, axis=AX.X, op=ALU.add), R=[TAB], W=[GNB])
                cx.op("dve", lambda: nc.vector.tensor_scalar(out=gn2[:, :], in0=gn2[:, :], scalar1=1.0 / 64, scalar2=GN_EPS,
                                                              op0=ALU.mult, op1=ALU.add), R=[GNB], W=[GNB])
                cx.op("act", lambda: nc.scalar.activation(out=gn2[:, :], in_=gn2[:, :], func=AF.Sqrt), R=[GNB], W=[GNB])
                cx.op("dve", lambda: nc.vector.reciprocal(out=gn2[:, :], in_=gn2[:, :]), R=[GNB], W=[GNB])
                cx.op("dve", lambda: nc.vector.tensor_tensor(out=OSv, in0=OSv, in1=bc(gn2), op=ALU.mult), R=[GNB, OSB], W=[OSB])
                cx.op("dve", lambda: nc.vector.tensor_tensor(out=OSv, in0=OSv, in1=lw, op=ALU.mult), R=[B_p, OSB], W=[OSB])
                cx.op("dve", lambda: nc.vector.tensor_tensor(out=OSv, in0=OSv, in1=lb, op=ALU.add), R=[B_p, OSB], W=[OSB])
                cx.op("dve", lambda: nc.vector.tensor_tensor(out=tA, in0=VC[:, :, :], in1=bc(bns), op=ALU.mult), R=[VCB, BNB], W=[TAB])
                cx.op("dve", lambda: nc.vector.tensor_tensor(out=OSv, in0=OSv, in1=tA, op=ALU.add), R=[TAB, OSB], W=[OSB])
                for c0 in range(NC_):
                    q, off = (c0 * 64) // QW, (c0 * 64) % QW
                    items = []
                    for h in range(2):
                        o = ps[h * 64:(h + 1) * 64, (4 + q) * 512 + off:(4 + q) * 512 + off + 64]
                        gc_ = slice(hp * 128 + h * 64, hp * 128 + (h + 1) * 64)
                        items.append((o, SG0[:, c0 * 64:(c0 + 1) * 64], g2a[:, gc_], True, False))
                        items.append((o, SG1[:, c0 * 64:(c0 + 1) * 64], g2b[:, gc_], False, True))
                    cx.mmg(items, R=[SHB, B_p], W=[PSB[4 + q]])
                for q in range(NQ):
                    cw_ = QW // 64
                    cx.op("dve", lambda q=q: nc.vector.tensor_tensor(out=ytok[:, q * cw_:(q + 1) * cw_, :], in0=OS[:, q * cw_:(q + 1) * cw_, :],
                                                                      in1=bigq(q).rearrange("p (c v) -> p c v", v=64), op=ALU.mult),
                          R=[OSB, PSB[4 + q]], W=[YKB])
                for c0 in range(NC_):
                    q, off = (c0 * 64) // QW, (c0 * 64) % QW
                    items = []
                    for h in range(2):
                        o = ps[h * 64:(h + 1) * 64, (4 + q) * 512 + off:(4 + q) * 512 + off + 64]
                        items.append((o, ytok[h * 64:(h + 1) * 64, c0, :], blk(identb, h), True, True))
                    cx.mmg(items, R=[YKB, B_c], W=[PSB[4 + q]])
                for q in range(NQ):
                    cx.op("act", lambda q=q: nc.scalar.copy(out=YTs[:, q * QW:(q + 1) * QW], in_=bigq(q)), R=[PSB[4 + q]], W=[YTB])
                cx.dma("sp", yT[CD + hp * 128:CD + (hp + 1) * 128, seg * T:(seg + 1) * T], YTs[:, :], R=[YTB])

            def run_seg(seg, mode):
                seg_shared(seg)
                stop_at(2.01)
                if mode != "aug":
                    conv_seg(seg)
                for hp in range(NHP):
                    hp_prep(seg, hp)
                    stop_at(2.02)
                    for d in range(2):
                        dir_prep(seg, hp, d)
                        stop_at(2.03)
                        init_state(hp, d, mode)
                        chunk_loop(hp, d, mode == "aug", d)
                        if mode == "aug":
                            i_ = hp * 2 + d
                            gtoks.append(cx.dma("sp", gin[:, i_ * 192:(i_ + 1) * 192], Hf[:, :], R=[HFB]))
                    if mode != "aug":
                        post(seg, hp)

            stop_at(1.5)
            B_g = Buf()
            if not main_:
                run_seg(0, "aug")
                stop_at(2.1)
                if pre:
                    cx.barrier()
                    e_ = _Stop()
                    e_.nc = nc
                    raise e_
                for tg in gtoks:
                    cx.E["pool"].wait(tg)
                cc_tok = cx.E["pool"].stamp(nc.gpsimd.collective_compute("AllGather", ALU.bypass, replica_groups=[list(range(NCORES))],
                                                                         ins=[gin], outs=[gout]))
                B_g.w = cc_tok
                stop_at(2.2)
            for seg in range(1, NSEG):
                run_seg(seg, "zero")
            stop_at(2.3)
            gv = gout.rearrange("(k p) (i c) -> p k i c", p=128, c=192)
            for hp in range(NHP):
                for d in range(2):
                    i = hp * 2 + d
                    cx.dma("sp", GA[:, :, :], gv[:, :, i, :], R=[B_g], W=[GAB])
                    cx.op("dve", lambda: nc.vector.memset(Hf[:, 0:64], 0.0), W=[HFB])
                    ks = list(range(NCORES)) if d == 0 else list(range(NCORES - 1, -1, -1))
                    for k in ks:
                        g = ncb()
                        cx.mmg([(ps[:, g * 512:g * 512 + 128], GA[:, k, 64:192], cf("ident"), True, True)], R=[GAB, B_c], W=[PSB[g]])
                        cx.op("act", lambda g=g: nc.scalar.copy(out=PTt[:, :], in_=ps[:, g * 512:g * 512 + 128]), R=[PSB[g]], W=[PTB])
                        g = ncb()
                        cx.mmg([(ps[:, g * 512:g * 512 + 64], PTt[:, :], Hf[:, 0:64], True, True)], R=[PTB, HFB], W=[PSB[g]])
                        cx.op("dve", lambda g=g, k=k: nc.vector.tensor_tensor(out=Htmp[:, 0:64], in0=ps[:, g * 512:g * 512 + 64], in1=GA[:, k, 0:64], op=ALU.add),
                              R=[PSB[g], GAB], W=[HTB])
                        cx.op("dve", lambda: nc.vector.tensor_tensor(out=Htmp[:, 0:64], in0=Htmp[:, 0:64], in1=Hf[:, 0:64], op=ALU.subtract),
                              R=[HTB, HFB], W=[HTB])
                        mc = pcol[:, pc["mk"] + d * 8 + k:pc["mk"] + d * 8 + k + 1]
                        cx.op("dve", lambda mc=mc: nc.vector.scalar_tensor_tensor(out=Hf[:, 0:64], in0=Htmp[:, 0:64], scalar=mc, in1=Hf[:, 0:64],
                                                                                    op0=ALU.mult, op1=ALU.add), R=[HTB, HFB, B_c], W=[HFB])
                    cx.op("dve", lambda i=i: nc.vector.tensor_copy(out=Hin[:, i, :], in_=Hf[:, 0:64]), R=[HFB], W=[HINB])
            stop_at(2.4)
            run_seg(0, "hin")
            cx.barrier()
            stop_at(2.5)

        with ExitStack() as ph:
            st = alloc13(ph, NT)
            yTt = sb(ph, "yTt", [128, KC, TB], BF16)
            YB3 = Buf()
            fng = sb(ph, "fng", [128, D], F32)
            FGB = Buf()
            cx.dma("sp", fng[:], bass.AP(fng_in.tensor, 0, [[0, 128], [1, D]]), W=[FGB])
            xt, XB, hT, HB, aT, AB = st["xt"], st["XB"], st["hT"], st["HB"], st["aT"], st["AB"]
            otoks = []
            for b in range(NBLK):
                for t in range(NT):
                    cx.dma("sp", xt[t][:], x1s[b * TB + t * 128:b * TB + (t + 1) * 128, :], W=[XB[t]])
                for kc in range(KC):
                    cx.dma("sp", yTt[:, kc, :], yT[kc * 128:(kc + 1) * 128, b * TB:(b + 1) * TB], W=[YB3])
                proj_tok(st, yTt, YB3, KC, wo_in, xt, XB, NT, 1.0)
                norm_transpose(st, xt, XB, pc["n2"], hT, HB, NT, 2)
                ffn(st, wg_in[1], wu_in[1], wd_in[1], hT, HB, aT, AB, xt, XB, NT)
                ss, junk, SSB = st["ss"], st["junk"], st["SSB"]
                for t in range(NT):
                    cx.op("act", lambda t=t: nc.scalar.activation(out=junk[:], in_=xt[t][:], func=AF.Square, accum_out=ss[:, 4 * t:4 * t + 1]),
                          R=[XB[t]], W=[SSB[t]])
                    cx.op("dve", lambda t=t: nc.vector.tensor_scalar(out=ss[:, 4 * t + 1:4 * t + 2], in0=ss[:, 4 * t:4 * t + 1], scalar1=1.0 / D,
                                                                     scalar2=RMS_EPS, op0=ALU.mult, op1=ALU.add), R=[SSB[t]], W=[SSB[t]])
                    cx.op("act", lambda t=t: nc.scalar.activation(out=ss[:, 4 * t + 2:4 * t + 3], in_=ss[:, 4 * t + 1:4 * t + 2], func=AF.Sqrt),
                          R=[SSB[t]], W=[SSB[t]])
                    cx.op("dve", lambda t=t: nc.vector.reciprocal(out=ss[:, 4 * t + 3:4 * t + 4], in_=ss[:, 4 * t + 2:4 * t + 3]),
                          R=[SSB[t]], W=[SSB[t]])
                    cx.op("dve", lambda t=t: nc.vector.scalar_tensor_tensor(out=xt[t][:], in0=xt[t][:], scalar=ss[:, 4 * t + 3:4 * t + 4],
                                                                              in1=fng[:], op0=ALU.mult, op1=ALU.mult),
                          R=[SSB[t], FGB], W=[XB[t]])
                    otoks.append(cx.dma("sp", out_d[b * TB + t * 128:b * TB + (t + 1) * 128, :], xt[t][:], R=[XB[t]]))
            for tk in otoks[-24:]:
                cx.E["sp"].wait(tk)
            cx.barrier()
    return nc


def host_prep(cfg, inputs, xs, xhs, core):
    c = derive(cfg)
    D, KC, NFF, CD, RD, CC, NHP, NZC, pc, NPC = c["D"], c["KC"], c["NFF"], c["CD"], c["RD"], c["CC"], c["NHP"], c["NZC"], c["pc"], c["NPC"]
    f = np.float32
    m = {"x": np.ascontiguousarray(xs, f), "xh": np.ascontiguousarray(xhs, f)}
    return m


def shared_prep(cfg, I):
    c = derive(cfg)
    D, DFF, KC, NFF, CD, RD, CC, NHP, NZC, pc, NPC = (c["D"], c["DFF"], c["KC"], c["NFF"], c["CD"], c["RD"], c["CC"], c["NHP"],
                                                      c["NZC"], c["pc"], c["NPC"])
    f = np.float32
    sh = {}

    def chunked(w):
        N = w.shape[1]
        return np.ascontiguousarray(w.reshape(KC, 128, N // 128, 128).transpose(2, 1, 0, 3).reshape(N // 128, 128, KC * 128), f)

    for i, nm in ((1, "ffn1"), (2, "ffn2")):
        sh[f"wg{i}"] = chunked(I[f"{nm}_w_gate"][0])
        sh[f"wu{i}"] = chunked(I[f"{nm}_w_up"][0])
        sh[f"wd{i}"] = np.ascontiguousarray(I[f"{nm}_w_down"][0].reshape(NFF, 128, D), f)
    win = np.zeros((D, NZC * 128), f)
    win[:, :c["INC"]] = I["w_in"][0]
    sh["win"] = chunked(win)
    sh["wo"] = np.ascontiguousarray(I["w_out"][0].reshape(KC, 128, D), f)
    pcol = np.zeros((128, NPC), f)
    col = lambda v: np.asarray(v, f).reshape(-1, 128).T
    pcol[:, pc["n1"]:pc["n1"] + KC] = col(I["ffn1_norm"][0])
    pcol[:, pc["nm"]:pc["nm"] + KC] = col(I["mix_norm"][0])
    pcol[:, pc["n2"]:pc["n2"] + KC] = col(I["ffn2_norm"][0])
    for k in range(3):
        pcol[:, pc["cw"] + k * CC:pc["cw"] + (k + 1) * CC] = col(I["conv_w"][0, k])
    mu = np.zeros(((NZC - 3 * CC) * 128,), f)
    mu[:I["mu_shift"].shape[1]] = I["mu_shift"][0]
    pcol[:, pc["mu"]:pc["mu"] + NZC - 3 * CC] = col(mu)
    pcol[:, pc["kk"]:pc["kk"] + NHP] = col(I["k_k"][0])
    pcol[:, pc["ka"]:pc["ka"] + NHP] = col(I["k_a"][0])
    pcol[:, pc["rk"]:pc["rk"] + NHP] = col(I["r_k"][0])
    for d in range(2):
        pcol[:, pc["a0"] + d * NHP:pc["a0"] + (d + 1) * NHP] = col(I["a0"][0, d])
    sh["pcol"] = pcol
    idx = np.arange(128)
    same = (idx[:, None] // 64) == (idx[None, :] // 64)
    s_ = idx[:, None] % 64
    t_ = idx[None, :] % 64
    sc = -math.exp(-0.5)
    mats = [np.eye(128), same * 1.0,
            same * (s_ <= t_) * sc, same * (s_ < t_) * sc, same * (s_ >= t_) * sc, same * (s_ > t_) * sc,
            same * (t_ < s_), same * (s_ < t_), same * (s_ <= t_),
            same * (t_ > s_), same * (s_ > t_), same * (s_ >= t_)]
    sh["cst"] = np.ascontiguousarray(np.concatenate([np.asarray(a, f) for a in mats], axis=1), f)
    sh["w2t"] = np.ascontiguousarray(I["w2"][0].reshape(128, RD), f)
    sh["a2t"] = np.ascontiguousarray(I["a2"][0].reshape(128, RD), f)
    sh["g2"] = np.ascontiguousarray(I["g2"][0], f)
    sh["w0r"] = np.ascontiguousarray(I["w0"][0].reshape(1, 2 * RD), f)
    lw = np.zeros((128, NHP * 64), f)
    lb = np.zeros((128, NHP * 64), f)
    for hp in range(NHP):
        for h in range(2):
            lw[h * 64:(h + 1) * 64, hp * 64:(hp + 1) * 64] = I["ln_x_w"][0][(2 * hp + h) * 64:(2 * hp + h + 1) * 64][None, :]
            lb[h * 64:(h + 1) * 64, hp * 64:(hp + 1) * 64] = I["ln_x_b"][0][(2 * hp + h) * 64:(2 * hp + h + 1) * 64][None, :]
    sh["lnw"], sh["lnb"] = lw, lb
    sh["fng"] = np.ascontiguousarray(I["final_norm"].reshape(1, D), f)
    return sh, c


def run(cfg, I):
    sh, c = shared_prep(cfg, I)
    T, D, pc = c["T"], c["D"], c["pc"]
    xp = np.asarray(I["x_prompt"], np.float32)[0]
    xsamp = np.asarray(I["x_sample"], np.float32)
    in_maps = []
    for core in range(NCORES):
        xs = np.concatenate([xp[core * T:(core + 1) * T], xsamp[2 * core], xsamp[2 * core + 1]], axis=0)
        xh = np.zeros((128, D), np.float32)
        if core > 0:
            xh[0] = xp[core * T - 1]
        if core < NCORES - 1:
            xh[1] = xp[(core + 1) * T]
        m = dict(sh)
        pcol = sh["pcol"].copy()
        for k in range(NCORES):
            pcol[:, pc["mk"] + k] = 1.0 if k < core else 0.0
            pcol[:, pc["mk"] + 8 + k] = 1.0 if k > core else 0.0
        m["pcol"] = pcol
        m["x"] = np.ascontiguousarray(xs)
        m["xh"] = xh
        in_maps.append(m)
    if cfg.get("mode", "fused") == "split":
        pre_keys = ("xh", "wg1", "wu1", "wd1", "win", "pcol", "cst", "w2t", "a2t", "g2", "w0r", "lnw", "lnb")
        pre_maps = []
        for m in in_maps:
            pm = {k: m[k] for k in pre_keys}
            pm["x"] = np.ascontiguousarray(m["x"][0:T])
            pre_maps.append(pm)
        nc1 = build(dict(cfg, mode="pre"))
        r1 = run_bass_kernel_spmd(nc1, pre_maps, core_ids=list(range(NCORES)))
        gall = np.ascontiguousarray(np.concatenate([np.asarray(r1.results[k]["gin"], np.float32) for k in range(NCORES)], axis=0))
        for k, m in enumerate(in_maps):
            m["gout"] = gall
            m["zT0"] = np.asarray(r1.results[k]["zT0"], np.float32)
            m["x1s0"] = np.asarray(r1.results[k]["x1s0"], np.float32)
        nc = build(dict(cfg, mode="main"))
    else:
        nc = build(cfg)
    res = run_bass_kernel_spmd(nc, in_maps, core_ids=list(range(NCORES)))
    yp = np.zeros((1, NCORES * T, D), np.float32)
    ysm = np.zeros((2 * NCORES, T, D), np.float32)
    for core in range(NCORES):
        o = res.results[core]["out"]
        yp[0, core * T:(core + 1) * T] = o[0:T]
        ysm[2 * core] = o[T:2 * T]
        ysm[2 * core + 1] = o[2 * T:3 * T]
    return (yp, ysm), res


def kernel(**inputs):
    I = {k: np.asarray(v) for k, v in inputs.items()}
    (yp, ysm), _ = run(dict(FULL, mode=MODE), I)
    return (yp, ysm)
```

```python
import math
from contextlib import ExitStack
import numpy as np
import concourse.bass as bass
import concourse.mybir as mybir
from concourse.bass_utils import run_bass_kernel_spmd

F32 = mybir.dt.float32
BF16 = mybir.dt.bfloat16
AF = mybir.ActivationFunctionType
ALU = mybir.AluOpType
AX = mybir.AxisListType

NCORES = 8
MODE = "split"
FULL = dict(D=2048, DFF=5632, T=2048, NSEG=3)
RMS_EPS = 1e-6
GN_EPS = 64e-5
CH = 64


class _Stop(Exception):
    pass


class Tok:
    __slots__ = ("sem", "val", "eng")

    def __init__(self, sem, val, eng):
        self.sem, self.val, self.eng = sem, val, eng


class Buf:
    __slots__ = ("w", "r")

    def __init__(self):
        self.w = None
        self.r = {}


class Eng:
    def __init__(self, ctx, name, h):
        self.ctx, self.name, self.h = ctx, name, h
        self.seen = {}
        self.nsem = 0
        self.new_sem()

    def new_sem(self):
        self.sem = self.ctx.es.enter_context(self.ctx.nc.semaphore(f"s_{self.name}{self.nsem}"))
        self.nsem += 1
        self.cnt = 0

    def wait(self, tok):
        if tok is None:
            return
        k = id(tok.sem)
        if self.seen.get(k, 0) < tok.val:
            self.h.wait_ge(tok.sem, tok.val)
            self.seen[k] = tok.val

    def stamp(self, ins):
        if self.cnt >= 30000:
            self.new_sem()
        self.cnt += 1
        ins.then_inc(self.sem, 1)
        return Tok(self.sem, self.cnt, self.name)


class Ctx:
    def __init__(self, nc, es):
        self.nc, self.es = nc, es
        self.E = {}
        for name, h in (("pe", nc.tensor), ("act", nc.scalar), ("dve", nc.vector),
                        ("pool", nc.gpsimd), ("sp", nc.sync)):
            self.E[name] = Eng(self, name, h)
        self.dsem = {}
        for q in ("sp", "pool"):
            self.dsem[q] = [[es.enter_context(nc.semaphore(f"d_{q}{i}")), 0] for i in range(24)]
        self.dptr = {"sp": 0, "pool": 0}
        self.rr = 0

    def _deps(self, e, R, W):
        eng = self.E[e]
        pe = e == "pe"
        for b in R:
            t = b.w
            if t is not None and not (pe and t.eng == "pe"):
                eng.wait(t)
        for b in W:
            t = b.w
            if t is not None and not (pe and t.eng == "pe"):
                eng.wait(t)
            for t in b.r.values():
                if not (pe and t.eng == "pe"):
                    eng.wait(t)

    def _mark(self, tok, R, W):
        for b in R:
            b.r[tok.eng if tok.eng != "dma" else id(tok.sem)] = tok
        for b in W:
            b.w = tok
            b.r = {}

    def op(self, e, fn, R=(), W=()):
        self._deps(e, R, W)
        tok = self.E[e].stamp(fn())
        self._mark(tok, R, W)
        return tok

    def mmg(self, items, R=(), W=()):
        self._deps("pe", R, W)
        ins = None
        for (o, l, r, st, sp) in items:
            ins = self.nc.tensor.matmul(o, l, r, start=st, stop=sp)
        tok = self.E["pe"].stamp(ins)
        self._mark(tok, R, W)
        return tok

    def dma(self, q, out, in_, R=(), W=()):
        self._deps(q, R, W)
        eng = self.E[q]
        ring = self.dsem[q]
        i = self.dptr[q]
        self.dptr[q] = (i + 1) % len(ring)
        sem, uses = ring[i]
        if uses >= 1800:
            sem = self.es.enter_context(self.nc.semaphore(f"d_{q}{i}_{self.rr}"))
            self.rr += 1
            ring[i] = [sem, 0]
            uses = 0
        elif uses > 0:
            eng.wait(Tok(sem, 16 * uses, "dma"))
        eng.h.dma_start(out=out, in_=in_).then_inc(sem, 16)
        ring[i][1] = uses + 1
        tok = Tok(sem, 16 * (uses + 1), "dma")
        self._mark(tok, R, W)
        return tok

    def barrier(self):
        toks = [Tok(e.sem, e.cnt, e.name) for e in self.E.values() if e.cnt > 0]
        for q in self.dsem:
            for sem, uses in self.dsem[q]:
                if uses > 0:
                    toks.append(Tok(sem, 16 * uses, "dma"))
        for e in self.E.values():
            for t in toks:
                if t.eng != e.name:
                    e.wait(t)


def APx(t, dims, off=0):
    return bass.AP(t.tensor, t.offset + off, [list(t.ap[0])] + [list(d) for d in dims])


def derive(cfg):
    D, DFF, T, NSEG = cfg["D"], cfg["DFF"], cfg["T"], cfg["NSEG"]
    c = dict(cfg)
    c["KC"] = D // 128
    c["NFF"] = DFF // 128
    c["CD"] = D // 2
    c["RD"] = D // 2
    c["CC"] = c["CD"] // 128
    c["NHP"] = c["RD"] // 128
    c["INC"] = 3 * c["CD"] + 3 * c["RD"] + 128 + 128 + 160
    c["NZC"] = (c["INC"] + 127) // 128
    c["NTOK"] = NSEG * T
    c["TB"] = min(512, T)
    c["NT"] = c["TB"] // 128
    c["CB"] = min(512, D)
    c["NB"] = D // c["CB"]
    c["NCH"] = T // CH
    o = 0
    pc = {}
    for nm, n in (("n1", c["KC"]), ("nm", c["KC"]), ("n2", c["KC"]), ("cw", 3 * c["CC"]),
                  ("mu", c["NZC"] - 3 * c["CC"]), ("kk", c["NHP"]), ("ka", c["NHP"]), ("rk", c["NHP"]),
                  ("a0", 2 * c["NHP"]), ("mk", 16)):
        pc[nm] = o
        o += n
    c["pc"] = pc
    c["NPC"] = o
    return c


def build(cfg):
    try:
        return _build(cfg)
    except _Stop as e:
        return e.nc


def _build(cfg):
    c = derive(cfg)
    D, DFF, T, NSEG = c["D"], c["DFF"], c["T"], c["NSEG"]
    KC, NFF, CD, RD, CC, NHP, NZC = c["KC"], c["NFF"], c["CD"], c["RD"], c["CC"], c["NHP"], c["NZC"]
    NTOK, TB, NT, CB, NB, NCH, pc, NPC = c["NTOK"], c["TB"], c["NT"], c["CB"], c["NB"], c["NCH"], c["pc"], c["NPC"]
    NBLK = NTOK // TB
    TQ = (T + 511) // 512 if T >= 512 else 1
    QW = min(512, T)
    NQ = T // QW

    nc = bass.Bass("TRN2", target_bir_lowering=False)
    dt = nc.dram_tensor
    mode = cfg.get("mode", "fused")
    pre, main_ = mode == "pre", mode == "main"
    ein = lambda name, shape: dt(name, shape, F32, kind="ExternalInput").ap()
    x_in = ein("x", [T if pre else NTOK, D])
    xh_in = ein("xh", [128, D])
    nffn = (1,) if pre else (1, 2)
    wg_in = [ein(f"wg{i}", [NFF, 128, KC * 128]) for i in nffn]
    wu_in = [ein(f"wu{i}", [NFF, 128, KC * 128]) for i in nffn]
    wd_in = [ein(f"wd{i}", [NFF, 128, D]) for i in nffn]
    win_in = ein("win", [NZC, 128, KC * 128])
    wo_in = None if pre else ein("wo", [KC, 128, D])
    pcol_in = ein("pcol", [128, NPC])
    cst_in = ein("cst", [128, 12 * 128])
    w2t_in = ein("w2t", [128, RD])
    a2t_in = ein("a2t", [128, RD])
    g2_in = ein("g2", [160, RD])
    w0r_in = ein("w0r", [1, 2 * RD])
    lnw_in = ein("lnw", [128, NHP * 64])
    lnb_in = ein("lnb", [128, NHP * 64])
    fng_in = None if pre else ein("fng", [1, D])
    out_d = None if pre else dt("out", [NTOK, D], F32, kind="ExternalOutput").ap()
    dbg = cfg.get("debug", False)
    skind = "ExternalOutput" if dbg else "Internal"
    k0 = "ExternalOutput" if pre else ("ExternalInput" if main_ else skind)
    x1s0 = dt("x1s0", [T, D], F32, kind=k0).ap()
    x1s12 = dt("x1s12", [NTOK - T, D], F32, kind=skind).ap()
    zT0 = dt("zT0", [NZC * 128, T + 2], F32, kind=k0).ap()
    zT12 = dt("zT12", [NZC * 128, NSEG - 1, T + 2], F32, kind=skind).ap()

    class _ZT:
        def __getitem__(self, idx):
            r, sg, c_ = idx
            return zT0[r, c_] if sg == 0 else zT12[r, sg - 1, c_]

    class _X1:
        def __getitem__(self, idx):
            r, c_ = idx
            if r.start < T:
                return x1s0[r, c_]
            return x1s12[r.start - T:r.stop - T, c_]
    zT, x1s = _ZT(), _X1()
    yT = dt("yT", [D, NTOK], BF16, kind=skind).ap()
    gin = dt("gin", [128, NHP * 2 * 192], F32, kind="ExternalOutput" if pre else "Internal").ap()
    gout = dt("gout", [NCORES * 128, NHP * 2 * 192], F32, kind="ExternalInput" if main_ else "Internal").ap()

    with ExitStack() as es:
        es.enter_context(nc.allow_non_contiguous_dma(reason="single halo columns"))
        cx = Ctx(nc, es)
        uid = [0]

        def sb(st, name, shape, dty):
            uid[0] += 1
            return st.enter_context(nc.sbuf_tensor(f"sb{uid[0]}_{name}", shape, dty))
        ps = es.enter_context(nc.psum_tensor("ps", [128, 8 * 512], F32))
        PSB = [Buf() for _ in range(8)]
        bank = lambda b: ps[:, b * 512:(b + 1) * 512]

        pcol = sb(es, "pcol", [128, NPC], F32)
        cstf = sb(es, "cstf", [128, 12 * 128], F32)
        cstb = sb(es, "cstb", [128, 12 * 128], BF16)
        B_c = Buf()
        cx.dma("sp", pcol[:], pcol_in, W=[B_c])
        cx.dma("sp", cstf[:], cst_in, W=[B_c])
        cx.dma("pool", cstb[:], cst_in, W=[B_c])
        CI = dict(ident=0, bones=1, trif=2, trixf=3, trib=4, trixb=5, msf=6, mstf=7, mitf=8, msb=9, mstb=10, mitb=11)
        cf = lambda nm: cstf[:, CI[nm] * 128:(CI[nm] + 1) * 128]
        cb = lambda nm: cstb[:, CI[nm] * 128:(CI[nm] + 1) * 128]
        identb = cb("ident")

        def norm_transpose(st, xt, XB, ncol, hT, HB, nt, tagc):
            ss, junk, xn, XNB, SSB = st["ss"], st["junk"], st["xn"], st["XNB"], st["SSB"]
            for t in range(nt):
                cx.op("act", lambda t=t: nc.scalar.activation(out=junk[:], in_=xt[t][:], func=AF.Square,
                                                              accum_out=ss[:, 4 * t:4 * t + 1]), R=[XB[t]], W=[SSB[t]])
                cx.op("dve", lambda t=t: nc.vector.tensor_scalar(out=ss[:, 4 * t + 1:4 * t + 2], in0=ss[:, 4 * t:4 * t + 1],
                                                                 scalar1=1.0 / D, scalar2=RMS_EPS, op0=ALU.mult, op1=ALU.add),
                      R=[SSB[t]], W=[SSB[t]])
                cx.op("act", lambda t=t: nc.scalar.activation(out=ss[:, 4 * t + 2:4 * t + 3], in_=ss[:, 4 * t + 1:4 * t + 2],
                                                              func=AF.Sqrt), R=[SSB[t]], W=[SSB[t]])
                cx.op("dve", lambda t=t: nc.vector.reciprocal(out=ss[:, 4 * t + 3:4 * t + 4], in_=ss[:, 4 * t + 2:4 * t + 3]),
                      R=[SSB[t]], W=[SSB[t]])
                cx.op("dve", lambda t=t: nc.vector.tensor_scalar(out=xn[t][:], in0=xt[t][:], scalar1=ss[:, 4 * t + 3:4 * t + 4],
                                                                 scalar2=None, op0=ALU.mult), R=[XB[t], SSB[t]], W=[XNB[t]])
            for kc in range(KC):
                b = st["pb"][kc % 2]
                cx.mmg([(bank(b)[:, t * 128:(t + 1) * 128], xn[t][:, kc * 128:(kc + 1) * 128], identb, True, True)
                        for t in range(nt)], R=[XNB[t] for t in range(nt)] + [B_c], W=[PSB[b]])
                e = "act" if kc % 2 == 0 else "dve"
                if e == "act":
                    cx.op("act", lambda kc=kc, b=b: nc.scalar.activation(out=hT[:, kc, 0:nt * 128], in_=bank(b)[:, 0:nt * 128],
                                                                         func=AF.Copy, scale=pcol[:, ncol + kc:ncol + kc + 1]),
                          R=[PSB[b], B_c], W=[HB])
                else:
                    cx.op("dve", lambda kc=kc, b=b: nc.vector.tensor_scalar(out=hT[:, kc, 0:nt * 128], in0=bank(b)[:, 0:nt * 128],
                                                                            scalar1=pcol[:, ncol + kc:ncol + kc + 1], scalar2=None,
                                                                            op0=ALU.mult), R=[PSB[b], B_c], W=[HB])

        def wload(st, src, cols):
            i = st["wptr"]
            st["wptr"] = (i + 1) % len(st["wring"])
            slot, b = st["wring"][i], st["WRB"][i]
            cx.dma("pool", slot[:, 0:cols], src, W=[b])
            return slot, b

        def ffn(st, wg, wu, wd, hT, HB, aT, AB, xt, XB, nt):
            ntok = nt * 128
            for j in range(NFF):
                sg_, bg = wload(st, wg[j], KC * 128)
                su_, bu = wload(st, wu[j], KC * 128)
                pg, pu = 2 + (j % 2) * 2, 3 + (j % 2) * 2
                cx.mmg([(bank(pg)[:, 0:ntok], sg_[:, kc * 128:(kc + 1) * 128], hT[:, kc, 0:ntok], kc == 0, kc == KC - 1)
                        for kc in range(KC)], R=[bg, HB], W=[PSB[pg]])
                cx.mmg([(bank(pu)[:, 0:ntok], su_[:, kc * 128:(kc + 1) * 128], hT[:, kc, 0:ntok], kc == 0, kc == KC - 1)
                        for kc in range(KC)], R=[bu, HB], W=[PSB[pu]])
                sgt, SGB = st["sg"][j % 2], st["SGB"][j % 2]
                cx.op("act", lambda pg=pg, sgt=sgt: nc.scalar.activation(out=sgt[:, 0:ntok], in_=bank(pg)[:, 0:ntok], func=AF.Silu),
                      R=[PSB[pg]], W=[SGB])
                cx.op("dve", lambda pu=pu, sgt=sgt, j=j: nc.vector.tensor_tensor(out=aT[:, j, 0:ntok], in0=sgt[:, 0:ntok],
                                                                                   in1=bank(pu)[:, 0:ntok], op=ALU.mult),
                      R=[SGB, PSB[pu]], W=[AB])
            proj_tok(st, aT, AB, NFF, wd, xt, XB, nt, 0.5)

        def proj_tok(st, lT, LB, J, w, xt, XB, nt, scale):
            npb = max(1, min(NB, 8 // nt))
            for n0 in range(0, NB, npb):
                nbs = list(range(n0, min(NB, n0 + npb)))
                accs = [(t, n, (t * npb + (n - n0))) for t in range(nt) for n in nbs]
                for j in range(J):
                    wcols = len(nbs) * CB
                    sl, bw = wload(st, w[j][:, n0 * CB:n0 * CB + wcols], wcols)
                    for (t, n, bk) in accs:
                        cx.mmg([(bank(bk)[:, 0:CB], lT[:, j, t * 128:(t + 1) * 128], sl[:, (n - n0) * CB:(n - n0 + 1) * CB],
                                 j == 0, j == J - 1)], R=[bw, LB], W=[PSB[bk]])
                for (t, n, bk) in accs:
                    cx.op("dve", lambda t=t, n=n, bk=bk: nc.vector.scalar_tensor_tensor(
                        out=xt[t][:, n * CB:(n + 1) * CB], in0=bank(bk)[:, 0:CB], scalar=float(scale),
                        in1=xt[t][:, n * CB:(n + 1) * CB], op0=ALU.mult, op1=ALU.add), R=[PSB[bk]], W=[XB[t]])

        def alloc13(ph, ntile):
            st = {}
            st["xt"] = [sb(ph, f"xt{t}", [128, D], F32) for t in range(ntile)]
            st["XB"] = [Buf() for _ in range(ntile)]
            st["xn"] = [sb(ph, f"xn{t}", [128, D], BF16) for t in range(ntile)]
            st["XNB"] = [Buf() for _ in range(ntile)]
            st["junk"] = sb(ph, "junk", [128, D], BF16)
            st["ss"] = sb(ph, "ss", [128, 4 * ntile], F32)
            st["SSB"] = [Buf() for _ in range(ntile)]
            st["hT"] = sb(ph, "hT", [128, KC, ntile * 128], BF16)
            st["HB"] = Buf()
            st["aT"] = sb(ph, "aT", [128, NFF, ntile * 128], BF16)
            st["AB"] = Buf()
            st["sg"] = [sb(ph, f"sg{i}", [128, ntile * 128], F32) for i in range(2)]
            st["SGB"] = [Buf(), Buf()]
            NW = 6
            st["wring"] = [sb(ph, f"wr{i}", [128, D], BF16) for i in range(NW)]
            st["WRB"] = [Buf() for _ in range(NW)]
            st["wptr"] = 0
            st["pb"] = [0, 1]
            return st

        def stop_at(k):
            if cfg.get('phases', 99) == k:
                cx.barrier()
                e_ = _Stop()
                e_.nc = nc
                raise e_

        with ExitStack() as ph:
            st = alloc13(ph, NT)
            zst = [sb(ph, f"zst{i}", [128, TB], F32) for i in range(3)]
            ZSB = [Buf() for _ in range(3)]
            xt, XB, hT, HB, aT, AB = st["xt"], st["XB"], st["hT"], st["HB"], st["aT"], st["AB"]
            if main_:
                blocks = [(b, NT) for b in range(T // TB, NBLK)]
            else:
                blocks = [(b, NT) for b in range(T // TB if pre else NBLK)] + [(-1, 1)]
            for (b, nt) in blocks:
                for t in range(nt):
                    src = x_in[b * TB + t * 128:b * TB + (t + 1) * 128, :] if b >= 0 else xh_in
                    cx.dma("sp", xt[t][:], src, W=[XB[t]])
                norm_transpose(st, xt, XB, pc["n1"], hT, HB, nt, 0)
                ffn(st, wg_in[0], wu_in[0], wd_in[0], hT, HB, aT, AB, xt, XB, nt)
                if b >= 0:
                    for t in range(nt):
                        cx.dma("sp", x1s[b * TB + t * 128:b * TB + (t + 1) * 128, :], xt[t][:], R=[XB[t]])
                norm_transpose(st, xt, XB, pc["nm"], hT, HB, nt, 1)
                ntok = nt * 128
                for j in range(NZC):
                    sw, bw = wload(st, win_in[j], KC * 128)
                    pz = 2 + (j % 4)
                    cx.mmg([(bank(pz)[:, 0:ntok], sw[:, kc * 128:(kc + 1) * 128], hT[:, kc, 0:ntok], kc == 0, kc == KC - 1)
                            for kc in range(KC)], R=[bw, HB], W=[PSB[pz]])
                    zs, zb = zst[j % 3], ZSB[j % 3]
                    if j % 2 == 0:
                        cx.op("act", lambda pz=pz, zs=zs: nc.scalar.copy(out=zs[:, 0:ntok], in_=bank(pz)[:, 0:ntok]), R=[PSB[pz]], W=[zb])
                    else:
                        cx.op("dve", lambda pz=pz, zs=zs: nc.vector.tensor_copy(out=zs[:, 0:ntok], in_=bank(pz)[:, 0:ntok]), R=[PSB[pz]], W=[zb])
                    if b >= 0:
                        seg, t0 = (b * TB) // T, (b * TB) % T
                        cx.dma("sp", zT[j * 128:(j + 1) * 128, seg, 1 + t0:1 + t0 + TB], zs[:, 0:TB], R=[zb])
                    else:
                        cx.dma("sp", zT[j * 128:(j + 1) * 128, 0, 0:1], zs[:, 0:1], R=[zb])
                        cx.dma("sp", zT[j * 128:(j + 1) * 128, 0, T + 1:T + 2], zs[:, 1:2], R=[zb])
            cx.barrier()
            stop_at(1)

        with ExitStack() as ph:
            TP = T + 2
            stg = [sb(ph, "stg0", [128, TP], F32)]
            stg.append(stg[0])
            STB = [Buf()]
            STB.append(STB[0])
            stp = [0]
            tmpA = sb(ph, "tmpA", [128, T], F32)
            tmpB = sb(ph, "tmpB", [128, T], F32)
            TAB, TBB = Buf(), Buf()
            TW = sb(ph, "TW", [128, T], BF16)
            ZA = sb(ph, "ZA", [128, T], BF16)
            SG0 = sb(ph, "SG0", [128, T], BF16)
            SG1 = sb(ph, "SG1", [32, T], BF16)
            SHB = Buf()
            w2t = sb(ph, "w2t", [128, RD], BF16)
            a2t = sb(ph, "a2t", [128, RD], BF16)
            g2a = sb(ph, "g2a", [128, RD], BF16)
            g2b = sb(ph, "g2b", [32, RD], BF16)
            w0r = sb(ph, "w0r", [1, 2 * RD], BF16)
            ones1 = sb(ph, "ones1", [1, 128], BF16)
            onesc = sb(ph, "onesc", [128, 1], BF16)
            lnw = sb(ph, "lnw", [128, NHP * 64], F32)
            lnb = sb(ph, "lnb", [128, NHP * 64], F32)
            omka = sb(ph, "omka", [128, NHP], F32)
            B_p = Buf()
            cx.dma("pool", w2t[:], w2t_in, W=[B_p])
            cx.dma("pool", a2t[:], a2t_in, W=[B_p])
            cx.dma("pool", g2a[:], g2_in[0:128, :], W=[B_p])
            cx.dma("pool", g2b[:], g2_in[128:160, :], W=[B_p])
            cx.dma("pool", w0r[:], w0r_in, W=[B_p])
            cx.dma("sp", lnw[:], lnw_in, W=[B_p])
            cx.dma("sp", lnb[:], lnb_in, W=[B_p])
            cx.op("dve", lambda: nc.vector.memset(ones1[:], 1.0), W=[B_p])
            cx.op("dve", lambda: nc.vector.memset(onesc[:], 1.0), W=[B_p])
            cx.op("dve", lambda: nc.vector.tensor_scalar(out=omka[:], in0=pcol[:, pc["ka"]:pc["ka"] + NHP], scalar1=-1.0,
                                                          scalar2=1.0, op0=ALU.mult, op1=ALU.add), R=[B_c], W=[B_p])
            Rt = sb(ph, "Rt", [128, T], F32)
            Kx = sb(ph, "Kx", [128, T], F32)
            Vt = sb(ph, "Vt", [128, T], BF16)
            KK = sb(ph, "KK", [128, T], F32)
            RKV = [Buf(), Buf(), Buf(), Buf()]
            VC = sb(ph, "VC", [128, NCH, 64], BF16)
            VCB = Buf()
            Aa = sb(ph, "Aa", [128, T], F32)
            LW = sb(ph, "LW", [128, T // 128, 128], F32)
            EEN = max(4 * T, 7424 if cfg.get("GR", 4) == 4 else 4096)
            EE = sb(ph, "EE", [128, EEN], BF16)
            E1 = EE[:, 0:2 * T].bitcast(F32)
            E2 = EE[:, 2 * T:4 * T].bitcast(F32)
            KD = tmpB
            AAB, LWB, E1B, E2B, KDB = Buf(), Buf(), Buf(), Buf(), TBB
            ATt = sb(ph, "ATt", [128, T], BF16)
            BTt = sb(ph, "BTt", [128, T], BF16)
            KTt = sb(ph, "KTt", [128, T], BF16)
            RTt = sb(ph, "RTt", [128, T], BF16)
            RKt = sb(ph, "RKt", [128, T], BF16)
            ATB, BTB, KTB, RTB, RKB = Buf(), Buf(), Buf(), Buf(), Buf()
            GCt = sb(ph, "GCt", [128, NCH], F32)
            GCB = Buf()
            bns = sb(ph, "bns", [128, NCH], F32)
            BNB = Buf()
            OS = sb(ph, "OS", [128, NCH, 64], F32)
            OSB = Buf()
            GR = cfg.get("GR", 4)
            NR = 2
            garr = lambda nm, w, dty: [sb(ph, f"{nm}{i}", [128, GR, w], dty) for i in range(NR)]
            X_ = [garr("X_a", 128, BF16), garr("X_b", 128, BF16)]
            Y_ = [garr("Y_a", 128, BF16), garr("Y_b", 128, BF16)]
            P_ = [garr("P_a", 128, BF16), garr("P_b", 128, BF16)]
            XB_ = [[Buf() for _ in range(NR)] for _ in range(2)]
            YB_ = [[Buf() for _ in range(NR)] for _ in range(2)]
            PB_ = [[Buf() for _ in range(NR)] for _ in range(2)]
            names = ["LAKT", "MRBT", "MRKT", "Atok", "Btok", "Ktok", "W1T"]
            SM = {n: garr(n, 128, BF16) for n in names}
            SMB = {n: [Buf() for _ in range(NR)] for n in names}
            W2p = garr("W2p", 64, BF16)
            W2 = garr("W2", 64, F32)
            W2pB = [Buf() for _ in range(NR)]
            W2B = [Buf() for _ in range(NR)]
            eoff = [0]

            def carve(n):
                o = eoff[0]
                eoff[0] = o + n
                return EE[:, o:o + n]
            g3 = lambda w: carve(GR * w).rearrange("p (g c) -> p g c", c=w)
            for lev in range(2):
                for arr, bufs in ((X_, XB_), (Y_, YB_), (P_, PB_)):
                    arr[lev].append(g3(128))
                    bufs[lev].append(Buf())
            for n in names:
                SM[n].append(g3(128))
                SMB[n].append(Buf())
            W2p.append(g3(64))
            W2pB.append(Buf())
            W2.append(carve(GR * 128).bitcast(F32).rearrange("p (g c) -> p g c", c=64))
            W2B.append(Buf())
            NR = 3
            Ut = [sb(ph, f"Ut{i}", [128, 192], BF16) for i in range(3)]
            UB = [Buf() for _ in range(3)]
            Hf = sb(ph, "Hf", [128, 192], F32)
            Hb = sb(ph, "Hb", [128, 192], BF16)
            Htmp = sb(ph, "Htmp", [128, 192], F32)
            HFB, HBB, HTB = Buf(), Buf(), Buf()
            Hin = sb(ph, "Hin", [128, NHP * 2, 64], F32)
            HINB = Buf()
            gtoks = []
            GA = sb(ph, "GA", [128, NCORES, 192], F32)
            GAB = Buf()
            PTt = sb(ph, "PTt", [128, 128], F32)
            PTB = Buf()
            ytok = sb(ph, "ytok", [128, NCH, 64], BF16)
            YKB = Buf()
            YTs, YTB = RKt, RKB
            gn1 = sb(ph, "gn1", [128, NCH], F32)
            gn2 = sb(ph, "gn2", [128, NCH], F32)
            GNB = Buf()
            cx.op("dve", lambda: nc.vector.memset(ps[:, 0:2048], 0.0), W=[PSB[0], PSB[1], PSB[2], PSB[3]])
            bigq = lambda q: ps[:, (4 + q) * 512:(4 + q) * 512 + QW]
            bdp, cbp, gset = [0], [0], [0]

            def nbd():
                i = bdp[0]
                bdp[0] = (i + 1) % 4
                return i

            def ncb():
                i = cbp[0]
                cbp[0] = 1 - i
                return 6 + i

            def load_shift(chunk, seg, outs, func=None, rows=128, mucol=None):
                i = stp[0]
                stp[0] = 1 - i
                s, SB_ = stg[i], STB[i]
                cx.dma("sp", s[0:rows, :] if seg == 0 else s[0:rows, 1:T + 1],
                       zT[chunk * 128:chunk * 128 + rows, seg, :] if seg == 0 else zT[chunk * 128:chunk * 128 + rows, seg, 1:T + 1],
                       W=[SB_])
                if seg != 0:
                    cx.op("dve", lambda: nc.vector.memset(s[0:rows, 0:1], 0.0), W=[SB_])
                    cx.op("dve", lambda: nc.vector.memset(s[0:rows, T + 1:T + 2], 0.0), W=[SB_])
                out, OB, odt = outs
                mu = pcol[0:rows, mucol:mucol + 1]
                cx.op("dve", lambda: nc.vector.tensor_tensor(out=tmpA[0:rows, :], in0=s[0:rows, 0:T], in1=s[0:rows, 2:T + 2], op=ALU.add),
                      R=[SB_], W=[TAB])
                cx.op("dve", lambda: nc.vector.scalar_tensor_tensor(out=tmpA[0:rows, :], in0=tmpA[0:rows, :], scalar=0.5,
                                                                      in1=s[0:rows, 1:T + 1], op0=ALU.mult, op1=ALU.subtract),
                      R=[SB_, TAB], W=[TAB])
                if func is None:
                    cx.op("dve", lambda: nc.vector.scalar_tensor_tensor(out=out, in0=tmpA[0:rows, :], scalar=mu,
                                                                          in1=s[0:rows, 1:T + 1], op0=ALU.mult, op1=ALU.add),
                          R=[SB_, TAB, B_c], W=[OB])
                else:
                    cx.op("dve", lambda: nc.vector.scalar_tensor_tensor(out=tmpB[0:rows, :], in0=tmpA[0:rows, :], scalar=mu,
                                                                          in1=s[0:rows, 1:T + 1], op0=ALU.mult, op1=ALU.add),
                          R=[SB_, TAB, B_c], W=[TBB])
                    cx.op("act", lambda: nc.scalar.activation(out=out, in_=tmpB[0:rows, :], func=func), R=[TBB], W=[OB])

            def conv_seg(seg):
                for cc in range(CC):
                    sb_, SBb = E1, E1B
                    sc_, SBc = stg[0], STB[0]
                    chb, chc, chx = cc, CC + cc, 2 * CC + cc
                    cx.dma("sp", sb_[:, 0:T], zT[chb * 128:(chb + 1) * 128, seg, 1:T + 1], W=[SBb])
                    lo, hi = (0, TP) if seg == 0 else (1, T + 1)
                    cx.dma("sp", sc_[:, lo:hi], zT[chc * 128:(chc + 1) * 128, seg, lo:hi], W=[SBc])
                    cx.dma("sp", tmpB[:, :], zT[chx * 128:(chx + 1) * 128, seg, 1:T + 1], W=[TBB])
                    if seg == 0:
                        cx.dma("sp", gn1[:, 0:1], zT[chx * 128:(chx + 1) * 128, seg, 0:1], W=[GNB])
                        cx.dma("sp", gn1[:, 1:2], zT[chx * 128:(chx + 1) * 128, seg, T + 1:T + 2], W=[GNB])
                    else:
                        cx.op("dve", lambda: nc.vector.memset(gn1[:, 0:2], 0.0), W=[GNB])
                        cx.op("dve", lambda: nc.vector.memset(sc_[:, 0:1], 0.0), W=[SBc])
                        cx.op("dve", lambda: nc.vector.memset(sc_[:, T + 1:T + 2], 0.0), W=[SBc])
                    cx.op("dve", lambda: nc.vector.tensor_tensor(out=sc_[:, 1:T + 1], in0=sc_[:, 1:T + 1], in1=tmpB[:, :], op=ALU.mult),
                          R=[TBB], W=[SBc])
                    cx.op("dve", lambda: nc.vector.tensor_tensor(out=sc_[:, 0:1], in0=sc_[:, 0:1], in1=gn1[:, 0:1], op=ALU.mult),
                          R=[GNB], W=[SBc])
                    cx.op("dve", lambda: nc.vector.tensor_tensor(out=sc_[:, T + 1:T + 2], in0=sc_[:, T + 1:T + 2], in1=gn1[:, 1:2], op=ALU.mult),
                          R=[GNB], W=[SBc])
                    cw = lambda k: pcol[:, pc["cw"] + k * CC + cc:pc["cw"] + k * CC + cc + 1]
                    cx.op("dve", lambda: nc.vector.tensor_scalar(out=tmpA[:, :], in0=sc_[:, 0:T], scalar1=cw(0), scalar2=None, op0=ALU.mult),
                          R=[SBc, B_c], W=[TAB])
                    cx.op("dve", lambda: nc.vector.scalar_tensor_tensor(out=tmpA[:, :], in0=sc_[:, 1:T + 1], scalar=cw(1), in1=tmpA[:, :],
                                                                          op0=ALU.mult, op1=ALU.add), R=[SBc, B_c, TAB], W=[TAB])
                    cx.op("dve", lambda: nc.vector.scalar_tensor_tensor(out=tmpA[:, :], in0=sc_[:, 2:T + 2], scalar=cw(2), in1=tmpA[:, :],
                                                                          op0=ALU.mult, op1=ALU.add), R=[SBc, B_c, TAB], W=[TAB])
                    cx.op("dve", lambda: nc.vector.tensor_tensor(out=YTs[:, :], in0=tmpA[:, :], in1=sb_[:, 0:T], op=ALU.mult),
                          R=[TAB, SBb], W=[YTB])
                    cx.dma("sp", yT[cc * 128:(cc + 1) * 128, seg * T:(seg + 1) * T], YTs[:, :], R=[YTB])

            def seg_shared(seg):
                base = 3 * CC + 3 * NHP
                mub = pc["mu"]
                load_shift(base, seg, (TW[:, :], SHB, BF16), func=AF.Tanh, mucol=mub + 3 * NHP)
                load_shift(base + 1, seg, (ZA[:, :], SHB, BF16), func=None, mucol=mub + 3 * NHP + 1)
                load_shift(base + 2, seg, (SG0[:, :], SHB, BF16), func=AF.Sigmoid, mucol=mub + 3 * NHP + 2)
                load_shift(base + 3, seg, (SG1[:, :], SHB, BF16), func=AF.Sigmoid, rows=32, mucol=mub + 3 * NHP + 3)

            def hp_prep(seg, hp):
                mub = pc["mu"]
                load_shift(3 * CC + hp, seg, (Rt[:, :], RKV[0], F32), mucol=mub + hp)
                load_shift(3 * CC + NHP + hp, seg, (Kx[:, :], RKV[1], F32), mucol=mub + NHP + hp)
                load_shift(3 * CC + 2 * NHP + hp, seg, (Vt[:, :], RKV[2], BF16), mucol=mub + 2 * NHP + hp)
                cx.op("dve", lambda: nc.vector.tensor_scalar(out=KK[:, :], in0=Kx[:, :], scalar1=pcol[:, pc["kk"] + hp:pc["kk"] + hp + 1],
                                                              scalar2=None, op0=ALU.mult), R=[RKV[1], B_c], W=[RKV[3]])
                cx.op("act", lambda: nc.scalar.activation(out=RKt[:, :], in_=KK[:, :], func=AF.Square), R=[RKV[3]], W=[RKB])
                for q in range(NQ):
                    cx.mmg([(bigq(q), cb("bones"), RKt[:, q * QW:(q + 1) * QW], True, True)], R=[RKB, B_c], W=[PSB[4 + q]])
                    cx.op("dve", lambda q=q: nc.vector.tensor_scalar(out=tmpA[:, q * QW:(q + 1) * QW], in0=bigq(q), scalar1=1e-24,
                                                                      scalar2=None, op0=ALU.max), R=[PSB[4 + q]], W=[TAB])
                cx.op("act", lambda: nc.scalar.activation(out=tmpA[:, :], in_=tmpA[:, :], func=AF.Sqrt), R=[TAB], W=[TAB])
                cx.op("dve", lambda: nc.vector.reciprocal(out=tmpA[:, :], in_=tmpA[:, :]), R=[TAB], W=[TAB])
                cx.op("dve", lambda: nc.vector.tensor_tensor(out=KK[:, :], in0=KK[:, :], in1=tmpA[:, :], op=ALU.mult), R=[TAB, RKV[3]], W=[RKV[3]])
                for c0 in range(NCH):
                    q, off = (c0 * 64) // QW, (c0 * 64) % QW
                    items = []
                    for h in range(2):
                        items.append((ps[h * 64:(h + 1) * 64, (4 + q) * 512 + off:(4 + q) * 512 + off + 64],
                                      Vt[h * 64:(h + 1) * 64, c0 * 64:(c0 + 1) * 64],
                                      cstb[h * 64:(h + 1) * 64, CI["ident"] * 128 + h * 64:CI["ident"] * 128 + (h + 1) * 64], True, True))
                    cx.mmg(items, R=[RKV[2], B_c], W=[PSB[4 + q]])
                for q in range(NQ):
                    cx.op("act", lambda q=q: nc.scalar.copy(out=VC[:, q * (QW // 64):(q + 1) * (QW // 64), :],
                                                            in_=bigq(q).rearrange("p (c v) -> p c v", v=64)), R=[PSB[4 + q]], W=[VCB])

            def dir_prep(seg, hp, d):
                dn = "f" if d == 0 else "b"
                hs = slice(d * 64, (d + 1) * 64)
                cs = slice(hp * 128, (hp + 1) * 128)
                for q in range(NQ):
                    cx.mmg([(bigq(q), a2t[hs, cs], ZA[hs, q * QW:(q + 1) * QW], True, True)], R=[SHB, B_p], W=[PSB[4 + q]])
                    cx.op("act", lambda q=q: nc.scalar.activation(out=Aa[:, q * QW:(q + 1) * QW], in_=bigq(q), func=AF.Sigmoid,
                                                                  bias=pcol[:, pc["a0"] + d * NHP + hp:pc["a0"] + d * NHP + hp + 1]),
                          R=[PSB[4 + q], B_c], W=[AAB])
                for tt in range(T // 128):
                    q, off = (tt * 128) // QW, (tt * 128) % QW
                    o = ps[:, (4 + q) * 512 + off:(4 + q) * 512 + off + 128]
                    cx.mmg([(o, TW[hs, tt * 128:(tt + 1) * 128], w2t[hs, cs], True, False),
                            (o, ones1[0:1, 0:128], w0r[0:1, d * RD + hp * 128:d * RD + (hp + 1) * 128], False, True)],
                           R=[SHB, B_p], W=[PSB[4 + q]])
                for q in range(NQ):
                    cx.op("act", lambda q=q: nc.scalar.activation(out=LW[:, q * (QW // 128):(q + 1) * (QW // 128), :],
                                                                  in_=bigq(q).rearrange("p (t c) -> p t c", c=128), func=AF.Sigmoid),
                          R=[PSB[4 + q]], W=[LWB])
                for (trn, which) in (("tri" + dn, 0), ("trix" + dn, 1)):
                    for tt in range(T // 128):
                        q, off = (tt * 128) // QW, (tt * 128) % QW
                        o = ps[:, (4 + q) * 512 + off:(4 + q) * 512 + off + 128]
                        cx.mmg([(o, LW[:, tt, :], cf(trn), True, True)], R=[LWB, B_c], W=[PSB[4 + q]])
                    if which == 0:
                        for q in range(NQ):
                            cx.op("act", lambda q=q: nc.scalar.activation(out=E1[:, q * QW:(q + 1) * QW], in_=bigq(q), func=AF.Exp),
                                  R=[PSB[4 + q]], W=[E1B])
                            cx.op("act", lambda q=q: nc.scalar.activation(out=E2[:, q * QW:(q + 1) * QW], in_=bigq(q), func=AF.Exp, scale=-1.0),
                                  R=[PSB[4 + q]], W=[E2B])
                        cx.op("dve", lambda: nc.vector.tensor_tensor(out=RTt[:, :], in0=Rt[:, :], in1=E1[:, :], op=ALU.mult),
                              R=[RKV[0], E1B], W=[RTB])
                        gcol = 63 if d == 0 else 0
                        cx.op("dve", lambda: nc.vector.tensor_copy(out=GCt[:, :], in_=E1.rearrange("p (c j) -> p c j", j=64)[:, :, gcol]),
                              R=[E1B], W=[GCB])
                        cx.op("dve", lambda: nc.vector.tensor_tensor(out=tmpA[:, :], in0=KK[:, :], in1=Aa[:, :], op=ALU.mult),
                              R=[RKV[3], AAB], W=[TAB])
                        cx.op("dve", lambda: nc.vector.tensor_tensor(out=BTt[:, :], in0=tmpA[:, :], in1=E2[:, :], op=ALU.mult),
                              R=[TAB, E2B], W=[BTB])
                        cx.op("dve", lambda: nc.vector.tensor_scalar(out=tmpA[:, :], in0=Aa[:, :],
                                                                      scalar1=pcol[:, pc["ka"] + hp:pc["ka"] + hp + 1],
                                                                      scalar2=omka[:, hp:hp + 1], op0=ALU.mult, op1=ALU.add),
                              R=[AAB, B_c, B_p], W=[TAB])
                        cx.op("dve", lambda: nc.vector.tensor_tensor(out=KD[:, :], in0=Kx[:, :], in1=tmpA[:, :], op=ALU.mult),
                              R=[TAB, RKV[1]], W=[KDB])
                        cx.op("dve", lambda: nc.vector.tensor_tensor(out=KTt[:, :], in0=KD[:, :], in1=E2[:, :], op=ALU.mult),
                              R=[KDB, E2B], W=[KTB])
                        cx.op("dve", lambda: nc.vector.scalar_tensor_tensor(out=RKt[:, :], in0=Rt[:, :],
                                                                              scalar=pcol[:, pc["rk"] + hp:pc["rk"] + hp + 1],
                                                                              in1=KD[:, :], op0=ALU.mult, op1=ALU.mult),
                              R=[RKV[0], KDB, B_c], W=[RKB])
                    else:
                        for q in range(NQ):
                            cx.op("act", lambda q=q: nc.scalar.activation(out=E1[:, q * QW:(q + 1) * QW], in_=bigq(q), func=AF.Exp),
                                  R=[PSB[4 + q]], W=[E1B])
                        cx.op("dve", lambda: nc.vector.scalar_tensor_tensor(out=ATt[:, :], in0=KK[:, :], scalar=-1.0, in1=E1[:, :],
                                                                              op0=ALU.mult, op1=ALU.mult), R=[RKV[3], E1B], W=[ATB])
                q0 = 0
                for c0 in range(NCH):
                    items = [(ps[h * 64:(h + 1) * 64, 4 * 512 + c0:4 * 512 + c0 + 1], RKt[h * 64:(h + 1) * 64, c0 * 64:(c0 + 1) * 64],
                              onesc[h * 64:(h + 1) * 64, 0:1], True, True) for h in range(2)]
                    cx.mmg(items, R=[RKB, B_p], W=[PSB[4]])
                if d == 0:
                    cx.op("dve", lambda: nc.vector.tensor_copy(out=bns[:, :], in_=ps[:, 4 * 512:4 * 512 + NCH]), R=[PSB[4]], W=[BNB])
                else:
                    cx.op("dve", lambda: nc.vector.tensor_tensor(out=bns[:, :], in0=bns[:, :], in1=ps[:, 4 * 512:4 * 512 + NCH], op=ALU.add),
                          R=[PSB[4]], W=[BNB])

            def blk(ap, h):
                return ap[h * 64:(h + 1) * 64, h * 64:(h + 1) * 64]

            def chunk_loop(hp, d, aug, outmode):
                dn = "f" if d == 0 else "b"
                NS = 192 if aug else 64
                order = list(range(NCH)) if d == 0 else list(range(NCH - 1, -1, -1))
                groups = [order[g0:g0 + GR] for g0 in range(0, NCH, GR)]
                gsets = [i % NR for i in range(len(groups))]
                cx.barrier()

                def pre_gen(grp, st_):
                    n_ = len(grp)
                    csl = [slice(c0 * 64, (c0 + 1) * 64) for c0 in grp]
                    bankv = lambda b_, w: ps[:, b_ * 512:b_ * 512 + n_ * w].rearrange("p (g c) -> p g c", c=w)
                    bcast = lambda ap_: APx(ap_, [[0, n_], [1, 128]])

                    def bd_mm(lt, LB_, rt, RB_, ident_rhs=False):
                        b_ = nbd()
                        items = []
                        for gi in range(n_):
                            for h in range(2):
                                o = ps[h * 64:(h + 1) * 64, b_ * 512 + gi * 128 + h * 64:b_ * 512 + gi * 128 + (h + 1) * 64]
                                rr = blk(identb, h) if ident_rhs else rt[h * 64:(h + 1) * 64, csl[gi]]
                                items.append((o, lt[h * 64:(h + 1) * 64, csl[gi]], rr, True, True))
                        cx.mmg(items, R=[LB_, RB_] if not ident_rhs else [LB_, B_c], W=[PSB[b_]])
                        return b_

                    def masked(b_, mask, outt, OB_):
                        cx.op("dve", lambda: nc.vector.tensor_tensor(out=outt[:, 0:n_, :], in0=bankv(b_, 128), in1=bcast(cf(mask)), op=ALU.mult),
                              R=[PSB[b_], B_c], W=[OB_])
                    X0, Y0, P0 = X_[0][st_], Y_[0][st_], P_[0][st_]
                    masked(bd_mm(ATt, ATB, BTt, BTB), "ms" + dn, X0, XB_[0][st_])
                    yield
                    masked(bd_mm(BTt, BTB, ATt, ATB), "mst" + dn, Y0, YB_[0][st_])
                    yield
                    cx.op("dve", lambda: nc.vector.tensor_tensor(out=P0[:, 0:n_, :], in0=Y0[:, 0:n_, :], in1=bcast(cb("ident")), op=ALU.add),
                          R=[YB_[0][st_], B_c], W=[PB_[0][st_]])
                    yield
                    masked(bd_mm(KTt, KTB, ATt, ATB), "mst" + dn, SM["LAKT"][st_], SMB["LAKT"][st_])
                    yield
                    if not aug:
                        masked(bd_mm(BTt, BTB, RTt, RTB), "mit" + dn, SM["MRBT"][st_], SMB["MRBT"][st_])
                        yield
                        masked(bd_mm(KTt, KTB, RTt, RTB), "mit" + dn, SM["MRKT"][st_], SMB["MRKT"][st_])
                        yield
                    for (nm, src, SB_) in (("Atok", ATt, ATB), ("Btok", BTt, BTB), ("Ktok", KTt, KTB)):
                        b_ = bd_mm(src, SB_, None, None, ident_rhs=True)
                        cx.op("act", lambda nm=nm, b_=b_: nc.scalar.copy(out=SM[nm][st_][:, 0:n_, :], in_=bankv(b_, 128)),
                              R=[PSB[b_]], W=[SMB[nm][st_]])
                        yield
                    cur = 0
                    for lev in range(5):
                        nxt = 1 - cur
                        last = lev == 4
                        Xc, Yc, Pc = X_[cur][st_], Y_[cur][st_], P_[cur][st_]
                        Xn, Yn, Pn = X_[nxt][st_], Y_[nxt][st_], P_[nxt][st_]
                        b_ = nbd()
                        cx.mmg([(ps[:, b_ * 512 + gi * 128:b_ * 512 + (gi + 1) * 128], Yc[:, gi, :], Xc[:, gi, :], True, True) for gi in range(n_)],
                               R=[YB_[cur][st_], XB_[cur][st_]], W=[PSB[b_]])
                        cx.op("act", lambda b_=b_, Xn=Xn: nc.scalar.copy(out=Xn[:, 0:n_, :], in_=bankv(b_, 128)), R=[PSB[b_]], W=[XB_[nxt][st_]])
                        yield
                        if not last:
                            b_ = nbd()
                            cx.mmg([(ps[:, b_ * 512 + gi * 128:b_ * 512 + (gi + 1) * 128], Xc[:, gi, :], Yc[:, gi, :], True, True) for gi in range(n_)],
                                   R=[YB_[cur][st_], XB_[cur][st_]], W=[PSB[b_]])
                            cx.op("dve", lambda b_=b_, Yn=Yn: nc.vector.tensor_copy(out=Yn[:, 0:n_, :], in_=bankv(b_, 128)), R=[PSB[b_]], W=[YB_[nxt][st_]])
                            yield
                        b_ = nbd()
                        cx.mmg([(ps[:, b_ * 512 + gi * 128:b_ * 512 + (gi + 1) * 128], Xn[:, gi, :], Pc[:, gi, :], True, True) for gi in range(n_)],
                               R=[XB_[nxt][st_], PB_[cur][st_]], W=[PSB[b_]])
                        cx.op("dve", lambda b_=b_, Pn=Pn, Pc=Pc: nc.vector.tensor_tensor(out=Pn[:, 0:n_, :], in0=bankv(b_, 128), in1=Pc[:, 0:n_, :], op=ALU.add),
                              R=[PSB[b_], PB_[cur][st_]], W=[PB_[nxt][st_]])
                        yield
                        cur = nxt
                    PF, PFB = P_[cur][st_], PB_[cur][st_]
                    LAKT, Atok, Btok, Ktok, W1T = SM["LAKT"][st_], SM["Atok"][st_], SM["Btok"][st_], SM["Ktok"][st_], SM["W1T"][st_]
                    b_ = ncb()
                    cx.mmg([(ps[:, b_ * 512 + gi * 64:b_ * 512 + (gi + 1) * 64], LAKT[:, gi, :], VC[:, grp[gi], :], True, True) for gi in range(n_)],
                           R=[SMB["LAKT"][st_], VCB], W=[PSB[b_]])
                    cx.op("act", lambda b_=b_: nc.scalar.copy(out=W2p[st_][:, 0:n_, :], in_=bankv(b_, 64)), R=[PSB[b_]], W=[W2pB[st_]])
                    yield
                    b_ = nbd()
                    cx.mmg([(ps[:, b_ * 512 + gi * 128:b_ * 512 + (gi + 1) * 128], Atok[:, gi, :], PF[:, gi, :], True, True) for gi in range(n_)],
                           R=[SMB["Atok"][st_], PFB], W=[PSB[b_]])
                    cx.op("dve", lambda b_=b_: nc.vector.tensor_copy(out=W1T[:, 0:n_, :], in_=bankv(b_, 128)), R=[PSB[b_]], W=[SMB["W1T"][st_]])
                    yield
                    b_ = ncb()
                    cx.mmg([(ps[:, b_ * 512 + gi * 64:b_ * 512 + (gi + 1) * 64], PF[:, gi, :], W2p[st_][:, gi, :], True, True) for gi in range(n_)],
                           R=[PFB, W2pB[st_]], W=[PSB[b_]])
                    cx.op("act", lambda b_=b_: nc.scalar.copy(out=W2[st_][:, 0:n_, :], in_=bankv(b_, 64)), R=[PSB[b_]], W=[W2B[st_]])
                    yield

                def chain_gen(grp, st_):
                    n_ = len(grp)
                    csl = [slice(c0 * 64, (c0 + 1) * 64) for c0 in grp]
                    LAKT, Atok, Btok, Ktok, W1T = SM["LAKT"][st_], SM["Atok"][st_], SM["Btok"][st_], SM["Ktok"][st_], SM["W1T"][st_]
                    for gi, c0 in enumerate(grp):
                        ui = c0 % 3
                        U, UB_ = Ut[ui], UB[ui]
                        qa = ps[:, 4 * 512:4 * 512 + NS]
                        cx.mmg([(qa, W1T[:, gi, :], Hb[:, 0:NS], True, True)], R=[SMB["W1T"][st_], HBB], W=[PSB[4]])
                        cx.op("dve", lambda qa=qa, U=U, gi=gi: nc.vector.tensor_tensor(out=U[:, 0:64], in0=qa[:, 0:64], in1=W2[st_][:, gi, :], op=ALU.add),
                              R=[PSB[4], W2B[st_]], W=[UB_])
                        yield
                        if aug:
                            cx.op("act", lambda qa=qa, U=U: nc.scalar.copy(out=U[:, 64:192], in_=qa[:, 64:192]), R=[PSB[4]], W=[UB_])
                            yield
                        else:
                            ob = ncb()
                            oa = ps[:, ob * 512:ob * 512 + 64]
                            items = [(oa[h * 64:(h + 1) * 64, :], RTt[h * 64:(h + 1) * 64, csl[gi]], Hb[h * 64:(h + 1) * 64, 0:64], True, False)
                                     for h in range(2)]
                            items += [(oa, SM["MRKT"][st_][:, gi, :], VC[:, c0, :], False, False),
                                      (oa, SM["MRBT"][st_][:, gi, :], U[:, 0:64], False, True)]
                            cx.mmg(items, R=[SMB["MRKT"][st_], SMB["MRBT"][st_], VCB, UB_, RTB, HBB], W=[PSB[ob]])
                            if outmode == 0:
                                cx.op("act", lambda oa=oa, c0=c0: nc.scalar.copy(out=OS[:, c0, :], in_=oa), R=[PSB[ob]], W=[OSB])
                                yield
                            else:
                                cx.op("dve", lambda oa=oa, c0=c0: nc.vector.tensor_tensor(out=OS[:, c0, :], in0=oa, in1=OS[:, c0, :], op=ALU.add),
                                      R=[PSB[ob], OSB], W=[OSB])
                                yield
                        ha = ps[:, 5 * 512:5 * 512 + NS]
                        cx.mmg([(ha, Btok[:, gi, :], U[:, 0:NS], True, False),
                                (ha[:, 0:64], Ktok[:, gi, :], VC[:, c0, :], False, True)],
                               R=[SMB["Btok"][st_], SMB["Ktok"][st_], UB_, VCB], W=[PSB[5]])
                        cx.op("dve", lambda ha=ha: nc.vector.tensor_tensor(out=Htmp[:, 0:NS], in0=ha, in1=Hf[:, 0:NS], op=ALU.add),
                              R=[PSB[5], HFB], W=[HTB])
                        cx.op("dve", lambda c0=c0: nc.vector.tensor_scalar(out=Hf[:, 0:NS], in0=Htmp[:, 0:NS], scalar1=GCt[:, c0:c0 + 1],
                                                                             scalar2=None, op0=ALU.mult), R=[HTB, GCB], W=[HFB])
                        cx.op("act", lambda c0=c0: nc.scalar.activation(out=Hb[:, 0:NS], in_=Htmp[:, 0:NS], func=AF.Copy,
                                                                         scale=GCt[:, c0:c0 + 1]), R=[HTB, GCB], W=[HBB])
                        yield


                pres = {}

                def p_start(i):
                    if i < len(groups) and i not in pres:
                        pres[i] = pre_gen(groups[i], gsets[i])

                def p_step(i, n):
                    for _ in range(n):
                        g_ = pres.get(i)
                        if g_ is None:
                            return
                        try:
                            next(g_)
                        except StopIteration:
                            pres[i] = None

                p_start(0)
                p_start(1)
                for gix, grp in enumerate(groups):
                    while pres.get(gix) is not None:
                        p_step(gix, 1)
                        p_step(gix + 1, 1)
                    p_start(gix + 2)
                    for _ in chain_gen(grp, gsets[gix]):
                        p_step(gix + 1, cfg.get("PRATIO", 2))
                        p_step(gix + 2, 1)
                cx.barrier()

            def init_state(hp, d, mode):
                if mode == "zero":
                    cx.op("dve", lambda: nc.vector.memset(Hf[:, :], 0.0), W=[HFB])
                    cx.op("dve", lambda: nc.vector.memset(Hb[:, :], 0.0), W=[HBB])
                elif mode == "aug":
                    cx.op("dve", lambda: nc.vector.memset(Hf[:, 0:64], 0.0), W=[HFB])
                    cx.op("dve", lambda: nc.vector.tensor_copy(out=Hf[:, 64:192], in_=cf("ident")), R=[B_c], W=[HFB])
                    cx.op("dve", lambda: nc.vector.memset(Hb[:, 0:64], 0.0), W=[HBB])
                    cx.op("dve", lambda: nc.vector.tensor_copy(out=Hb[:, 64:192], in_=cf("ident")), R=[B_c], W=[HBB])
                else:
                    i = hp * 2 + d
                    cx.op("dve", lambda: nc.vector.tensor_copy(out=Hf[:, 0:64], in_=Hin[:, i, :]), R=[HINB], W=[HFB])
                    cx.op("dve", lambda: nc.vector.tensor_copy(out=Hb[:, 0:64], in_=Hin[:, i, :]), R=[HINB], W=[HBB])

            def post(seg, hp):
                NC_ = NCH
                bc = lambda t_: APx(t_[:, :], [[1, NC_], [0, 64]])
                lw = APx(lnw[:, hp * 64:(hp + 1) * 64], [[0, NC_], [1, 64]])
                lb = APx(lnb[:, hp * 64:(hp + 1) * 64], [[0, NC_], [1, 64]])
                OSv = OS[:, :, :]
                cx.op("dve", lambda: nc.vector.tensor_reduce(out=gn1[:, :], in_=OSv, axis=AX.X, op=ALU.add), R=[OSB], W=[GNB])
                cx.op("dve", lambda: nc.vector.tensor_scalar(out=gn1[:, :], in0=gn1[:, :], scalar1=1.0 / 64, scalar2=None, op0=ALU.mult),
                      R=[GNB], W=[GNB])
                cx.op("dve", lambda: nc.vector.tensor_tensor(out=OSv, in0=OSv, in1=bc(gn1), op=ALU.subtract), R=[GNB, OSB], W=[OSB])
                tA = tmpA[:, 0:NC_ * 64].rearrange("p (c v) -> p c v", v=64)
                cx.op("act", lambda: nc.scalar.activation(out=tA, in_=OSv, func=AF.Square), R=[OSB], W=[TAB])
                cx.op("dve", lambda: nc.vector.tensor_reduce(out=gn2[:, :], in_=tA, axis=AX.X, op=ALU.add), R=[TAB], W=[GNB])
                cx.op("dve", lambda: nc.vector.tensor_scalar(out=gn2[:, :], in0=gn2[:, :], scalar1=1.0 / 64, scalar2=GN_EPS,
                                                              op0=ALU.mult, op1=ALU.add), R=[GNB], W=[GNB])
                cx.op("act", lambda: nc.scalar.activation(out=gn2[:, :], in_=gn2[:, :], func=AF.Sqrt), R=[GNB], W=[GNB])
                cx.op("dve", lambda: nc.vector.reciprocal(out=gn2[:, :], in_=gn2[:, :]), R=[GNB], W=[GNB])
                cx.op("dve", lambda: nc.vector.tensor_tensor(out=OSv, in0=OSv, in1=bc(gn2), op=ALU.mult), R=[GNB, OSB], W=[OSB])
                cx.op("dve", lambda: nc.vector.tensor_tensor(out=OSv, in0=OSv, in1=lw, op=ALU.mult), R=[B_p, OSB], W=[OSB])
                cx.op("dve", lambda: nc.vector.tensor_tensor(out=OSv, in0=OSv, in1=lb, op=ALU.add), R=[B_p, OSB], W=[OSB])
                cx.op("dve", lambda: nc.vector.tensor_tensor(out=tA, in0=VC[:, :, :], in1=bc(bns), op=ALU.mult), R=[VCB, BNB], W=[TAB])
                cx.op("dve", lambda: nc.vector.tensor_tensor(out=OSv, in0=OSv, in1=tA, op=ALU.add), R=[TAB, OSB], W=[OSB])
                for c0 in range(NC_):
                    q, off = (c0 * 64) // QW, (c0 * 64) % QW
                    items = []
                    for h in range(2):
                        o = ps[h * 64:(h + 1) * 64, (4 + q) * 512 + off:(4 + q) * 512 + off + 64]
                        gc_ = slice(hp * 128 + h * 64, hp * 128 + (h + 1) * 64)
                        items.append((o, SG0[:, c0 * 64:(c0 + 1) * 64], g2a[:, gc_], True, False))
                        items.append((o, SG1[:, c0 * 64:(c0 + 1) * 64], g2b[:, gc_], False, True))
                    cx.mmg(items, R=[SHB, B_p], W=[PSB[4 + q]])
                for q in range(NQ):
                    cw_ = QW // 64
                    cx.op("dve", lambda q=q: nc.vector.tensor_tensor(out=ytok[:, q * cw_:(q + 1) * cw_, :], in0=OS[:, q * cw_:(q + 1) * cw_, :],
                                                                      in1=bigq(q).rearrange("p (c v) -> p c v", v=64), op=ALU.mult),
                          R=[OSB, PSB[4 + q]], W=[YKB])
                for c0 in range(NC_):
                    q, off = (c0 * 64) // QW, (c0 * 64) % QW
                    items = []
                    for h in range(2):
                        o = ps[h * 64:(h + 1) * 64, (4 + q) * 512 + off:(4 + q) * 512 + off + 64]
                        items.append((o, ytok[h * 64:(h + 1) * 64, c0, :], blk(identb, h), True, True))
                    cx.mmg(items, R=[YKB, B_c], W=[PSB[4 + q]])
                for q in range(NQ):
                    cx.op("act", lambda q=q: nc.scalar.copy(out=YTs[:, q * QW:(q + 1) * QW], in_=bigq(q)), R=[PSB[4 + q]], W=[YTB])
                cx.dma("sp", yT[CD + hp * 128:CD + (hp + 1) * 128, seg * T:(seg + 1) * T], YTs[:, :], R=[YTB])

            def run_seg(seg, mode):
                seg_shared(seg)
                stop_at(2.01)
                if mode != "aug":
                    conv_seg(seg)
                for hp in range(NHP):
                    hp_prep(seg, hp)
                    stop_at(2.02)
                    for d in range(2):
                        dir_prep(seg, hp, d)
                        stop_at(2.03)
                        init_state(hp, d, mode)
                        chunk_loop(hp, d, mode == "aug", d)
                        if mode == "aug":
                            i_ = hp * 2 + d
                            gtoks.append(cx.dma("sp", gin[:, i_ * 192:(i_ + 1) * 192], Hf[:, :], R=[HFB]))
                    if mode != "aug":
                        post(seg, hp)

            stop_at(1.5)
            B_g = Buf()
            if not main_:
                run_seg(0, "aug")
                stop_at(2.1)
                if pre:
                    cx.barrier()
                    e_ = _Stop()
                    e_.nc = nc
                    raise e_
                for tg in gtoks:
                    cx.E["pool"].wait(tg)
                cc_tok = cx.E["pool"].stamp(nc.gpsimd.collective_compute("AllGather", ALU.bypass, replica_groups=[list(range(NCORES))],
                                                                         ins=[gin], outs=[gout]))
                B_g.w = cc_tok
                stop_at(2.2)
            for seg in range(1, NSEG):
                run_seg(seg, "zero")
            stop_at(2.3)
            gv = gout.rearrange("(k p) (i c) -> p k i c", p=128, c=192)
            for hp in range(NHP):
                for d in range(2):
                    i = hp * 2 + d
                    cx.dma("sp", GA[:, :, :], gv[:, :, i, :], R=[B_g], W=[GAB])
                    cx.op("dve", lambda: nc.vector.memset(Hf[:, 0:64], 0.0), W=[HFB])
                    ks = list(range(NCORES)) if d == 0 else list(range(NCORES - 1, -1, -1))
                    for k in ks:
                        g = ncb()
                        cx.mmg([(ps[:, g * 512:g * 512 + 128], GA[:, k, 64:192], cf("ident"), True, True)], R=[GAB, B_c], W=[PSB[g]])
                        cx.op("act", lambda g=g: nc.scalar.copy(out=PTt[:, :], in_=ps[:, g * 512:g * 512 + 128]), R=[PSB[g]], W=[PTB])
                        g = ncb()
                        cx.mmg([(ps[:, g * 512:g * 512 + 64], PTt[:, :], Hf[:, 0:64], True, True)], R=[PTB, HFB], W=[PSB[g]])
                        cx.op("dve", lambda g=g, k=k: nc.vector.tensor_tensor(out=Htmp[:, 0:64], in0=ps[:, g * 512:g * 512 + 64], in1=GA[:, k, 0:64], op=ALU.add),
                              R=[PSB[g], GAB], W=[HTB])
                        cx.op("dve", lambda: nc.vector.tensor_tensor(out=Htmp[:, 0:64], in0=Htmp[:, 0:64], in1=Hf[:, 0:64], op=ALU.subtract),
                              R=[HTB, HFB], W=[HTB])
                        mc = pcol[:, pc["mk"] + d * 8 + k:pc["mk"] + d * 8 + k + 1]
                        cx.op("dve", lambda mc=mc: nc.vector.scalar_tensor_tensor(out=Hf[:, 0:64], in0=Htmp[:, 0:64], scalar=mc, in1=Hf[:, 0:64],
                                                                                    op0=ALU.mult, op1=ALU.add), R=[HTB, HFB, B_c], W=[HFB])
                    cx.op("dve", lambda i=i: nc.vector.tensor_copy(out=Hin[:, i, :], in_=Hf[:, 0:64]), R=[HFB], W=[HINB])
            stop_at(2.4)
            run_seg(0, "hin")
            cx.barrier()
            stop_at(2.5)

        with ExitStack() as ph:
            st = alloc13(ph, NT)
            yTt = sb(ph, "yTt", [128, KC, TB], BF16)
            YB3 = Buf()
            fng = sb(ph, "fng", [128, D], F32)
            FGB = Buf()
            cx.dma("sp", fng[:], bass.AP(fng_in.tensor, 0, [[0, 128], [1, D]]), W=[FGB])
            xt, XB, hT, HB, aT, AB = st["xt"], st["XB"], st["hT"], st["HB"], st["aT"], st["AB"]
            otoks = []
            for b in range(NBLK):
                for t in range(NT):
                    cx.dma("sp", xt[t][:], x1s[b * TB + t * 128:b * TB + (t + 1) * 128, :], W=[XB[t]])
                for kc in range(KC):
                    cx.dma("sp", yTt[:, kc, :], yT[kc * 128:(kc + 1) * 128, b * TB:(b + 1) * TB], W=[YB3])
                proj_tok(st, yTt, YB3, KC, wo_in, xt, XB, NT, 1.0)
                norm_transpose(st, xt, XB, pc["n2"], hT, HB, NT, 2)
                ffn(st, wg_in[1], wu_in[1], wd_in[1], hT, HB, aT, AB, xt, XB, NT)
                ss, junk, SSB = st["ss"], st["junk"], st["SSB"]
                for t in range(NT):
                    cx.op("act", lambda t=t: nc.scalar.activation(out=junk[:], in_=xt[t][:], func=AF.Square, accum_out=ss[:, 4 * t:4 * t + 1]),
                          R=[XB[t]], W=[SSB[t]])
                    cx.op("dve", lambda t=t: nc.vector.tensor_scalar(out=ss[:, 4 * t + 1:4 * t + 2], in0=ss[:, 4 * t:4 * t + 1], scalar1=1.0 / D,
                                                                     scalar2=RMS_EPS, op0=ALU.mult, op1=ALU.add), R=[SSB[t]], W=[SSB[t]])
                    cx.op("act", lambda t=t: nc.scalar.activation(out=ss[:, 4 * t + 2:4 * t + 3], in_=ss[:, 4 * t + 1:4 * t + 2], func=AF.Sqrt),
                          R=[SSB[t]], W=[SSB[t]])
                    cx.op("dve", lambda t=t: nc.vector.reciprocal(out=ss[:, 4 * t + 3:4 * t + 4], in_=ss[:, 4 * t + 2:4 * t + 3]),
                          R=[SSB[t]], W=[SSB[t]])
                    cx.op("dve", lambda t=t: nc.vector.scalar_tensor_tensor(out=xt[t][:], in0=xt[t][:], scalar=ss[:, 4 * t + 3:4 * t + 4],
                                                                              in1=fng[:], op0=ALU.mult, op1=ALU.mult),
                          R=[SSB[t], FGB], W=[XB[t]])
                    otoks.append(cx.dma("sp", out_d[b * TB + t * 128:b * TB + (t + 1) * 128, :], xt[t][:], R=[XB[t]]))
            for tk in otoks[-24:]:
                cx.E["sp"].wait(tk)
            cx.barrier()
    return nc


def host_prep(cfg, inputs, xs, xhs, core):
    c = derive(cfg)
    D, KC, NFF, CD, RD, CC, NHP, NZC, pc, NPC = c["D"], c["KC"], c["NFF"], c["CD"], c["RD"], c["CC"], c["NHP"], c["NZC"], c["pc"], c["NPC"]
    f = np.float32
    m = {"x": np.ascontiguousarray(xs, f), "xh": np.ascontiguousarray(xhs, f)}
    return m


def shared_prep(cfg, I):
    c = derive(cfg)
    D, DFF, KC, NFF, CD, RD, CC, NHP, NZC, pc, NPC = (c["D"], c["DFF"], c["KC"], c["NFF"], c["CD"], c["RD"], c["CC"], c["NHP"],
                                                      c["NZC"], c["pc"], c["NPC"])
    f = np.float32
    sh = {}

    def chunked(w):
        N = w.shape[1]
        return np.ascontiguousarray(w.reshape(KC, 128, N // 128, 128).transpose(2, 1, 0, 3).reshape(N // 128, 128, KC * 128), f)

    for i, nm in ((1, "ffn1"), (2, "ffn2")):
        sh[f"wg{i}"] = chunked(I[f"{nm}_w_gate"][0])
        sh[f"wu{i}"] = chunked(I[f"{nm}_w_up"][0])
        sh[f"wd{i}"] = np.ascontiguousarray(I[f"{nm}_w_down"][0].reshape(NFF, 128, D), f)
    win = np.zeros((D, NZC * 128), f)
    win[:, :c["INC"]] = I["w_in"][0]
    sh["win"] = chunked(win)
    sh["wo"] = np.ascontiguousarray(I["w_out"][0].reshape(KC, 128, D), f)
    pcol = np.zeros((128, NPC), f)
    col = lambda v: np.asarray(v, f).reshape(-1, 128).T
    pcol[:, pc["n1"]:pc["n1"] + KC] = col(I["ffn1_norm"][0])
    pcol[:, pc["nm"]:pc["nm"] + KC] = col(I["mix_norm"][0])
    pcol[:, pc["n2"]:pc["n2"] + KC] = col(I["ffn2_norm"][0])
    for k in range(3):
        pcol[:, pc["cw"] + k * CC:pc["cw"] + (k + 1) * CC] = col(I["conv_w"][0, k])
    mu = np.zeros(((NZC - 3 * CC) * 128,), f)
    mu[:I["mu_shift"].shape[1]] = I["mu_shift"][0]
    pcol[:, pc["mu"]:pc["mu"] + NZC - 3 * CC] = col(mu)
    pcol[:, pc["kk"]:pc["kk"] + NHP] = col(I["k_k"][0])
    pcol[:, pc["ka"]:pc["ka"] + NHP] = col(I["k_a"][0])
    pcol[:, pc["rk"]:pc["rk"] + NHP] = col(I["r_k"][0])
    for d in range(2):
        pcol[:, pc["a0"] + d * NHP:pc["a0"] + (d + 1) * NHP] = col(I["a0"][0, d])
    sh["pcol"] = pcol
    idx = np.arange(128)
    same = (idx[:, None] // 64) == (idx[None, :] // 64)
    s_ = idx[:, None] % 64
    t_ = idx[None, :] % 64
    sc = -math.exp(-0.5)
    mats = [np.eye(128), same * 1.0,
            same * (s_ <= t_) * sc, same * (s_ < t_) * sc, same * (s_ >= t_) * sc, same * (s_ > t_) * sc,
            same * (t_ < s_), same * (s_ < t_), same * (s_ <= t_),
            same * (t_ > s_), same * (s_ > t_), same * (s_ >= t_)]
    sh["cst"] = np.ascontiguousarray(np.concatenate([np.asarray(a, f) for a in mats], axis=1), f)
    sh["w2t"] = np.ascontiguousarray(I["w2"][0].reshape(128, RD), f)
    sh["a2t"] = np.ascontiguousarray(I["a2"][0].reshape(128, RD), f)
    sh["g2"] = np.ascontiguousarray(I["g2"][0], f)
    sh["w0r"] = np.ascontiguousarray(I["w0"][0].reshape(1, 2 * RD), f)
    lw = np.zeros((128, NHP * 64), f)
    lb = np.zeros((128, NHP * 64), f)
    for hp in range(NHP):
        for h in range(2):
            lw[h * 64:(h + 1) * 64, hp * 64:(hp + 1) * 64] = I["ln_x_w"][0][(2 * hp + h) * 64:(2 * hp + h + 1) * 64][None, :]
            lb[h * 64:(h + 1) * 64, hp * 64:(hp + 1) * 64] = I["ln_x_b"][0][(2 * hp + h) * 64:(2 * hp + h + 1) * 64][None, :]
    sh["lnw"], sh["lnb"] = lw, lb
    sh["fng"] = np.ascontiguousarray(I["final_norm"].reshape(1, D), f)
    return sh, c


def run(cfg, I):
    sh, c = shared_prep(cfg, I)
    T, D, pc = c["T"], c["D"], c["pc"]
    xp = np.asarray(I["x_prompt"], np.float32)[0]
    xsamp = np.asarray(I["x_sample"], np.float32)
    in_maps = []
    for core in range(NCORES):
        xs = np.concatenate([xp[core * T:(core + 1) * T], xsamp[2 * core], xsamp[2 * core + 1]], axis=0)
        xh = np.zeros((128, D), np.float32)
        if core > 0:
            xh[0] = xp[core * T - 1]
        if core < NCORES - 1:
            xh[1] = xp[(core + 1) * T]
        m = dict(sh)
        pcol = sh["pcol"].copy()
        for k in range(NCORES):
            pcol[:, pc["mk"] + k] = 1.0 if k < core else 0.0
            pcol[:, pc["mk"] + 8 + k] = 1.0 if k > core else 0.0
        m["pcol"] = pcol
        m["x"] = np.ascontiguousarray(xs)
        m["xh"] = xh
        in_maps.append(m)
    if cfg.get("mode", "fused") == "split":
        pre_keys = ("xh", "wg1", "wu1", "wd1", "win", "pcol", "cst", "w2t", "a2t", "g2", "w0r", "lnw", "lnb")
        pre_maps = []
        for m in in_maps:
            pm = {k: m[k] for k in pre_keys}
            pm["x"] = np.ascontiguousarray(m["x"][0:T])
            pre_maps.append(pm)
        nc1 = build(dict(cfg, mode="pre"))
        r1 = run_bass_kernel_spmd(nc1, pre_maps, core_ids=list(range(NCORES)))
        gall = np.ascontiguousarray(np.concatenate([np.asarray(r1.results[k]["gin"], np.float32) for k in range(NCORES)], axis=0))
        for k, m in enumerate(in_maps):
            m["gout"] = gall
            m["zT0"] = np.asarray(r1.results[k]["zT0"], np.float32)
            m["x1s0"] = np.asarray(r1.results[k]["x1s0"], np.float32)
        nc = build(dict(cfg, mode="main"))
    else:
        nc = build(cfg)
    res = run_bass_kernel_spmd(nc, in_maps, core_ids=list(range(NCORES)))
    yp = np.zeros((1, NCORES * T, D), np.float32)
    ysm = np.zeros((2 * NCORES, T, D), np.float32)
    for core in range(NCORES):
        o = res.results[core]["out"]
        yp[0, core * T:(core + 1) * T] = o[0:T]
        ysm[2 * core] = o[T:2 * T]
        ysm[2 * core + 1] = o[2 * T:3 * T]
    return (yp, ysm), res


def kernel(**inputs):
    I = {k: np.asarray(v) for k, v in inputs.items()}
    (yp, ysm), _ = run(dict(FULL, mode=MODE), I)
    return (yp, ysm)
```
